# Optimizing a Trainium2 kernel written in Bass

```python
import jax, jax.numpy as jnp
from jax import lax
import numpy as np

D_MODEL = 2048
BATCH = 4
SEQ = 2048
DEPTH = 4

GRID_W = 64
CTX_LEN = 256
EPS = 1e-6
ROPE_BASE = 10000.0
NEG_INF = -1e30
F32 = jnp.float32

BRANCH_WIDTH = D_MODEL // 2
N_BRANCH = 3

RET_HEAD_DIM = 128
RET_HEADS = BRANCH_WIDTH // RET_HEAD_DIM
RET_CHUNK = 128
RET_GN_EPS = 1e-5

SWA_HEAD_DIM = 128
SWA_Q_HEADS = BRANCH_WIDTH // SWA_HEAD_DIM
SWA_KV_HEADS = SWA_Q_HEADS // 4
SWA_WINDOW = 128
SWA_BLOCK = 128

RWKV_HEAD_DIM = 64
RWKV_HEADS = BRANCH_WIDTH // RWKV_HEAD_DIM
RWKV_W_RANK = 64
RWKV_A_RANK = 64
RWKV_G_RANK = 128
RWKV_GN_EPS = 64e-5

D_FF = 4 * D_MODEL

RET_IN = 4 * BRANCH_WIDTH
SWA_IN = (SWA_Q_HEADS + 2 * SWA_KV_HEADS) * SWA_HEAD_DIM
RWKV_IN = 3 * BRANCH_WIDTH + 2 * (RWKV_W_RANK + RWKV_A_RANK) + RWKV_G_RANK
GATE_IN = N_BRANCH * D_MODEL
N_IN = RET_IN + SWA_IN + RWKV_IN + GATE_IN
IN_SPLITS = (RET_IN, SWA_IN, RWKV_IN, GATE_IN)
RWKV_SPLITS = (BRANCH_WIDTH, BRANCH_WIDTH, BRANCH_WIDTH, 2 * RWKV_W_RANK, 2 * RWKV_A_RANK, RWKV_G_RANK)

kernel_name = 'hybrid_retention_swa_rwkv7_prefix_dit'


def _split(t, sizes):
    offsets = np.cumsum(np.asarray(sizes))[:-1].tolist()
    return jnp.split(t, offsets, axis=-1)


def _flip(t, rev, axis):
    return jnp.flip(t, axis=axis) if rev else t


def rms_norm(x, g):
    x32 = x.astype(F32)
    y = x32 * lax.rsqrt(jnp.mean(jnp.square(x32), axis=-1, keepdims=True) + EPS)
    return y.astype(x.dtype) * g


def modulate(x, shift, scale):
    return x * (1 + scale) + shift


def sq_relu_mlp(x, w1, w2):
    return jnp.square(jax.nn.relu(x @ w1)) @ w2


def axial_rope_tables(row, col):
    n_freq = SWA_HEAD_DIM // 4
    inv_freq = ROPE_BASE ** (-jnp.arange(n_freq, dtype=F32) / n_freq)
    ang_r = row.astype(F32)[:, None] * inv_freq
    ang_c = col.astype(F32)[:, None] * inv_freq
    return (jnp.cos(ang_r), jnp.sin(ang_r), jnp.cos(ang_c), jnp.sin(ang_c))


def _rotate(x, cos, sin):
    cos = cos[None, :, None, :].astype(x.dtype)
    sin = sin[None, :, None, :].astype(x.dtype)
    x1, x2 = jnp.split(x, 2, axis=-1)
    return jnp.concatenate([x1 * cos - x2 * sin, x2 * cos + x1 * sin], axis=-1)


def apply_axial_rope(x, rope):
    cos_r, sin_r, cos_c, sin_c = rope
    x_row, x_col = jnp.split(x, 2, axis=-1)
    return jnp.concatenate([_rotate(x_row, cos_r, sin_r), _rotate(x_col, cos_c, sin_c)], axis=-1)


def retention_chunkwise(q, k, v, log_gamma, s0):
    b, h, t, _ = q.shape
    dv = v.shape[-1]
    n = t // RET_CHUNK
    q, k, v = (z.astype(F32).reshape(b, h, n, RET_CHUNK, z.shape[-1]) for z in (q, k, v))
    idx = jnp.arange(RET_CHUNK, dtype=F32)
    lg = log_gamma[:, None]
    diff = idx[:, None] - idx[None, :]
    intra = jnp.where(diff >= 0, jnp.exp(lg[:, :, None] * jnp.maximum(diff, 0.0)), 0.0)
    scores = jnp.einsum('bhncd,bhnsd->bhncs', q, k) * intra[None, :, None]
    y_intra = jnp.einsum('bhncs,bhnsv->bhncv', scores, v)
    q_decay = jnp.exp(lg * (idx + 1.0))
    k_decay = jnp.exp(lg * (RET_CHUNK - 1.0 - idx))
    chunk_kv = jnp.einsum('bhncd,hc,bhncv->nbhdv', k, k_decay, v)
    chunk_decay = jnp.exp(log_gamma * RET_CHUNK)[None, :, None, None]

    def step(s, kv):
        return s * chunk_decay + kv, s

    s_final, s_prev = lax.scan(step, s0, chunk_kv)
    y_cross = jnp.einsum('bhncd,hc,nbhdv->bhncv', q, q_decay, s_prev)
    return s_final, (y_intra + y_cross).reshape(b, h, t, dv)


def _ret_output(y, g):
    mean = jnp.mean(y, axis=-1, keepdims=True)
    var = jnp.mean(jnp.square(y - mean), axis=-1, keepdims=True)
    y = (y - mean) * lax.rsqrt(var + RET_GN_EPS)
    b, h, t, dv = y.shape
    y = y.transpose(0, 2, 1, 3).reshape(b, t, h * dv).astype(g.dtype)
    return jax.nn.silu(g) * y


def retention_mixer(p, pc, decay_logit):
    log_gamma = jax.nn.log_sigmoid(decay_logit.astype(F32))

    def heads(t):
        b, t_len, _ = t.shape
        return t.reshape(b, t_len, RET_HEADS, RET_HEAD_DIM).transpose(0, 2, 1, 3)

    def qkvg(t):
        q, k, v, g = jnp.split(t, 4, axis=-1)
        return heads(q), heads(k) * (RET_HEAD_DIM ** -0.5), heads(v), g

    q, k, v, g = qkvg(p)
    qc, kc, vc, gc = qkvg(pc)
    outs, outs_c = [], []
    for direction in range(2):
        rev = direction == 1
        s0 = jnp.zeros(qc.shape[:2] + (RET_HEAD_DIM, RET_HEAD_DIM), F32)
        s_ctx, y_c = retention_chunkwise(_flip(qc, rev, 2), _flip(kc, rev, 2), _flip(vc, rev, 2), log_gamma[direction], s0)
        _, y_l = retention_chunkwise(_flip(q, rev, 2), _flip(k, rev, 2), _flip(v, rev, 2), log_gamma[direction], s_ctx)
        outs.append(_flip(y_l, rev, 2))
        outs_c.append(_flip(y_c, rev, 2))
    return _ret_output(outs[0] + outs[1], g), _ret_output(outs_c[0] + outs_c[1], gc)


def banded_attention_with_ctx(q, k, v, kc, vc, sink):
    b, t, hq, d = q.shape
    g = hq // SWA_KV_HEADS
    nb = t // SWA_BLOCK
    blk = SWA_BLOCK
    tc = kc.shape[1]
    scale = d ** -0.5
    qb = q.reshape(b, nb, blk, SWA_KV_HEADS, g, d)

    def band(z):
        zp = jnp.pad(z, ((0, 0), (blk, blk), (0, 0), (0, 0))).reshape(b, nb + 2, blk, SWA_KV_HEADS, d)
        return jnp.concatenate([zp[:, :-2], zp[:, 1:-1], zp[:, 2:]], axis=2)

    kw, vw = band(k), band(v)
    s_win = jnp.einsum('bnqhgd,bnkhd->bnhgqk', qb, kw).astype(F32) * scale
    blocks = jnp.arange(nb)[:, None, None]
    qpos = blocks * blk + jnp.arange(blk)[None, :, None]
    kpos = (blocks - 1) * blk + jnp.arange(3 * blk)[None, None, :]
    valid = (jnp.abs(qpos - kpos) <= SWA_WINDOW) & (kpos >= 0) & (kpos < t)
    s_win = jnp.where(valid[None, :, None, None], s_win, NEG_INF)
    s_ctx = jnp.einsum('bnqhgd,bchd->bnhgqc', qb, kc).astype(F32) * scale
    sink_col = jnp.broadcast_to(sink.astype(F32).reshape(SWA_KV_HEADS, g)[None, None, :, :, None, None], s_win.shape[:-1] + (1,))
    probs = jax.nn.softmax(jnp.concatenate([s_win, s_ctx, sink_col], axis=-1), axis=-1)
    p_win = probs[..., :3 * blk].astype(v.dtype)
    p_ctx = probs[..., 3 * blk:3 * blk + tc].astype(v.dtype)
    out = jnp.einsum('bnhgqk,bnkhd->bnqhgd', p_win, vw) + jnp.einsum('bnhgqc,bchd->bnqhgd', p_ctx, vc)
    return out.reshape(b, t, hq * d)


def ctx_attention(qc, kc, vc, sink):
    b, tc, hq, d = qc.shape
    g = hq // SWA_KV_HEADS
    qg = qc.reshape(b, tc, SWA_KV_HEADS, g, d)
    s = jnp.einsum('bqhgd,bkhd->bhgqk', qg, kc).astype(F32) * (d ** -0.5)
    sink_col = jnp.broadcast_to(sink.astype(F32).reshape(SWA_KV_HEADS, g)[None, :, :, None, None], s.shape[:-1] + (1,))
    probs = jax.nn.softmax(jnp.concatenate([s, sink_col], axis=-1), axis=-1)
    out = jnp.einsum('bhgqk,bkhd->bqhgd', probs[..., :tc].astype(vc.dtype), vc)
    return out.reshape(b, tc, hq * d)


def swa_mixer(p, pc, sink, rope, with_ctx_out):
    sizes = (SWA_Q_HEADS * SWA_HEAD_DIM, SWA_KV_HEADS * SWA_HEAD_DIM, SWA_KV_HEADS * SWA_HEAD_DIM)

    def split_heads(t):
        b, t_len, _ = t.shape
        q, k, v = _split(t, sizes)
        return (q.reshape(b, t_len, SWA_Q_HEADS, SWA_HEAD_DIM),
                k.reshape(b, t_len, SWA_KV_HEADS, SWA_HEAD_DIM),
                v.reshape(b, t_len, SWA_KV_HEADS, SWA_HEAD_DIM))

    q, k, v = split_heads(p)
    qc, kc, vc = split_heads(pc)
    y = banded_attention_with_ctx(apply_axial_rope(q, rope), apply_axial_rope(k, rope), v, kc, vc, sink)
    y_c = ctx_attention(qc, kc, vc, sink) if with_ctx_out else None
    return y, y_c


def centred_token_shift(p, mu):
    prev = jnp.pad(p, ((0, 0), (1, 0), (0, 0)))[:, :-1]
    nxt = jnp.pad(p, ((0, 0), (0, 1), (0, 0)))[:, 1:]
    return p + mu[0] * (prev - p) + mu[1] * (nxt - p)


def rwkv_features(p, lp):
    p = centred_token_shift(p, lp['rwkv_mu'])
    r, k, v, wd, ad, gd = _split(p, RWKV_SPLITS)
    b, t, _ = p.shape
    wd = wd.reshape(b, t, 2, RWKV_W_RANK)
    ad = ad.reshape(b, t, 2, RWKV_A_RANK)
    w_raw = (lp['rwkv_w0'] + jnp.einsum('btnr,nrw->btnw', jnp.tanh(wd), lp['rwkv_w_up'])).astype(F32)
    log_decay = -jnp.exp(-jax.nn.softplus(-w_raw) - 0.5)
    a = jax.nn.sigmoid((lp['rwkv_a0'] + jnp.einsum('btnr,nrw->btnw', ad, lp['rwkv_a_up'])).astype(F32))
    g = jax.nn.sigmoid(gd) @ lp['rwkv_g_up']
    kk = (k * lp['rwkv_k_k']).astype(F32).reshape(b, t, RWKV_HEADS, RWKV_HEAD_DIM)
    kk = kk / jnp.maximum(jnp.sqrt(jnp.sum(jnp.square(kk), axis=-1, keepdims=True)), 1e-12)
    k_dir = k[:, :, None, :].astype(F32) * (1 + (a - 1) * lp['rwkv_k_a'])

    def hd(z):
        return z.reshape(z.shape[:-1] + (RWKV_HEADS, RWKV_HEAD_DIM))

    return {'r': hd(r.astype(F32)), 'k': hd(k_dir), 'v': hd(v.astype(F32)),
            'log_decay': hd(log_decay), 'a': hd(a), 'kk': kk, 'g': g}


def rwkv_scan(s0, r, w, k, v, a, b):
    def step(s, inp):
        r_t, w_t, k_t, v_t, a_t, b_t = inp
        sa = jnp.einsum('bhvk,bhk->bhv', s, a_t)
        s = s * w_t[:, :, None, :] + sa[..., None] * b_t[:, :, None, :] + v_t[..., None] * k_t[:, :, None, :]
        return s, jnp.einsum('bhvk,bhk->bhv', s, r_t)

    xs = tuple(jnp.moveaxis(z, 1, 0) for z in (r, w, k, v, a, b))
    s, ys = lax.scan(step, s0, xs)
    return s, jnp.moveaxis(ys, 0, 1)


def rwkv_direction_inputs(f, d, rev):
    ins = (f['r'], jnp.exp(f['log_decay'][:, :, d]), f['k'][:, :, d], f['v'], -f['kk'], f['kk'] * f['a'][:, :, d])
    return tuple(_flip(z, rev, 1) for z in ins)


def rwkv_output(y, f, lp):
    b, t = y.shape[:2]
    mean = jnp.mean(y, axis=-1, keepdims=True)
    var = jnp.mean(jnp.square(y - mean), axis=-1, keepdims=True)
    y = ((y - mean) * lax.rsqrt(var + RWKV_GN_EPS)).reshape(b, t, BRANCH_WIDTH) * lp['rwkv_ln_g']
    bonus = jnp.sum(f['r'][:, :, None] * f['k'] * lp['rwkv_r_k'], axis=-1, keepdims=True) * f['v'][:, :, None]
    bonus = jnp.sum(bonus, axis=2).reshape(b, t, BRANCH_WIDTH)
    return ((y + bonus) * f['g']).astype(f['g'].dtype)


def rwkv_mixer(p, pc, lp):
    f = rwkv_features(p, lp)
    fc = rwkv_features(pc, lp)
    bsz = p.shape[0]
    ys, ycs = [], []
    for d in range(2):
        rev = d == 1
        s0 = jnp.zeros((bsz, RWKV_HEADS, RWKV_HEAD_DIM, RWKV_HEAD_DIM), F32)
        s_ctx, y_c = rwkv_scan(s0, *rwkv_direction_inputs(fc, d, rev))
        _, y_l = rwkv_scan(s_ctx, *rwkv_direction_inputs(f, d, rev))
        ys.append(_flip(y_l, rev, 1))
        ycs.append(_flip(y_c, rev, 1))
    return rwkv_output(ys[0] + ys[1], f, lp), rwkv_output(ycs[0] + ycs[1], fc, lp)


def merge_branches(ys, gate_proj, w_branch, w_out):
    gates = jax.nn.sigmoid(gate_proj.reshape(gate_proj.shape[:-1] + (N_BRANCH, D_MODEL)))
    proj = jnp.einsum('btnw,nwd->btnd', jnp.stack(ys, axis=-2), w_branch)
    return jnp.sum(gates * proj, axis=-2) @ w_out


def hybrid_layer(h, hc, mod, mod_c, rope, lp, update_ctx):
    sh1, sc1, g1, sh2, sc2, g2 = jnp.split(mod, 6, axis=-1)
    sh1c, sc1c, g1c, sh2c, sc2c, g2c = jnp.split(mod_c, 6, axis=-1)
    u = modulate(rms_norm(h, lp['norm1_g']), sh1, sc1)
    uc = modulate(rms_norm(hc, lp['norm1_g']), sh1c, sc1c)
    p_ret, p_swa, p_rwkv, p_gate = _split(u @ lp['w_in'], IN_SPLITS)
    pc_ret, pc_swa, pc_rwkv, pc_gate = _split(uc @ lp['w_in'], IN_SPLITS)
    y_ret, yc_ret = retention_mixer(p_ret, pc_ret, lp['ret_decay'])
    y_swa, yc_swa = swa_mixer(p_swa, pc_swa, lp['swa_sink'], rope, update_ctx)
    y_rwkv, yc_rwkv = rwkv_mixer(p_rwkv, pc_rwkv, lp)
    h = h + g1 * merge_branches((y_ret, y_swa, y_rwkv), p_gate, lp['w_branch'], lp['w_out'])
    h = h + g2 * sq_relu_mlp(modulate(rms_norm(h, lp['norm2_g']), sh2, sc2), lp['w_ff1'], lp['w_ff2'])
    if update_ctx:
        hc = hc + g1c * merge_branches((yc_ret, yc_swa, yc_rwkv), pc_gate, lp['w_branch'], lp['w_out'])
        hc = hc + g2c * sq_relu_mlp(modulate(rms_norm(hc, lp['norm2_g']), sh2c, sc2c), lp['w_ff1'], lp['w_ff2'])
    return h, hc


def setup_inputs(seed: int = 0) -> dict:
    key = jax.random.key(seed)
    keys = iter(list(jax.random.split(key, 32)))

    def normal(shape, scale):
        return jax.random.normal(next(keys), shape, F32) * scale

    L, D, W = DEPTH, D_MODEL, BRANCH_WIDTH
    ret_base = jnp.log(2.0 ** (5.0 + jnp.arange(RET_HEADS, dtype=F32)) - 1.0)
    return {
        'x': normal((BATCH, SEQ, D), 1.0),
        'c': normal((BATCH, D), 1.0),
        'ctx': normal((BATCH, CTX_LEN, D), 1.0),
        'c_ctx': normal((D,), 1.0),
        'norm1_g': 1.0 + normal((L, D), 0.02),
        'norm2_g': 1.0 + normal((L, D), 0.02),
        'w_mod': normal((L, D, 6 * D), 0.5 * D ** -0.5),
        'b_mod': normal((L, 6 * D), 0.01),
        'w_in': normal((L, D, N_IN), D ** -0.5),
        'ret_decay': ret_base + normal((L, 2, RET_HEADS), 0.1),
        'swa_sink': normal((L, SWA_Q_HEADS), 0.5),
        'rwkv_mu': jax.random.uniform(next(keys), (L, 2, RWKV_IN), F32, 0.0, 0.5),
        'rwkv_w0': jnp.linspace(-6.5, -1.5, W, dtype=F32) + normal((L, 2, W), 0.1),
        'rwkv_w_up': normal((L, 2, RWKV_W_RANK, W), 0.1 * RWKV_W_RANK ** -0.5),
        'rwkv_a0': normal((L, 2, W), 0.1),
        'rwkv_a_up': normal((L, 2, RWKV_A_RANK, W), 0.1 * RWKV_A_RANK ** -0.5),
        'rwkv_g_up': normal((L, RWKV_G_RANK, W), RWKV_G_RANK ** -0.5),
        'rwkv_k_k': 0.85 + normal((L, W), 0.02),
        'rwkv_k_a': 1.0 + normal((L, W), 0.02),
        'rwkv_r_k': normal((L, RWKV_HEADS, RWKV_HEAD_DIM), 0.1),
        'rwkv_ln_g': 1.0 + normal((L, W), 0.02),
        'w_branch': normal((L, N_BRANCH, W, D), W ** -0.5),
        'w_out': normal((L, D, D), D ** -0.5),
        'w_ff1': normal((L, D, D_FF), D ** -0.5),
        'w_ff2': normal((L, D_FF, D), D_FF ** -0.5),
        'final_g': 1.0 + normal((D,), 0.02),
    }


def reference(x, c, ctx, c_ctx, norm1_g, norm2_g, w_mod, b_mod, w_in, ret_decay, swa_sink, rwkv_mu,
              rwkv_w0, rwkv_w_up, rwkv_a0, rwkv_a_up, rwkv_g_up, rwkv_k_k, rwkv_k_a, rwkv_r_k, rwkv_ln_g,
              w_branch, w_out, w_ff1, w_ff2, final_g):
    seq = x.shape[1]
    rows = seq // GRID_W
    row = jnp.repeat(jnp.arange(rows), GRID_W)
    col = jnp.tile(jnp.arange(GRID_W), rows)
    rope = axial_rope_tables(row, col)
    silu_c = jax.nn.silu(c)
    silu_cc = jax.nn.silu(c_ctx)
    h, hc = x, ctx
    for layer in range(DEPTH):
        mod = (silu_c @ w_mod[layer] + b_mod[layer])[:, None, :]
        mod_c = silu_cc @ w_mod[layer] + b_mod[layer]
        lp = {'norm1_g': norm1_g[layer], 'norm2_g': norm2_g[layer], 'w_in': w_in[layer],
              'ret_decay': ret_decay[layer], 'swa_sink': swa_sink[layer], 'rwkv_mu': rwkv_mu[layer],
              'rwkv_w0': rwkv_w0[layer], 'rwkv_w_up': rwkv_w_up[layer], 'rwkv_a0': rwkv_a0[layer],
              'rwkv_a_up': rwkv_a_up[layer], 'rwkv_g_up': rwkv_g_up[layer], 'rwkv_k_k': rwkv_k_k[layer],
              'rwkv_k_a': rwkv_k_a[layer], 'rwkv_r_k': rwkv_r_k[layer], 'rwkv_ln_g': rwkv_ln_g[layer],
              'w_branch': w_branch[layer], 'w_out': w_out[layer], 'w_ff1': w_ff1[layer], 'w_ff2': w_ff2[layer]}
        h, hc = hybrid_layer(h, hc, mod, mod_c, rope, lp, layer < DEPTH - 1)
    return rms_norm(h, final_g)
```

```python
import contextlib
import numpy as np
import concourse.bass as bass
import concourse.mybir as mybir
from concourse.bass_utils import run_bass_kernel_spmd

F32 = mybir.dt.float32
BF16 = mybir.dt.bfloat16
AF = mybir.ActivationFunctionType
ALU = mybir.AluOpType
AX = mybir.AxisListType

D = 2048
KC = 16
SEQ = 2048
CTX = 256
T = SEQ + CTX
DEPTH = 4
N_IN = 15232
DFF = 8192
EPS = 1e-6
TT512 = [(0, 512), (512, 512), (1024, 512), (1536, 512), (2048, 256)]
NT128 = T // 128


NSEM_DMA = 24


class _E:
    def __init__(self, P, name, eng, step, host=None):
        self.name = name
        self.eng = eng
        self.step = step
        self.is_dma = step == 16
        if self.is_dma:
            self.sems = [P.nc.alloc_semaphore(name=f"s_{name}_{i}") for i in range(NSEM_DMA)]
        else:
            self.sem = P.nc.alloc_semaphore(name="s_" + name)
        self.count = 0
        self.host = host if host is not None else self
        self.seen = {}
        self.seen_dma = set()

    def sem_for(self, idx):
        if self.is_dma:
            k = idx - 1
            return self.sems[k % NSEM_DMA], 16 * (k // NSEM_DMA + 1)
        return self.sem, idx


class Prog:
    def __init__(self, nc):
        self.nc = nc
        self.E = {}
        for name, eng in (("pe", nc.tensor), ("act", nc.scalar), ("dve", nc.vector),
                          ("pool", nc.gpsimd), ("sp", nc.sync)):
            self.E[name] = _E(self, name, eng, 1)
        for name, host in (("q_sp", "sp"), ("q_pool", "pool"), ("q_act", "act")):
            self.E[name] = _E(self, name, self.E[host].eng, 16, host=self.E[host])
        self.lastw = {}
        self.readers = {}
        self.ninst = 0

    def _wait(self, H, e, idx):
        Dp = self.E[e]
        if Dp.is_dma:
            if (e, idx) in H.seen_dma:
                return
            sem, val = Dp.sem_for(idx)
            H.eng.wait_ge(sem, val)
            H.seen_dma.add((e, idx))
        else:
            if H.seen.get(e, 0) >= idx:
                return
            H.eng.wait_ge(Dp.sem, idx)
            H.seen[e] = idx

    def emit(self, en, fn, reads=(), writes=()):
        E = self.E[en]
        H = E.host
        ps_r = [r for r in reads if r.startswith("PS:")]
        if ps_r:
            reads = [r for r in reads if not r.startswith("PS:")]
            writes = list(writes) + [r for r in ps_r if r not in writes]
        deps = set()
        for r in reads:
            ent = self.lastw.get(r)
            if ent is not None:
                deps.add(ent)
        for w in writes:
            ent = self.lastw.get(w)
            if ent is not None:
                deps.add(ent)
            for ent in self.readers.get(w, ()):
                deps.add(ent)
        mx = {}
        for e, idx in deps:
            Dp = self.E[e]
            if Dp.is_dma:
                self._wait(H, e, idx)
            else:
                if Dp is E and en == "pe":
                    continue
                if mx.get(e, 0) < idx:
                    mx[e] = idx
        for e, idx in mx.items():
            self._wait(H, e, idx)
        if E.is_dma and E.count >= NSEM_DMA:
            self._wait(H, en, E.count + 1 - NSEM_DMA)
        inst = fn()
        self.ninst += 1
        E.count += 1
        sem, _ = E.sem_for(E.count)
        inst.then_inc(sem, E.step)
        ent = (en, E.count)
        for w in writes:
            self.lastw[w] = ent
            self.readers[w] = []
        for r in reads:
            lst = self.readers.setdefault(r, [])
            if not E.is_dma:
                lst[:] = [x for x in lst if x[0] != en]
            lst.append(ent)
        return inst

    def _wait_all(self, H):
        for n, Dp in self.E.items():
            if Dp.count == 0:
                continue
            if Dp.is_dma:
                for idx in range(max(1, Dp.count - NSEM_DMA + 1), Dp.count + 1):
                    self._wait(H, n, idx)
            elif Dp is not H:
                self._wait(H, n, Dp.count)

    def barrier(self):
        for hn in ("pe", "act", "dve", "pool", "sp"):
            self._wait_all(self.E[hn])
        self.lastw = {}
        self.readers = {}

    def finish(self):
        self._wait_all(self.E["sp"])


class Ctx:
    pass


def _consts():
    c = {}
    c["ident"] = np.eye(128, dtype=np.float32)
    i = np.arange(128)
    dfw = (i[None, :] - i[:, None]).astype(np.float32)
    c["ret_dfw"] = np.maximum(dfw, 0.0)
    c["ret_mfw"] = (dfw >= 0).astype(np.float32)
    c["ret_dbw"] = np.maximum(-dfw, 0.0)
    c["ret_mbw"] = (dfw <= 0).astype(np.float32)
    idx = i.astype(np.float32)
    c["ret_qe"] = np.stack([np.tile(idx + 1.0, (128, 1)), np.tile(128.0 - idx, (128, 1))], 1).astype(np.float32)
    c["ret_ke"] = np.stack([127.0 - idx, idx], 1).astype(np.float32)
    mprev = (i[:, None] >= i[None, :]).astype(np.float32)
    mnext = (i[:, None] <= i[None, :]).astype(np.float32)
    c["swa_mprev"] = np.tile(mprev, (1, 4))
    c["swa_mnext"] = np.tile(mnext, (1, 4))
    n_freq = 32
    inv = (10000.0 ** (-np.arange(n_freq, dtype=np.float32) / n_freq)).astype(np.float32)
    t = np.arange(SEQ)
    row = (t // 64).astype(np.float32)
    col = (t % 64).astype(np.float32)
    ang_r = (row[:, None] * inv).astype(np.float32)
    ang_c = (col[:, None] * inv).astype(np.float32)
    cos = np.concatenate([np.cos(ang_r), np.cos(ang_r), np.cos(ang_c), np.cos(ang_c)], 1).T
    sin = np.concatenate([-np.sin(ang_r), np.sin(ang_r), -np.sin(ang_c), np.sin(ang_c)], 1).T
    c["rope_cos"] = np.ascontiguousarray(cos, dtype=np.float32)
    c["rope_sin"] = np.ascontiguousarray(sin, dtype=np.float32)
    perm = np.zeros((128, 128), np.float32)
    for m in range(128):
        blk = m // 32
        src = (blk ^ 1) * 32 + (m % 32)
        perm[src, m] = 1.0
    c["rope_perm"] = perm
    bd = np.zeros((128, 128), np.float32)
    bd[:64, :64] = 1.0
    bd[64:, 64:] = 1.0
    c["bd64"] = bd
    j = np.arange(64)
    strict = (j[:, None] < j[None, :]).astype(np.float32)
    incl = (j[:, None] <= j[None, :]).astype(np.float32)
    c["rk_mask"] = np.concatenate([np.concatenate([strict, incl], 1), np.concatenate([strict, incl], 1)], 0)
    c["rk_strict_t"] = (j[:, None] > j[None, :]).astype(np.float32)
    rm = np.ones((128, T), np.float32)
    rm[:, ::64] = 0.0
    c["rk_reset"] = rm
    blk_f = np.concatenate([strict, incl], 1)
    blk_b = np.concatenate([strict.T, incl.T], 1)
    c["rk_gmask"] = np.stack([np.tile(blk_f, (1, 4)), np.tile(blk_b, (1, 4))], 0).astype(np.float32)
    c["rk_qmask"] = np.stack([np.tile(strict.T, (1, 2)), np.tile(strict, (1, 2))], 0).astype(np.float32)
    c["rk_i2"] = np.tile(np.eye(64, dtype=np.float32), (1, 2))
    return c


PARAM_SHAPES = {
    "norm1_g": (DEPTH, D), "norm2_g": (DEPTH, D), "w_mod": (DEPTH, D, 6 * D), "b_mod": (DEPTH, 6 * D),
    "w_in": (DEPTH, D, N_IN), "ret_decay": (DEPTH, 2, 8), "swa_sink": (DEPTH, 8),
    "rwkv_mu": (DEPTH, 2, 3456), "rwkv_w0": (DEPTH, 2, 1024), "rwkv_w_up": (DEPTH, 2, 64, 1024),
    "rwkv_a0": (DEPTH, 2, 1024), "rwkv_a_up": (DEPTH, 2, 64, 1024), "rwkv_g_up": (DEPTH, 128, 1024),
    "rwkv_k_k": (DEPTH, 1024), "rwkv_k_a": (DEPTH, 1024), "rwkv_r_k": (DEPTH, 16, 64),
    "rwkv_ln_g": (DEPTH, 1024), "w_branch": (DEPTH, 3, 1024, D), "w_out": (DEPTH, D, D),
    "w_ff1": (DEPTH, D, DFF), "w_ff2": (DEPTH, DFF, D), "final_g": (D,),
}


RQ, RK, RG, SQ, SK, RW, GT, PT_ROWS = 0, 1024, 2048, 3072, 4096, 4352, 7808, 13952
PV_RK, PV_RV, PV_SV, PV_COLS = 0, 1024, 2048, 2304
C_RET, C_SWA, C_RWKV, C_GATE = 0, 4096, 5632, 9088


def build(L=DEPTH, dump=(), stop_after=None):
    nc = bass.Bass("TRN2", target_bir_lowering=False)
    P = Prog(nc)
    K = Ctx()
    K.nc, K.P, K.L = nc, P, L
    K.dump = set(dump)

    def dram_in(name, shape, dt=F32):
        return nc.dram_tensor(name, list(shape), dt, kind="ExternalInput").ap()

    def dram(name, shape, dt):
        kind = "ExternalOutput" if name in K.dump else "Internal"
        return nc.dram_tensor(name, list(shape), dt, kind=kind).ap()

    K.dram = dram
    I = {}
    I["x"] = dram_in("x", (SEQ, D))
    I["ctx"] = dram_in("ctx", (CTX, D))
    I["cc"] = dram_in("cc", (2, D))
    for n, s in PARAM_SHAPES.items():
        s = tuple(s)
        if n != "final_g":
            s = (L,) + s[1:]
        I[n] = dram_in(n, s)
    for n, v in _consts().items():
        I["k_" + n] = dram_in("k_" + n, v.shape)
    K.I = I
    K.out = nc.dram_tensor("out", [SEQ, D], F32, kind="ExternalOutput").ap()

    K.H = dram("H", (T, D), F32)
    K.modrow = dram("modrow", (L, 2, 6 * D), F32)
    K.PT = dram("PT", (PT_ROWS, T), BF16)
    K.PV = dram("PV", (T, PV_COLS), BF16)
    K.YT = dram("YT", (3072, T), BF16)
    K.wb = {}
    for l in range(L):
        K.wb[l] = {
            "in": dram(f"wb{l}_in", (D, N_IN), BF16),
            "br": dram(f"wb{l}_br", (3072, D), BF16),
            "out": dram(f"wb{l}_out", (D, D), BF16),
            "ff1": dram(f"wb{l}_ff1", (D, DFF), BF16),
            "ff2": dram(f"wb{l}_ff2", (DFF, D), BF16),
        }

    with contextlib.ExitStack() as top:
        K.uniq = 0

        def sb(name, shape, dt, st=top):
            K.uniq += 1
            return st.enter_context(nc.sbuf_tensor(f"s{K.uniq}_{name}", list(shape), dt)).ap()

        def pst(name, shape, dt, st=top):
            K.uniq += 1
            return st.enter_context(nc.psum_tensor(f"p{K.uniq}_{name}", list(shape), dt)).ap()

        K.sb, K.pst = sb, pst
        K.ident_f = sb("ident_f", (128, 128), F32)
        K.ident_b = sb("ident_b", (128, 128), BF16)
        K.modT = sb("modT", (128, L * 96 * 2), F32)
        K.bank = [pst(f"bank{i}", (128, 512), F32) for i in range(8)]
        K.prep32 = [sb(f"prep32_{i}", (128, 1024), F32) for i in range(2)]
        K.prepb = [sb(f"prepb_{i}", (128, 1024), BF16) for i in range(2)]
        K.prep_i = 0

        P.emit("q_sp", lambda: nc.sync.dma_start(out=K.ident_f, in_=I["k_ident"]), writes=["ident_f"])
        P.emit("dve", lambda: nc.vector.tensor_copy(out=K.ident_b, in_=K.ident_f), reads=["ident_f"], writes=["ident_b"])
        P.emit("q_sp", lambda: nc.sync.dma_start(out=K.H[0:CTX, :], in_=I["ctx"]), writes=["H"])
        P.emit("q_sp", lambda: nc.sync.dma_start(out=K.H[CTX:T, :], in_=I["x"]), writes=["H"])

        stages = []
        stages.append(("prep0", lambda: stage_prep(K, 0)))
        stages.append(("mod", lambda: stage_mod(K)))
        for l in range(L):
            if l + 1 < L:
                stages.append((f"prep{l+1}", lambda l=l: stage_prep(K, l + 1)))
            stages.append((f"proj_{l}", lambda l=l: stage_np(K, l)))
            stages.append((f"ret_{l}", lambda l=l: stage_ret(K, l)))
            stages.append((f"swa_{l}", lambda l=l: stage_swa(K, l)))
            stages.append((f"rwkv_{l}", lambda l=l: stage_rwkv(K, l)))
            stages.append((f"tail_{l}", lambda l=l: stage_tail(K, l)))
        stages.append(("final", lambda: stage_final(K)))
        for name, fn in stages:
            fn()
            if stop_after == name:
                break
        P.finish()
    K.ninst = P.ninst
    return nc, K


def stage_prep(K, l):
    nc, P, I = K.nc, K.P, K.I
    jobs = [
        (I["w_in"][l], K.wb[l]["in"], D, N_IN, f"wb{l}_in"),
        (I["w_branch"][l].rearrange("n w d -> (n w) d"), K.wb[l]["br"], 3072, D, f"wb{l}_br"),
        (I["w_out"][l], K.wb[l]["out"], D, D, f"wb{l}_out"),
        (I["w_ff1"][l], K.wb[l]["ff1"], D, DFF, f"wb{l}_ff1"),
        (I["w_ff2"][l], K.wb[l]["ff2"], DFF, D, f"wb{l}_ff2"),
    ]
    tiles = []
    for src, dst, R, C, rn in jobs:
        for c0 in range(0, C, 1024):
            cw = min(1024, C - c0)
            for r0 in range(0, R, 128):
                tiles.append((src[r0:r0 + 128, c0:c0 + cw], dst[r0:r0 + 128, c0:c0 + cw], cw, rn))

    def load(i):
        s, d, cw, rn = tiles[i]
        b = (K.prep_i + i) % 2
        P.emit("q_pool", lambda: nc.gpsimd.dma_start(out=K.prep32[b][:, :cw], in_=s), writes=[f"prep32_{b}"])

    def conv_store(i):
        s, d, cw, rn = tiles[i]
        b = (K.prep_i + i) % 2
        P.emit("pool", lambda: nc.gpsimd.tensor_copy(out=K.prepb[b][:, :cw], in_=K.prep32[b][:, :cw]),
               reads=[f"prep32_{b}"], writes=[f"prepb_{b}"])
        P.emit("q_pool", lambda: nc.gpsimd.dma_start(out=d, in_=K.prepb[b][:, :cw]), reads=[f"prepb_{b}"], writes=[rn + "_part"])

    n = len(tiles)
    load(0)
    for i in range(n):
        if i + 1 < n:
            load(i + 1)
        conv_store(i)
        if i + 1 == n or tiles[i + 1][3] != tiles[i][3]:
            rn = tiles[i][3]
            ent = P.lastw[rn + "_part"]
            P.lastw[rn] = ent
            P.readers[rn] = []
    K.prep_i += n


def modv(K, l, j, r):
    o = ((l * 96 + j) * 2 + r)
    return K.modT[:, o:o + 1]


def load_fm_vec(K, st, src2d, n, dst, tag):
    nc, P = K.nc, K.P
    tmp = K.sb(f"lfv_{tag}", (n, 128), F32, st)
    ps = K.bank[7][:, :n]
    P.emit("q_sp", lambda: nc.sync.dma_start(out=tmp, in_=src2d), writes=[f"lfv_{tag}"])
    P.emit("pe", lambda: nc.tensor.transpose(out=ps, in_=tmp, identity=K.ident_f[0:n, 0:n]),
           reads=[f"lfv_{tag}", "ident_f"], writes=["PS:7"])
    P.emit("dve", lambda: nc.vector.tensor_copy(out=dst, in_=ps), reads=["PS:7"], writes=[tag])


def stage_mod(K):
    nc, P, I, L = K.nc, K.P, K.I, K.L
    with contextlib.ExitStack() as st:
        cc = K.sb("cc", (2, D), F32, st)
        sc = K.sb("sc", (2, D), F32, st)
        scT = K.sb("scT", (128, KC * 2), F32, st)
        wt = [K.sb(f"wmod{i}", (128, 2048), F32, st) for i in range(3)]
        bm = K.sb("bmod", (2, 2048), F32, st)
        mrow = [K.sb(f"mrow{i}", (2, 2048), F32, st) for i in range(2)]
        psA = [K.bank[j] for j in range(4)]
        psT = K.bank[4][:, :32]
        P.emit("q_sp", lambda: nc.sync.dma_start(out=cc, in_=I["cc"]), writes=["cc"])
        P.emit("act", lambda: nc.scalar.activation(out=sc, in_=cc, func=AF.Silu), reads=["cc"], writes=["sc"])
        for k in range(KC):
            P.emit("pe", lambda: nc.tensor.transpose(out=psT[:, 2 * k:2 * k + 2], in_=sc[0:2, k * 128:(k + 1) * 128],
                                                      identity=K.ident_f[0:2, 0:2]),
                   reads=["sc", "ident_f"], writes=["PS:4"])
        P.emit("dve", lambda: nc.vector.tensor_copy(out=scT, in_=psT), reads=["PS:4"], writes=["scT"])
        cnt = 0
        for l in range(L):
            for pc in range(6):
                cs = slice(pc * 2048, (pc + 1) * 2048)
                for k in range(KC):
                    b = cnt % 3
                    q = "q_sp" if cnt % 2 == 0 else "q_act"
                    eng = nc.sync if cnt % 2 == 0 else nc.scalar
                    cnt += 1
                    P.emit(q, lambda: eng.dma_start(out=wt[b], in_=I["w_mod"][l, k * 128:(k + 1) * 128, cs]),
                           writes=[f"wmod{b}"])
                    for j in range(4):
                        P.emit("pe", lambda: nc.tensor.matmul(psA[j][0:2, :], lhsT=scT[:, 2 * k:2 * k + 2],
                                                              rhs=wt[b][:, j * 512:(j + 1) * 512],
                                                              start=(k == 0), stop=(k == KC - 1)),
                               reads=["scT", f"wmod{b}"], writes=[f"PS:{j}"])
                for r in range(2):
                    P.emit("q_sp", lambda: nc.sync.dma_start(out=bm[r:r + 1, :], in_=I["b_mod"][l:l + 1, cs]),
                           writes=["bmod"])
                mr = mrow[(l * 6 + pc) % 2]
                mrn = f"mrow{(l * 6 + pc) % 2}"
                for j in range(4):
                    P.emit("dve", lambda: nc.vector.tensor_tensor(out=mr[:, j * 512:(j + 1) * 512], in0=psA[j][0:2, :],
                                                                  in1=bm[:, j * 512:(j + 1) * 512], op=ALU.add),
                           reads=[f"PS:{j}", "bmod"], writes=[mrn])
                P.emit("q_sp", lambda: nc.sync.dma_start(out=K.modrow[l, :, cs], in_=mr), reads=[mrn], writes=["modrow"])
                for i in range(16):
                    P.emit("pe", lambda: nc.tensor.transpose(out=psT[:, 2 * i:2 * i + 2], in_=mr[0:2, i * 128:(i + 1) * 128],
                                                              identity=K.ident_f[0:2, 0:2]),
                           reads=[mrn, "ident_f"], writes=["PS:4"])
                o = (l * 96 + pc * 16) * 2
                P.emit("dve", lambda: nc.vector.tensor_copy(out=K.modT[:, o:o + 32], in_=psT), reads=["PS:4"], writes=["modT"])
    P.barrier()


def stage_norm1(K, l, which=1):
    nc, P, I, L = K.nc, K.P, K.I, K.L
    gname = "norm1_g" if which == 1 else "norm2_g"
    jsh, jsc = (0, 16) if which == 1 else (48, 64)
    modv4 = K.modT.rearrange("p (l j r) -> p l j r", l=L, j=96, r=2)
    with contextlib.ExitStack() as st:
        gT = K.sb("gT", (128, KC), F32, st)
        A = K.sb("A1", (128, KC, 2), F32, st)
        hb = [K.sb(f"hb{i}", (128, D), F32, st) for i in range(2)]
        yb = [K.sb(f"yb{i}", (128, D), BF16, st) for i in range(2)]
        junk = K.sb("junk", (128, D), BF16, st)
        ss = K.sb("ss", (128, 2), F32, st)
        pT = [K.bank[i].bitcast(BF16)[:, :512] for i in range(4)]
        load_fm_vec(K, st, I[gname][l].rearrange("(kc p) -> kc p", p=128), KC, gT, "gT")
        for r in range(2):
            P.emit("dve", lambda: nc.vector.tensor_scalar(out=A[:, :, r], in0=modv4[:, l, jsc:jsc + 16, r], scalar1=1.0,
                                                          scalar2=None, op0=ALU.add), reads=["modT"], writes=["A1"])
            P.emit("dve", lambda: nc.vector.tensor_tensor(out=A[:, :, r], in0=A[:, :, r], in1=gT, op=ALU.mult),
                   reads=["A1", "gT"], writes=["A1"])
        if "dbg_A" in K.dump and l == 0 and which == 1:
            dA = K.dram("dbg_A", (128, KC * 2), F32)
            dg = K.dram("dbg_g", (128, KC), F32)
            P.emit("q_sp", lambda: nc.sync.dma_start(out=dA, in_=A.rearrange("p a b -> p (a b)")), reads=["A1"], writes=["dbg_A"])
            P.emit("q_sp", lambda: nc.sync.dma_start(out=dg, in_=gT), reads=["gT"], writes=["dbg_g"])
        for tt in range(NT128):
            b = tt % 2
            r = 1 if tt < 2 else 0
            P.emit("q_sp", lambda: nc.sync.dma_start(out=hb[b], in_=K.H[tt * 128:(tt + 1) * 128, :]),
                   reads=["H"], writes=[f"hb{b}"])
            P.emit("act", lambda: nc.scalar.activation(out=junk, in_=hb[b], func=AF.Square, accum_out=ss[:, b:b + 1]),
                   reads=[f"hb{b}"], writes=["junk", f"ss{b}"])
            P.emit("dve", lambda: nc.vector.tensor_scalar(out=ss[:, b:b + 1], in0=ss[:, b:b + 1], scalar1=1.0 / D, scalar2=EPS,
                                                          op0=ALU.mult, op1=ALU.add), reads=[f"ss{b}"], writes=[f"ss{b}"])
            P.emit("act", lambda: nc.scalar.activation(out=ss[:, b:b + 1], in_=ss[:, b:b + 1], func=AF.Sqrt),
                   reads=[f"ss{b}"], writes=[f"ss{b}"])
            P.emit("dve", lambda: nc.vector.reciprocal(out=ss[:, b:b + 1], in_=ss[:, b:b + 1]), reads=[f"ss{b}"], writes=[f"ss{b}"])
            P.emit("act", lambda: nc.scalar.activation(out=yb[b], in_=hb[b], func=AF.Copy, scale=ss[:, b:b + 1]),
                   reads=[f"hb{b}", f"ss{b}"], writes=[f"yb{b}"])
            for g4 in range(4):
                pt = pT[g4]
                for j in range(4):
                    dc = g4 * 4 + j
                    P.emit("pe", lambda: nc.tensor.transpose(out=pt[:, j * 128:(j + 1) * 128], in_=yb[b][:, dc * 128:(dc + 1) * 128],
                                                              identity=K.ident_b),
                           reads=[f"yb{b}", "ident_b"], writes=[f"PS:{g4}"])
                for j in range(4):
                    dc = g4 * 4 + j
                    P.emit("dve", lambda: nc.vector.tensor_scalar(out=K.uT[:, dc, tt * 128:(tt + 1) * 128],
                                                                  in0=pt[:, j * 128:(j + 1) * 128],
                                                                  scalar1=A[:, dc, r:r + 1], scalar2=modv(K, l, jsh + dc, r),
                                                                  op0=ALU.mult, op1=ALU.add),
                           reads=[f"PS:{g4}", "A1", "modT"], writes=["uT"])
    P.barrier()


def proj_groups():
    fm, tm = [], []
    s = 128.0 ** -0.5

    def add_fm(c0, n, dst, kind, scale=1.0):
        o = 0
        while o < n:
            w = min(512, n - o)
            fm.append((c0 + o, w, dst + o, kind, scale))
            o += w

    def add_tm(c0, n, dst, scale=1.0):
        o = 0
        while o < n:
            w = min(512, n - o)
            tm.append((c0 + o, w, dst + o, scale))
            o += w

    add_fm(C_RET + 0, 1024, RQ, "copy")
    add_fm(C_RET + 1024, 1024, RK, "copy", s)
    add_tm(C_RET + 1024, 1024, PV_RK, s)
    add_tm(C_RET + 2048, 1024, PV_RV)
    add_fm(C_RET + 3072, 1024, RG, "silu")
    add_fm(C_SWA + 0, 1024, SQ, "copy")
    add_fm(C_SWA + 1024, 256, SK, "copy")
    add_tm(C_SWA + 1280, 256, PV_SV)
    add_fm(C_RWKV, 3456, RW, "copy")
    add_fm(C_GATE, 6144, GT, "sigmoid")
    return fm, tm


def stage_proj(K, l):
    nc, P = K.nc, K.P
    wb = K.wb[l]["in"]
    wbn = f"wb{l}_in"
    fm, tm = proj_groups()
    with contextlib.ExitStack() as st:
        wbuf = [K.sb(f"wbuf{i}", (128, KC, 512), BF16, st) for i in range(2)]
        ot = [K.sb(f"ot{i}", (128, T), BF16, st) for i in range(2)]
        ot2 = [K.sb(f"ot2_{i}", (128, 512), BF16, st) for i in range(2)]
        ps = K.bank
        pc = 0
        oc = 0
        ec = 0
        for gi, (c0, n, dst, kind, scale) in enumerate(fm + [(g[0], g[1], g[2], "tm", g[3]) for g in tm]):
            b = gi % 2
            P.emit("q_sp", lambda: nc.sync.dma_start(out=wbuf[b][:, :, :n],
                                                     in_=wb[:, c0:c0 + n].rearrange("(kc p) n -> p kc n", p=128)),
                   reads=[wbn], writes=[f"wbuf{b}"])
            if kind != "tm":
                for ci in range(n // 128):
                    o = ot[oc % 2]
                    on = f"ot{oc % 2}"
                    oc += 1
                    for (t0, tn) in TT512:
                        p = ps[pc % 8]
                        pn = f"PS:{pc % 8}"
                        pc += 1
                        for k in range(KC):
                            P.emit("pe", lambda: nc.tensor.matmul(p[:, :tn], lhsT=wbuf[b][:, k, ci * 128:(ci + 1) * 128],
                                                                  rhs=K.uT[:, k, t0:t0 + tn], start=(k == 0), stop=(k == KC - 1)),
                                   reads=[f"wbuf{b}", "uT"], writes=[pn])
                        if kind == "copy":
                            if ec % 2 == 0:
                                P.emit("act", lambda: nc.scalar.activation(out=o[:, t0:t0 + tn], in_=p[:, :tn], func=AF.Copy, scale=scale),
                                       reads=[pn], writes=[on])
                            else:
                                P.emit("dve", lambda: nc.vector.tensor_scalar(out=o[:, t0:t0 + tn], in0=p[:, :tn], scalar1=scale,
                                                                              scalar2=None, op0=ALU.mult), reads=[pn], writes=[on])
                            ec += 1
                        else:
                            f = AF.Silu if kind == "silu" else AF.Sigmoid
                            P.emit("act", lambda: nc.scalar.activation(out=o[:, t0:t0 + tn], in_=p[:, :tn], func=f),
                                   reads=[pn], writes=[on])
                    r0 = dst + ci * 128
                    P.emit("q_sp", lambda: nc.sync.dma_start(out=K.PT[r0:r0 + 128, :], in_=o), reads=[on], writes=["PT"])
            else:
                for tt in range(NT128):
                    p = ps[pc % 8]
                    pn = f"PS:{pc % 8}"
                    pc += 1
                    o = ot2[oc % 2]
                    on = f"ot2_{oc % 2}"
                    oc += 1
                    for k in range(KC):
                        P.emit("pe", lambda: nc.tensor.matmul(p[:, :n], lhsT=K.uT[:, k, tt * 128:(tt + 1) * 128],
                                                              rhs=wbuf[b][:, k, :n], start=(k == 0), stop=(k == KC - 1)),
                               reads=[f"wbuf{b}", "uT"], writes=[pn])
                    if ec % 2 == 0:
                        P.emit("act", lambda: nc.scalar.activation(out=o[:, :n], in_=p[:, :n], func=AF.Copy, scale=scale),
                               reads=[pn], writes=[on])
                    else:
                        P.emit("dve", lambda: nc.vector.tensor_scalar(out=o[:, :n], in0=p[:, :n], scalar1=scale, scalar2=None,
                                                                      op0=ALU.mult), reads=[pn], writes=[on])
                    ec += 1
                    P.emit("q_sp", lambda: nc.sync.dma_start(out=K.PV[tt * 128:(tt + 1) * 128, dst:dst + n], in_=o[:, :n]),
                           reads=[on], writes=["PV"])
    P.barrier()


def stage_np(K, l):
    with contextlib.ExitStack() as st:
        K.uT = K.sb("uT", (128, KC, T), BF16, st)
        stage_norm1(K, l)
        stage_proj(K, l)


def bcast_row(K, st, src_row, n, tag):
    nc, P = K.nc, K.P
    row = K.sb(f"br_{tag}", (1, n), F32, st)
    ones = K.sb(f"bo_{tag}", (1, 128), F32, st)
    ps = K.bank[7][:, :n]
    dst = K.sb(f"bd_{tag}", (128, n), F32, st)
    P.emit("q_sp", lambda: nc.sync.dma_start(out=row, in_=src_row), writes=[f"br_{tag}"])
    P.emit("dve", lambda: nc.vector.memset(ones, 1.0), writes=[f"bo_{tag}"])
    P.emit("pe", lambda: nc.tensor.matmul(ps, lhsT=ones, rhs=row, start=True, stop=True),
           reads=[f"br_{tag}", f"bo_{tag}"], writes=["PS:7"])
    P.emit("dve", lambda: nc.vector.tensor_copy(out=dst, in_=ps), reads=["PS:7"], writes=[tag])
    return dst


def stage_ret(K, l):
    nc, P, I = K.nc, K.P, K.I
    with contextlib.ExitStack() as st:
        sb, pst = (lambda n, s, d: K.sb(n, s, d, st)), (lambda n, s, d: K.pst(n, s, d, st))
        cdf = [sb(f"cdf{d}", (128, 128), F32) for d in range(2)]
        cmk = [sb(f"cmk{d}", (128, 128), F32) for d in range(2)]
        cqe = sb("cqe", (128, 2, 128), F32)
        cke = sb("cke", (128, 2), F32)
        onesf = sb("onesf", (128, 128), F32)
        for d, (a, b) in enumerate((("k_ret_dfw", "k_ret_mfw"), ("k_ret_dbw", "k_ret_mbw"))):
            P.emit("q_sp", lambda: nc.sync.dma_start(out=cdf[d], in_=I[a]), writes=[f"cdf{d}"])
            P.emit("q_sp", lambda: nc.sync.dma_start(out=cmk[d], in_=I[b]), writes=[f"cmk{d}"])
        P.emit("q_sp", lambda: nc.sync.dma_start(out=cqe, in_=I["k_ret_qe"]), writes=["cqe"])
        P.emit("q_sp", lambda: nc.sync.dma_start(out=cke, in_=I["k_ret_ke"]), writes=["cke"])
        P.emit("dve", lambda: nc.vector.memset(onesf, 1.0 / 128.0), writes=["onesf"])
        dl = bcast_row(K, st, I["ret_decay"][l:l + 1].rearrange("o a h -> o (a h)"), 16, "dl")
        lg = sb("lg", (128, 16), F32)
        P.emit("act", lambda: nc.scalar.activation(out=lg, in_=dl, func=AF.Exp, scale=-1.0), reads=["dl"], writes=["lg"])
        P.emit("act", lambda: nc.scalar.activation(out=lg, in_=lg, func=AF.Ln, bias=1.0), reads=["lg"], writes=["lg"])
        P.emit("dve", lambda: nc.vector.tensor_scalar(out=lg, in0=lg, scalar1=-1.0, scalar2=None, op0=ALU.mult),
               reads=["lg"], writes=["lg"])
        DM = [sb(f"DM{d}", (128, 128), F32) for d in range(2)]
        qdec = [sb(f"qdec{d}", (128, 128), F32) for d in range(2)]
        kdec = sb("kdec", (128, 2), F32)
        gam = sb("gam", (128, 2), F32)
        inb = []
        for i in range(2):
            inb.append(dict(qT=sb(f"r_qT{i}", (128, T), BF16), kT=sb(f"r_kT{i}", (128, T), BF16),
                            sg=sb(f"r_sg{i}", (128, T), BF16), ktm=sb(f"r_ktm{i}", (128, NT128, 128), BF16),
                            vtm=sb(f"r_vtm{i}", (128, NT128, 128), BF16)))
        vt = sb("r_vt", (128, 2, NT128, 128), BF16)
        qt = [sb(f"r_qt{i}", (128, 128), BF16) for i in range(4)]
        sm = [sb(f"r_sm{i}", (128, 128), BF16) for i in range(4)]
        S = [sb(f"r_S{d}", (128, 128), F32) for d in range(2)]
        Sb = [sb(f"r_Sb{d}", (128, 128), BF16) for d in range(2)]
        yacc = [sb(f"r_y{d}", (128, T), F32) for d in range(2)]
        tmp = [sb(f"r_t{i}", (128, 512), F32) for i in range(4)]
        outb = [sb(f"r_o{i}", (128, 512), BF16) for i in range(2)]
        psS = [K.bank[i][:, :128] for i in range(2)]
        psY = [K.bank[2 + i][:, :128] for i in range(2)]
        psK = [K.bank[4 + i][:, :128] for i in range(2)]
        psM = K.bank[6]
        psV = K.bank[7]

        def load_head(h):
            B = inb[h % 2]
            i = h % 2
            for nm, row in (("qT", RQ), ("kT", RK), ("sg", RG)):
                P.emit("q_sp", lambda: nc.sync.dma_start(out=B[nm], in_=K.PT[row + h * 128:row + (h + 1) * 128, :]),
                       reads=["PT"], writes=[f"r_{nm}{i}"])
            for nm, col in (("ktm", PV_RK), ("vtm", PV_RV)):
                P.emit("q_act", lambda: nc.scalar.dma_start(
                    out=B[nm], in_=K.PV[:, col + h * 128:col + (h + 1) * 128].rearrange("(c p) d -> p c d", p=128)),
                    reads=["PV"], writes=[f"r_{nm}{i}"])

        load_head(0)
        cnt = 0
        oc = 0
        for h in range(8):
            if h + 1 < 8:
                load_head(h + 1)
            B = inb[h % 2]
            i = h % 2
            for d in range(2):
                col = d * 8 + h
                lgc = lg[:, col:col + 1]
                P.emit("act", lambda: nc.scalar.activation(out=DM[d], in_=cdf[d], func=AF.Exp, scale=lgc),
                       reads=["lg", f"cdf{d}"], writes=[f"DM{d}"])
                P.emit("dve", lambda: nc.vector.tensor_tensor(out=DM[d], in0=DM[d], in1=cmk[d], op=ALU.mult),
                       reads=[f"DM{d}", f"cmk{d}"], writes=[f"DM{d}"])
                P.emit("act", lambda: nc.scalar.activation(out=qdec[d], in_=cqe[:, d, :], func=AF.Exp, scale=lgc),
                       reads=["lg", "cqe"], writes=[f"qdec{d}"])
                P.emit("act", lambda: nc.scalar.activation(out=kdec[:, d:d + 1], in_=cke[:, d:d + 1], func=AF.Exp, scale=lgc),
                       reads=["lg", "cke"], writes=[f"kdec{d}"])
                P.emit("act", lambda: nc.scalar.activation(out=gam[:, d:d + 1], in_=lgc, func=AF.Exp, scale=128.0),
                       reads=["lg"], writes=[f"gam{d}"])
                P.emit("dve", lambda: nc.vector.tensor_scalar(out=vt[:, d], in0=B["vtm"], scalar1=kdec[:, d:d + 1], scalar2=None,
                                                              op0=ALU.mult), reads=[f"r_vtm{i}", f"kdec{d}"], writes=[f"r_vt{d}"])
                P.emit("dve", lambda: nc.vector.memset(S[d], 0.0), writes=[f"r_S{d}"])
                P.emit("dve", lambda: nc.vector.memset(Sb[d], 0.0), writes=[f"r_Sb{d}"])
            order = [list(range(NT128)), [1, 0] + list(range(NT128 - 1, 1, -1))]
            for step in range(NT128):
                for d in range(2):
                    c = order[d][step]
                    cs = slice(c * 128, (c + 1) * 128)
                    j = cnt % 2
                    j4 = cnt % 4
                    cnt += 1
                    P.emit("pe", lambda: nc.tensor.matmul(psS[j], lhsT=B["kT"][:, cs], rhs=B["qT"][:, cs], start=True, stop=True),
                           reads=[f"r_kT{i}", f"r_qT{i}"], writes=[f"PS:{j}"])
                    P.emit("dve", lambda: nc.vector.tensor_tensor(out=sm[j4], in0=psS[j], in1=DM[d], op=ALU.mult),
                           reads=[f"PS:{j}", f"DM{d}"], writes=[f"r_sm{j4}"])
                    P.emit("pool", lambda: nc.gpsimd.tensor_tensor(out=qt[j4], in0=B["qT"][:, cs], in1=qdec[d], op=ALU.mult),
                           reads=[f"r_qT{i}", f"qdec{d}"], writes=[f"r_qt{j4}"])
                    P.emit("pe", lambda: nc.tensor.matmul(psY[j], lhsT=B["vtm"][:, c, :], rhs=sm[j4], start=True, stop=False),
                           reads=[f"r_vtm{i}", f"r_sm{j4}"], writes=[f"PS:{2 + j}"])
                    P.emit("pe", lambda: nc.tensor.matmul(psY[j], lhsT=Sb[d], rhs=qt[j4], start=False, stop=True),
                           reads=[f"r_Sb{d}", f"r_qt{j4}"], writes=[f"PS:{2 + j}"])
                    P.emit("act", lambda: nc.scalar.copy(out=yacc[d][:, cs], in_=psY[j]), reads=[f"PS:{2 + j}"], writes=[f"r_y{d}"])
                    P.emit("pe", lambda: nc.tensor.matmul(psK[j], lhsT=B["ktm"][:, c, :], rhs=vt[:, d, c, :], start=True, stop=True),
                           reads=[f"r_ktm{i}", f"r_vt{d}"], writes=[f"PS:{4 + j}"])
                    P.emit("dve", lambda: nc.vector.scalar_tensor_tensor(out=S[d], in0=S[d], scalar=gam[:, d:d + 1], in1=psK[j],
                                                                         op0=ALU.mult, op1=ALU.add),
                           reads=[f"r_S{d}", f"gam{d}", f"PS:{4 + j}"], writes=[f"r_S{d}"])
                    P.emit("act", lambda: nc.scalar.copy(out=Sb[d], in_=S[d]), reads=[f"r_S{d}"], writes=[f"r_Sb{d}"])
            for (t0, tn) in TT512:
                ts_ = slice(t0, t0 + tn)
                y, ysq, msq, yc = tmp[0], tmp[1], tmp[2], tmp[3]
                P.emit("dve", lambda: nc.vector.tensor_tensor(out=y[:, :tn], in0=yacc[0][:, ts_], in1=yacc[1][:, ts_], op=ALU.add),
                       reads=["r_y0", "r_y1"], writes=["r_t0"])
                P.emit("act", lambda: nc.scalar.activation(out=ysq[:, :tn], in_=y[:, :tn], func=AF.Square), reads=["r_t0"], writes=["r_t1"])
                P.emit("pe", lambda: nc.tensor.matmul(psM[:, :tn], lhsT=onesf, rhs=y[:, :tn], start=True, stop=True),
                       reads=["onesf", "r_t0"], writes=["PS:6"])
                P.emit("pe", lambda: nc.tensor.matmul(psV[:, :tn], lhsT=onesf, rhs=ysq[:, :tn], start=True, stop=True),
                       reads=["onesf", "r_t1"], writes=["PS:7"])
                P.emit("act", lambda: nc.scalar.activation(out=msq[:, :tn], in_=psM[:, :tn], func=AF.Square), reads=["PS:6"], writes=["r_t2"])
                P.emit("dve", lambda: nc.vector.tensor_tensor(out=msq[:, :tn], in0=psV[:, :tn], in1=msq[:, :tn], op=ALU.subtract),
                       reads=["PS:7", "r_t2"], writes=["r_t2"])
                P.emit("dve", lambda: nc.vector.tensor_scalar(out=msq[:, :tn], in0=msq[:, :tn], scalar1=1e-5, scalar2=None, op0=ALU.add),
                       reads=["r_t2"], writes=["r_t2"])
                P.emit("act", lambda: nc.scalar.activation(out=msq[:, :tn], in_=msq[:, :tn], func=AF.Sqrt), reads=["r_t2"], writes=["r_t2"])
                P.emit("dve", lambda: nc.vector.reciprocal(out=msq[:, :tn], in_=msq[:, :tn]), reads=["r_t2"], writes=["r_t2"])
                P.emit("dve", lambda: nc.vector.tensor_tensor(out=yc[:, :tn], in0=y[:, :tn], in1=psM[:, :tn], op=ALU.subtract),
                       reads=["r_t0", "PS:6"], writes=["r_t3"])
                P.emit("dve", lambda: nc.vector.tensor_tensor(out=yc[:, :tn], in0=yc[:, :tn], in1=msq[:, :tn], op=ALU.mult),
                       reads=["r_t3", "r_t2"], writes=["r_t3"])
                ob = outb[oc % 2]
                obn = f"r_o{oc % 2}"
                oc += 1
                P.emit("dve", lambda: nc.vector.tensor_tensor(out=ob[:, :tn], in0=yc[:, :tn], in1=B["sg"][:, ts_], op=ALU.mult),
                       reads=["r_t3", f"r_sg{i}"], writes=[obn])
                P.emit("q_sp", lambda: nc.sync.dma_start(out=K.YT[h * 128:(h + 1) * 128, ts_], in_=ob[:, :tn]),
                       reads=[obn], writes=["YT"])
    P.barrier()


def stage_swa(K, l):
    nc, P, I = K.nc, K.P, K.I
    SC = 128.0 ** -0.5
    with contextlib.ExitStack() as st:
        sb = lambda n, s_, d: K.sb(n, s_, d, st)
        cosT = sb("w_cos", (128, SEQ), F32)
        sinT = sb("w_sin", (128, SEQ), F32)
        permf = sb("w_permf", (128, 128), F32)
        permb = sb("w_permb", (128, 128), BF16)
        mk32 = sb("w_mk32", (128, 2, 512), F32)
        mk = sb("w_mk", (128, 2, 512), BF16)
        onesb = sb("w_onesb", (128, 128), BF16)
        P.emit("q_sp", lambda: nc.sync.dma_start(out=cosT, in_=I["k_rope_cos"]), writes=["w_cos"])
        P.emit("q_sp", lambda: nc.sync.dma_start(out=sinT, in_=I["k_rope_sin"]), writes=["w_sin"])
        P.emit("q_sp", lambda: nc.sync.dma_start(out=permf, in_=I["k_rope_perm"]), writes=["w_permf"])
        P.emit("dve", lambda: nc.vector.tensor_copy(out=permb, in_=permf), reads=["w_permf"], writes=["w_permb"])
        P.emit("q_sp", lambda: nc.sync.dma_start(out=mk32[:, 0, :], in_=I["k_swa_mprev"]), writes=["w_mk32"])
        P.emit("q_sp", lambda: nc.sync.dma_start(out=mk32[:, 1, :], in_=I["k_swa_mnext"]), writes=["w_mk32"])
        P.emit("dve", lambda: nc.vector.tensor_copy(out=mk, in_=mk32), reads=["w_mk32"], writes=["w_mk"])
        P.emit("dve", lambda: nc.vector.memset(onesb, 1.0), writes=["w_onesb"])
        sk = bcast_row(K, st, I["swa_sink"][l:l + 1], 8, "sk")
        es = sb("w_es", (128, 8), F32)
        P.emit("act", lambda: nc.scalar.activation(out=es, in_=sk, func=AF.Exp), reads=["sk"], writes=["w_es"])
        raw = [sb(f"w_raw{i}", (128, T), BF16) for i in range(2)]
        kR = sb("w_kR", (128, T), BF16)
        vtm = sb("w_vtm", (128, NT128, 128), BF16)
        Qb = sb("w_Qb", (128, NT128, 4, 128), BF16)
        ysw = sb("w_ysw", (128, 4, T), BF16)
        Et = [sb(f"w_E{i}", (128, 512), BF16) for i in range(4)]
        t1 = [sb(f"w_t1_{i}", (128, 512), F32) for i in range(2)]
        t2 = [sb(f"w_t2_{i}", (128, 512), F32) for i in range(2)]
        den = sb("w_den", (128, 512), F32)
        rc = 0

        def rope_into(src, srcn, dst_fn, dstn):
            nonlocal rc
            for c in range(2):
                P.emit("pool", lambda: nc.gpsimd.tensor_copy(out=dst_fn(c), in_=src[:, c * 128:(c + 1) * 128]),
                       reads=[srcn], writes=[dstn])
            for q4 in range(4):
                t0 = CTX + q4 * 512
                b = rc % 2
                rc += 1
                pp = K.bank[6 + b]
                P.emit("pe", lambda: nc.tensor.matmul(pp, lhsT=permb, rhs=src[:, t0:t0 + 512], start=True, stop=True),
                       reads=["w_permb", srcn], writes=[f"PS:{6 + b}"])
                P.emit("dve", lambda: nc.vector.tensor_tensor(out=t1[b], in0=pp, in1=sinT[:, q4 * 512:(q4 + 1) * 512], op=ALU.mult),
                       reads=[f"PS:{6 + b}", "w_sin"], writes=[f"w_t1_{b}"])
                P.emit("pool", lambda: nc.gpsimd.tensor_tensor(out=t2[b], in0=src[:, t0:t0 + 512], in1=cosT[:, q4 * 512:(q4 + 1) * 512],
                                                               op=ALU.mult), reads=[srcn, "w_cos"], writes=[f"w_t2_{b}"])
                for c4 in range(4):
                    c = 2 + q4 * 4 + c4
                    P.emit("dve", lambda: nc.vector.tensor_tensor(out=dst_fn(c), in0=t1[b][:, c4 * 128:(c4 + 1) * 128],
                                                                  in1=t2[b][:, c4 * 128:(c4 + 1) * 128], op=ALU.add),
                           reads=[f"w_t1_{b}", f"w_t2_{b}"], writes=[dstn])

        lc = 0
        ec = 0
        bc = 0
        for g in range(2):
            r = raw[lc % 2]
            rn = f"w_raw{lc % 2}"
            lc += 1
            P.emit("q_sp", lambda: nc.sync.dma_start(out=r, in_=K.PT[SK + g * 128:SK + (g + 1) * 128, :]), reads=["PT"], writes=[rn])
            P.emit("q_act", lambda: nc.scalar.dma_start(
                out=vtm, in_=K.PV[:, PV_SV + g * 128:PV_SV + (g + 1) * 128].rearrange("(c p) d -> p c d", p=128)),
                reads=["PV"], writes=["w_vtm"])
            rope_into(r, rn, lambda c: kR[:, c * 128:(c + 1) * 128], "w_kR")
            for j in range(4):
                h = 4 * g + j
                r = raw[lc % 2]
                rn = f"w_raw{lc % 2}"
                lc += 1
                P.emit("q_sp", lambda: nc.sync.dma_start(out=r, in_=K.PT[SQ + h * 128:SQ + (h + 1) * 128, :]), reads=["PT"], writes=[rn])
                rope_into(r, rn, lambda c, j=j: Qb[:, c, j, :], "w_Qb")
            for qb in range(NT128):
                if qb < 2:
                    tiles = [(0, None), (1, None)]
                else:
                    n = qb - 2
                    tiles = []
                    if n > 0:
                        tiles.append((qb - 1, 0))
                    tiles.append((qb, None))
                    if n < 15:
                        tiles.append((qb + 1, 1))
                    tiles += [(0, None), (1, None)]
                ob = bc % 2
                bc += 1
                psO, psOn = K.bank[2 + ob], f"PS:{2 + ob}"
                psD, psDn = K.bank[4 + ob], f"PS:{4 + ob}"
                qrhs = Qb[:, qb].rearrange("p j t -> p (j t)")
                for ti, (kt, mi) in enumerate(tiles):
                    sbk = ec % 2
                    e4 = ec % 4
                    ec += 1
                    psS, psSn = K.bank[sbk], f"PS:{sbk}"
                    E, En = Et[e4], f"w_E{e4}"
                    P.emit("pe", lambda: nc.tensor.matmul(psS, lhsT=kR[:, kt * 128:(kt + 1) * 128], rhs=qrhs, start=True, stop=True),
                           reads=["w_kR", "w_Qb"], writes=[psSn])
                    P.emit("act", lambda: nc.scalar.activation(out=E, in_=psS, func=AF.Exp, scale=SC), reads=[psSn], writes=[En])
                    if mi is not None:
                        P.emit("dve", lambda: nc.vector.tensor_tensor(out=E, in0=E, in1=mk[:, mi, :], op=ALU.mult),
                               reads=[En, "w_mk"], writes=[En])
                    first, last = ti == 0, ti == len(tiles) - 1
                    P.emit("pe", lambda: nc.tensor.matmul(psO, lhsT=vtm[:, kt, :], rhs=E, start=first, stop=last),
                           reads=["w_vtm", En], writes=[psOn])
                    P.emit("pe", lambda: nc.tensor.matmul(psD, lhsT=onesb, rhs=E, start=first, stop=last),
                           reads=["w_onesb", En], writes=[psDn])
                for j in range(4):
                    col = 4 * g + j
                    P.emit("dve", lambda: nc.vector.tensor_scalar(out=den[:, j * 128:(j + 1) * 128], in0=psD[:, j * 128:(j + 1) * 128],
                                                                  scalar1=es[:, col:col + 1], scalar2=None, op0=ALU.add),
                           reads=[psDn, "w_es"], writes=["w_den"])
                P.emit("dve", lambda: nc.vector.reciprocal(out=den, in_=den), reads=["w_den"], writes=["w_den"])
                P.emit("dve", lambda: nc.vector.tensor_tensor(out=ysw[:, :, qb * 128:(qb + 1) * 128],
                                                              in0=psO.rearrange("p (j t) -> p j t", j=4),
                                                              in1=den.rearrange("p (j t) -> p j t", j=4), op=ALU.mult),
                       reads=[psOn, "w_den"], writes=["w_ysw"])
            for j in range(4):
                h = 4 * g + j
                P.emit("q_sp", lambda: nc.sync.dma_start(out=K.YT[1024 + h * 128:1024 + (h + 1) * 128, :], in_=ysw[:, j, :]),
                       reads=["w_ysw"], writes=["YT"])
    P.barrier()


def stage_rwkv(K, l):
    nc, P, I = K.nc, K.P, K.I
    NCH = T // 64
    with contextlib.ExitStack() as st:
        sb = lambda n, s_, d: K.sb(n, s_, d, st)
        E = lambda en, fn, r=(), w=(): P.emit(en, fn, reads=r, writes=w)
        reset = sb("k_reset", (128, T), BF16)
        gmask = sb("k_gmask", (64, 2, 512), F32)
        qmask = sb("k_qmask", (64, 2, 128), F32)
        i2 = sb("k_i2", (64, 128), F32)
        bd64 = sb("k_bd64", (128, 128), F32)
        E("q_sp", lambda: nc.sync.dma_start(out=gmask, in_=I["k_rk_gmask"].rearrange("d p x -> p d x")), w=["k_gmask"])
        E("q_sp", lambda: nc.sync.dma_start(out=qmask, in_=I["k_rk_qmask"].rearrange("d p x -> p d x")), w=["k_qmask"])
        E("q_sp", lambda: nc.sync.dma_start(out=i2, in_=I["k_rk_i2"]), w=["k_i2"])
        E("q_sp", lambda: nc.sync.dma_start(out=bd64, in_=I["k_bd64"]), w=["k_bd64"])
        muT = [sb(f"k_mu{i}", (128, 27), F32) for i in range(2)]
        for i in range(2):
            load_fm_vec(K, st, I["rwkv_mu"][l, i].rearrange("(c p) -> c p", p=128), 27, muT[i], f"k_mu{i}")
        w0T = sb("k_w0T", (128, 16), F32)
        a0T = sb("k_a0T", (128, 16), F32)
        load_fm_vec(K, st, I["rwkv_w0"][l].rearrange("d (c p) -> (d c) p", p=128), 16, w0T, "k_w0T")
        load_fm_vec(K, st, I["rwkv_a0"][l].rearrange("d (c p) -> (d c) p", p=128), 16, a0T, "k_a0T")
        kkT = sb("k_kkT", (128, 8), F32)
        kaT = sb("k_kaT", (128, 8), F32)
        ka1 = sb("k_ka1", (128, 8), F32)
        rkT = sb("k_rkT", (128, 8), F32)
        lnT = sb("k_lnT", (128, 8), F32)
        load_fm_vec(K, st, I["rwkv_k_k"][l].rearrange("(c p) -> c p", p=128), 8, kkT, "k_kkT")
        load_fm_vec(K, st, I["rwkv_k_a"][l].rearrange("(c p) -> c p", p=128), 8, kaT, "k_kaT")
        load_fm_vec(K, st, I["rwkv_r_k"][l].rearrange("(c e) n -> c (e n)", e=2), 8, rkT, "k_rkT")
        load_fm_vec(K, st, I["rwkv_ln_g"][l].rearrange("(c p) -> c p", p=128), 8, lnT, "k_lnT")
        E("dve", lambda: nc.vector.tensor_scalar(out=ka1, in0=kaT, scalar1=-1.0, scalar2=1.0, op0=ALU.mult, op1=ALU.add),
          ["k_kaT"], ["k_ka1"])
        up32 = sb("k_up32", (128, 1024), F32)
        wupb = sb("k_wupb", (128, 1024), BF16)
        aupb = sb("k_aupb", (128, 1024), BF16)
        gupb = sb("k_gupb", (128, 1024), BF16)
        for src, dst, dn in ((I["rwkv_w_up"][l].rearrange("d r w -> (d r) w"), wupb, "k_wupb"),
                             (I["rwkv_a_up"][l].rearrange("d r w -> (d r) w"), aupb, "k_aupb"),
                             (I["rwkv_g_up"][l], gupb, "k_gupb")):
            E("q_sp", lambda: nc.sync.dma_start(out=up32, in_=src), w=["k_up32"])
            E("dve", lambda: nc.vector.tensor_copy(out=dst, in_=up32), ["k_up32"], [dn])
        raw = sb("k_raw", (128, T), BF16)
        rs = sb("k_rs", (128, T), F32)
        ks = sb("k_ks", (128, T), F32)
        vb = sb("k_vb", (128, T), BF16)
        T0 = sb("k_T0", (128, T), F32)
        T1 = sb("k_T1", (128, T), F32)
        T2 = sb("k_T2", (128, T), F32)
        kk = sb("k_kk", (128, T), F32)
        Lam = sb("k_Lam", (128, T), F32)
        ad = T1
        asum = sb("k_asum", (128, T), F32)
        yacc = sb("k_yacc", (128, T), F32)
        ar = sb("k_ar", (128, NCH, 2, 64), BF16)
        bk = sb("k_bk", (128, NCH, 2, 64), BF16)
        BKf = sb("k_BKf", (128, NCH, 2, 64), BF16)
        LC = sb("k_LC", (128, NCH), F32)
        GC = sb("k_GC", (128, NCH), F32)
        twd = sb("k_twd", (128, T), BF16)
        adT = sb("k_adT", (128, T), BF16)
        sgd = sb("k_sgd", (128, T), BF16)
        c3 = lambda a: a.rearrange("p (c t) -> p c t", t=64)
        E("q_sp", lambda: nc.sync.dma_start(out=T0, in_=I["k_rk_reset"]), w=["k_T0"])
        E("dve", lambda: nc.vector.tensor_copy(out=reset, in_=T0), ["k_T0"], ["k_reset"])

        def shift(row0, mi, dst, dstn):
            E("q_sp", lambda: nc.sync.dma_start(out=raw, in_=K.PT[row0:row0 + 128, :]), ["PT"], ["k_raw"])
            E("dve", lambda: nc.vector.tensor_tensor(out=T0[:, 1:T], in0=raw[:, 0:T - 1], in1=raw[:, 1:T], op=ALU.subtract),
              ["k_raw"], ["k_T0"])
            E("pool", lambda: nc.gpsimd.tensor_tensor(out=T1[:, 0:T - 1], in0=raw[:, 1:T], in1=raw[:, 0:T - 1], op=ALU.subtract),
              ["k_raw"], ["k_T1"])
            for b0 in (0, CTX):
                E("dve", lambda: nc.vector.tensor_scalar(out=T0[:, b0:b0 + 1], in0=raw[:, b0:b0 + 1], scalar1=-1.0, scalar2=None,
                                                         op0=ALU.mult), ["k_raw", "k_T0"], ["k_T0"])
            for b1 in (CTX - 1, T - 1):
                E("pool", lambda: nc.gpsimd.tensor_scalar(out=T1[:, b1:b1 + 1], in0=raw[:, b1:b1 + 1], scalar1=-1.0, scalar2=None,
                                                          op0=ALU.mult), ["k_raw", "k_T1"], ["k_T1"])
            E("dve", lambda: nc.vector.scalar_tensor_tensor(out=T0, in0=T0, scalar=muT[0][:, mi:mi + 1], in1=raw, op0=ALU.mult, op1=ALU.add),
              ["k_T0", "k_mu0", "k_raw"], ["k_T0"])
            E("dve", lambda: nc.vector.scalar_tensor_tensor(out=dst, in0=T1, scalar=muT[1][:, mi:mi + 1], in1=T0, op0=ALU.mult, op1=ALU.add),
              ["k_T1", "k_mu1", "k_T0"], [dstn])

        shift(RW + 3072, 24, T2, "k_T2")
        E("act", lambda: nc.scalar.activation(out=twd, in_=T2, func=AF.Tanh), ["k_T2"], ["k_twd"])
        shift(RW + 3200, 25, T2, "k_T2")
        E("act", lambda: nc.scalar.copy(out=adT, in_=T2), ["k_T2"], ["k_adT"])
        shift(RW + 3328, 26, T2, "k_T2")
        E("act", lambda: nc.scalar.activation(out=sgd, in_=T2, func=AF.Sigmoid), ["k_T2"], ["k_sgd"])

        TMb = [sb(f"a_TM{i}", (64, 4, 384), BF16) for i in range(2)]
        Vpb = [sb(f"a_Vp{i}", (64, 4, 2, 128), BF16) for i in range(2)]
        Up = [sb(f"k_Up{i}", (64, 2, 128), BF16) for i in range(2)]
        Ub = [sb(f"k_Ub{i}", (64, 128), BF16) for i in range(2)]
        W32 = sb("k_W32", (64, 128), F32)
        H32 = sb("k_H32", (128, 128), F32)
        Hb = sb("k_Hb", (128, 128), BF16)
        i8 = sb("k_i8", (64, 512), F32)
        for i in range(4):
            E("dve", lambda: nc.vector.tensor_copy(out=i8[:, i * 128:(i + 1) * 128], in_=i2), ["k_i2"], ["k_i8"])
        for i in range(2):
            E("dve", lambda: nc.vector.memset(Vpb[i], 0.0), w=[f"a_Vp{i}"])
            E("dve", lambda: nc.vector.memset(Up[i], 0.0), w=[f"k_Up{i}"])
        outb = [sb(f"k_ob{i}", (128, 512), BF16) for i in range(2)]
        cc = 0
        oc = 0

        for hp in range(8):
            cols = slice(hp * 128, (hp + 1) * 128)
            shift(RW + hp * 128, hp, rs, "k_rs")
            shift(RW + 1024 + hp * 128, 8 + hp, ks, "k_ks")
            shift(RW + 2048 + hp * 128, 16 + hp, T2, "k_T2")
            E("act", lambda: nc.scalar.copy(out=vb, in_=T2), ["k_T2"], ["k_vb"])
            E("dve", lambda: nc.vector.tensor_scalar(out=T0, in0=ks, scalar1=kkT[:, hp:hp + 1], scalar2=None, op0=ALU.mult),
              ["k_ks", "k_kkT"], ["k_T0"])
            E("act", lambda: nc.scalar.activation(out=T1, in_=T0, func=AF.Square), ["k_T0"], ["k_T1"])
            for (t0, tn) in TT512:
                tsl = slice(t0, t0 + tn)
                E("pe", lambda: nc.tensor.matmul(K.bank[0][:, :tn], lhsT=bd64, rhs=T1[:, tsl], start=True, stop=True),
                  ["k_bd64", "k_T1"], ["PS:0"])
                E("act", lambda: nc.scalar.activation(out=T2[:, tsl], in_=K.bank[0][:, :tn], func=AF.Sqrt), ["PS:0"], ["k_T2"])
            E("dve", lambda: nc.vector.tensor_scalar(out=T2, in0=T2, scalar1=1e-12, scalar2=None, op0=ALU.max), ["k_T2"], ["k_T2"])
            E("dve", lambda: nc.vector.reciprocal(out=T2, in_=T2), ["k_T2"], ["k_T2"])
            E("dve", lambda: nc.vector.tensor_tensor(out=kk, in0=T0, in1=T2, op=ALU.mult), ["k_T0", "k_T2"], ["k_kk"])
            for d in range(2):
                dsl = slice(d * 64, (d + 1) * 64)
                for (t0, tn) in TT512:
                    tsl = slice(t0, t0 + tn)
                    E("pe", lambda: nc.tensor.matmul(K.bank[0][:, :tn], lhsT=wupb[dsl, cols], rhs=twd[dsl, tsl], start=True, stop=True),
                      ["k_wupb", "k_twd"], ["PS:0"])
                    E("act", lambda: nc.scalar.activation(out=T2[:, tsl], in_=K.bank[0][:, :tn], func=AF.Sigmoid,
                                                          bias=w0T[:, d * 8 + hp:d * 8 + hp + 1]), ["PS:0", "k_w0T"], ["k_T2"])
                    E("pe", lambda: nc.tensor.matmul(K.bank[1][:, :tn], lhsT=aupb[dsl, cols], rhs=adT[dsl, tsl], start=True, stop=True),
                      ["k_aupb", "k_adT"], ["PS:1"])
                    E("act", lambda: nc.scalar.activation(out=ad[:, tsl], in_=K.bank[1][:, :tn], func=AF.Sigmoid,
                                                          bias=a0T[:, d * 8 + hp:d * 8 + hp + 1]), ["PS:1", "k_a0T"], ["k_T1"])
                E("dve", lambda: nc.vector.tensor_scalar(out=T2, in0=T2, scalar1=-0.6065306597126334, scalar2=None, op0=ALU.mult),
                  ["k_T2"], ["k_T2"])
                if d == 0:
                    E("pool", lambda: nc.gpsimd.tensor_copy(out=asum, in_=ad), ["k_T1"], ["k_asum"])
                else:
                    E("pool", lambda: nc.gpsimd.tensor_tensor(out=asum, in0=asum, in1=ad, op=ALU.add), ["k_T1", "k_asum"], ["k_asum"])
                E("dve", lambda: nc.vector.tensor_tensor_scan(out=Lam, data0=reset, data1=T2, initial=0.0, op0=ALU.mult, op1=ALU.add),
                  ["k_reset", "k_T2"], ["k_Lam"])
                E("dve", lambda: nc.vector.tensor_copy(out=LC, in_=c3(Lam)[:, :, 63]), ["k_Lam"], ["k_LC"])
                if d == 1:
                    for c in range(NCH):
                        csl = slice(c * 64, (c + 1) * 64)
                        E("pool", lambda: nc.gpsimd.tensor_scalar(out=Lam[:, csl], in0=Lam[:, csl], scalar1=-1.0, scalar2=LC[:, c:c + 1],
                                                                  op0=ALU.mult, op1=ALU.add), ["k_Lam", "k_LC"], ["k_Lam"])
                    E("dve", lambda: nc.vector.tensor_tensor(out=Lam, in0=Lam, in1=T2, op=ALU.add), ["k_Lam", "k_T2"], ["k_Lam"])
                E("act", lambda: nc.scalar.activation(out=GC, in_=LC, func=AF.Exp), ["k_LC"], ["k_GC"])
                E("act", lambda: nc.scalar.activation(out=T0, in_=Lam, func=AF.Exp), ["k_Lam"], ["k_T0"])
                E("dve", lambda: nc.vector.tensor_tensor(out=ar[:, :, 1, :], in0=c3(rs), in1=c3(T0), op=ALU.mult), ["k_rs", "k_T0"], ["k_ar"])
                E("dve", lambda: nc.vector.tensor_tensor(out=T0, in0=Lam, in1=T2, op=ALU.subtract), ["k_Lam", "k_T2"], ["k_T0"])
                E("act", lambda: nc.scalar.activation(out=T0, in_=T0, func=AF.Exp), ["k_T0"], ["k_T0"])
                E("dve", lambda: nc.vector.scalar_tensor_tensor(out=ar[:, :, 0, :], in0=c3(kk), scalar=-1.0, in1=c3(T0), op0=ALU.mult, op1=ALU.mult),
                  ["k_kk", "k_T0"], ["k_ar"])
                E("act", lambda: nc.scalar.activation(out=T2, in_=Lam, func=AF.Exp, scale=-1.0), ["k_Lam"], ["k_T2"])
                E("pool", lambda: nc.gpsimd.tensor_tensor(out=T0, in0=kk, in1=ad, op=ALU.mult), ["k_kk", "k_T1"], ["k_T0"])
                E("dve", lambda: nc.vector.tensor_scalar(out=T1, in0=ad, scalar1=kaT[:, hp:hp + 1], scalar2=ka1[:, hp:hp + 1],
                                                         op0=ALU.mult, op1=ALU.add), ["k_T1", "k_kaT", "k_ka1"], ["k_T1"])
                E("dve", lambda: nc.vector.tensor_tensor(out=T1, in0=T1, in1=ks, op=ALU.mult), ["k_T1", "k_ks"], ["k_T1"])
                E("dve", lambda: nc.vector.tensor_tensor(out=bk[:, :, 0, :], in0=c3(T0), in1=c3(T2), op=ALU.mult), ["k_T0", "k_T2"], ["k_bk"])
                E("pool", lambda: nc.gpsimd.tensor_tensor(out=bk[:, :, 1, :], in0=c3(T1), in1=c3(T2), op=ALU.mult), ["k_T1", "k_T2"], ["k_bk"])
                for c in range(NCH):
                    csl = slice(c * 64, (c + 1) * 64)
                    E("act", lambda: nc.scalar.activation(out=T2[:, csl], in_=Lam[:, csl], func=AF.Exp, scale=-1.0, bias=LC[:, c:c + 1]),
                      ["k_Lam", "k_LC"], ["k_T2"])
                E("dve", lambda: nc.vector.tensor_tensor(out=BKf[:, :, 0, :], in0=c3(T0), in1=c3(T2), op=ALU.mult), ["k_T0", "k_T2"], ["k_BKf"])
                E("pool", lambda: nc.gpsimd.tensor_tensor(out=BKf[:, :, 1, :], in0=c3(T1), in1=c3(T2), op=ALU.mult), ["k_T1", "k_T2"], ["k_BKf"])
                P.barrier()
                G32b = T0[0:64, 0:2048].rearrange("p (c x) -> p c x", c=4)
                Pb = [T1[0:64, i * 512:(i + 1) * 512] for i in range(2)]
                Qb = [T1[0:64, 1024 + i * 512:1024 + (i + 1) * 512] for i in range(2)]
                Ac = [T2[0:64, i * 512:(i + 1) * 512] for i in range(2)]
                TTb = [T2[0:64, 1024 + i * 512:1024 + (i + 1) * 512] for i in range(2)]
                Lb = Lam.bitcast(BF16)
                G16b = [Lb[0:64, i * 2048:(i + 1) * 2048].rearrange("p (c x) -> p c x", c=4) for i in range(2)]
                E("dve", lambda: nc.vector.memset(H32, 0.0), w=["k_H32"])
                E("dve", lambda: nc.vector.memset(Hb, 0.0), w=["k_Hb"])
                order = list(range(NCH)) if d == 0 else [3, 2, 1, 0] + list(range(NCH - 1, 3, -1))
                batches = [order[i:i + 4] for i in range(0, NCH, 4)][:RW_NB]

                def phaseA(cl, bi):
                    tb = K.bank[7].bitcast(BF16)
                    TM, Vp4 = TMb[bi], Vpb[bi]
                    for half in range(2):
                        for q2 in range(2):
                            c = cl[half * 2 + q2]
                            for q in range(2):
                                E("pe", lambda: nc.tensor.transpose(out=tb[0:64, q2 * 384 + q * 128:q2 * 384 + (q + 1) * 128],
                                                                    in_=BKf[:, c, q, :], identity=K.ident_b), ["k_BKf", "ident_b"], ["PS:7"])
                            E("pe", lambda: nc.tensor.transpose(out=tb[0:64, q2 * 384 + 256:q2 * 384 + 384], in_=vb[:, c * 64:(c + 1) * 64],
                                                                identity=K.ident_b), ["k_vb", "ident_b"], ["PS:7"])
                        hs = slice(half * 2, half * 2 + 2)
                        E("act", lambda: nc.scalar.copy(out=TM[:, hs, :], in_=tb[0:64, 0:768].rearrange("p (c x) -> p c x", c=2)),
                          ["PS:7"], [f"a_TM{bi}"])
                        E("act", lambda: nc.scalar.copy(out=Vp4[:, hs, 0, 0:64], in_=TM[:, hs, 256:320]), [f"a_TM{bi}"], [f"a_Vp{bi}"])
                        E("dve", lambda: nc.vector.tensor_copy(out=Vp4[:, hs, 1, 64:128], in_=TM[:, hs, 320:384]), [f"a_TM{bi}"], [f"a_Vp{bi}"])
                        yield
                    for half in range(2):
                        hs = slice(half * 2, half * 2 + 2)
                        for q2 in range(2):
                            c = cl[half * 2 + q2]
                            for e in range(2):
                                esl = slice(e * 64, (e + 1) * 64)
                                arc = ar[esl, c].rearrange("p a t -> p (a t)")
                                bnk, bn = K.bank[e], f"PS:{e}"
                                E("pe", lambda: nc.tensor.matmul(bnk[0:64, q2 * 256:q2 * 256 + 128], lhsT=bk[esl, c, 0, :], rhs=arc,
                                                                 start=True, stop=True), ["k_bk", "k_ar"], [bn])
                                E("pe", lambda: nc.tensor.matmul(bnk[0:64, q2 * 256 + 128:q2 * 256 + 256], lhsT=bk[esl, c, 1, :], rhs=arc,
                                                                 start=True, stop=True), ["k_bk", "k_ar"], [bn])
                        for e in range(2):
                            E("dve", lambda: nc.vector.tensor_tensor(out=G32b[:, hs, e * 256:(e + 1) * 256],
                                                                     in0=K.bank[e][0:64, :].rearrange("p (c x) -> p c x", c=2),
                                                                     in1=gmask[:, d, :].rearrange("p (c x) -> p c x", c=2), op=ALU.mult),
                              [f"PS:{e}", "k_gmask"], ["a_G32"])
                        yield
                    E("act", lambda: nc.scalar.copy(out=G16b[bi], in_=G32b), ["a_G32"], [f"a_G16_{bi}"])
                    Nv = G32b.rearrange("p c (e x) -> p c e x", e=2)[:, :, :, 0:64]
                    E("dve", lambda: nc.vector.tensor_tensor(out=Ac[0].rearrange("p (c e x) -> p c e x", c=4, e=2), in0=Nv,
                                                             in1=i8.rearrange("p (c e x) -> p c e x", c=4, e=2), op=ALU.add),
                      ["a_G32", "k_i8"], ["a_Ac0"])
                    for ci in range(4):
                        for e in range(2):
                            p_ = ci * 2 + e
                            E("pe", lambda: nc.tensor.transpose(out=K.bank[5][0:64, p_ * 64:(p_ + 1) * 64], in_=G32b[:, ci, e * 256:e * 256 + 64],
                                                                identity=K.ident_f[0:64, 0:64]), ["a_G32", "ident_f"], ["PS:5"])
                    E("act", lambda: nc.scalar.copy(out=Qb[0], in_=K.bank[5][0:64, :]), ["PS:5"], ["a_Qb0"])
                    yield
                    for kx in range(1, 6):
                        pi, po = (kx - 1) % 2, kx % 2
                        for ci in range(4):
                            for e in range(2):
                                p_ = ci * 2 + e
                                psl = slice(p_ * 64, (p_ + 1) * 64)
                                Pprev = G32b[:, ci, e * 256:e * 256 + 64] if kx == 1 else Pb[pi][:, psl]
                                Pn = "a_G32" if kx == 1 else f"a_Pb{pi}"
                                if kx < 5:
                                    E("pe", lambda: nc.tensor.matmul(K.bank[4][0:64, psl], lhsT=Qb[pi][:, psl], rhs=Pprev, start=True, stop=True),
                                      [f"a_Qb{pi}", Pn], ["PS:4"])
                                E("pe", lambda: nc.tensor.matmul(K.bank[5][0:64, psl], lhsT=Pprev, rhs=Qb[pi][:, psl], start=True, stop=True),
                                  [f"a_Qb{pi}", Pn], ["PS:5"])
                        if kx < 5:
                            E("dve", lambda: nc.vector.tensor_copy(out=Pb[po], in_=K.bank[4][0:64, :]), ["PS:4"], [f"a_Pb{po}"])
                        E("act", lambda: nc.scalar.copy(out=Qb[po], in_=K.bank[5][0:64, :]), ["PS:5"], [f"a_Qb{po}"])
                        yield
                        for p_ in range(8):
                            psl = slice(p_ * 64, (p_ + 1) * 64)
                            E("pe", lambda: nc.tensor.matmul(K.bank[6][0:64, psl], lhsT=Qb[po][:, psl], rhs=Ac[pi][:, psl], start=True, stop=True),
                              [f"a_Qb{po}", f"a_Ac{pi}"], ["PS:6"])
                        dst, dn = (TTb[bi], f"a_TT{bi}") if kx == 5 else (Ac[po], f"a_Ac{po}")
                        E("dve", lambda: nc.vector.tensor_tensor(out=dst, in0=Ac[pi], in1=K.bank[6][0:64, :], op=ALU.add),
                          [f"a_Ac{pi}", "PS:6"], [dn])
                        yield

                def phaseB(c, ci, bi):
                    nonlocal cc
                    csl = slice(c * 64, (c + 1) * 64)
                    j = cc % 2
                    cc += 1
                    G16 = G16b[bi][:, ci, :]
                    TM = TMb[bi][:, ci, :]
                    Vp = Vpb[bi][:, ci]
                    gn, tn_, vn, ttn = f"a_G16_{bi}", f"a_TM{bi}", f"a_Vp{bi}", f"a_TT{bi}"
                    b2, b3 = K.bank[2], K.bank[3]
                    E("pe", lambda: nc.tensor.matmul(b2[0:64, 0:128], lhsT=ar[:, c, 0, :], rhs=Hb, start=True, stop=True), ["k_ar", "k_Hb"], ["PS:2"])
                    for e in range(2):
                        e6 = slice(e * 64, (e + 1) * 64)
                        E("pe", lambda: nc.tensor.matmul(b2[0:64, 128 + e * 64:128 + (e + 1) * 64], lhsT=G16[:, e * 256 + 128:e * 256 + 192],
                                                         rhs=TM[:, 256 + e * 64:256 + (e + 1) * 64], start=True, stop=True), [gn, tn_], ["PS:2"])
                    E("act", lambda: nc.scalar.copy(out=W32, in_=b2[0:64, 0:128]), ["PS:2"], ["k_W32"])
                    E("dve", lambda: nc.vector.tensor_tensor(out=W32, in0=W32, in1=b2[0:64, 128:256], op=ALU.add), ["k_W32", "PS:2"], ["k_W32"])
                    for e in range(2):
                        e6 = slice(e * 64, (e + 1) * 64)
                        p_ = ci * 2 + e
                        E("pe", lambda: nc.tensor.matmul(b2[0:64, 256 + e * 64:256 + (e + 1) * 64], lhsT=TTb[bi][:, p_ * 64:(p_ + 1) * 64],
                                                         rhs=W32[:, e6], start=True, stop=True), [ttn, "k_W32"], ["PS:2"])
                    E("act", lambda: nc.scalar.copy(out=Ub[j], in_=b2[0:64, 256:384]), ["PS:2"], [f"k_Ub{j}"])
                    E("pe", lambda: nc.tensor.matmul(b3[:, 128:256], lhsT=TM[:, 0:128], rhs=Ub[j], start=True, stop=False), [tn_, f"k_Ub{j}"], ["PS:3"])
                    E("pe", lambda: nc.tensor.matmul(b3[:, 128:256], lhsT=TM[:, 128:256], rhs=TM[:, 256:384], start=False, stop=True), [tn_], ["PS:3"])
                    E("act", lambda: nc.scalar.copy(out=Up[j][:, 0, 0:64], in_=Ub[j][:, 0:64]), [f"k_Ub{j}"], [f"k_Up{j}"])
                    E("dve", lambda: nc.vector.tensor_copy(out=Up[j][:, 1, 64:128], in_=Ub[j][:, 64:128]), [f"k_Ub{j}"], [f"k_Up{j}"])
                    b1 = K.bank[3]
                    E("pe", lambda: nc.tensor.matmul(b1[:, 0:64], lhsT=Hb, rhs=ar[:, c, 1, :], start=True, stop=False), ["k_Hb", "k_ar"], ["PS:3"])
                    for e in range(2):
                        E("pe", lambda: nc.tensor.matmul(b1[:, 0:64], lhsT=Up[j][:, e, :], rhs=G16[:, e * 256 + 64:e * 256 + 128],
                                                         start=False, stop=False), [f"k_Up{j}", gn], ["PS:3"])
                        E("pe", lambda: nc.tensor.matmul(b1[:, 0:64], lhsT=Vp[:, e, :], rhs=G16[:, e * 256 + 192:e * 256 + 256],
                                                         start=False, stop=(e == 1)), [vn, gn], ["PS:3"])
                    E("dve", lambda: nc.vector.scalar_tensor_tensor(out=H32, in0=H32, scalar=GC[:, c:c + 1], in1=b3[:, 128:256],
                                                                    op0=ALU.mult, op1=ALU.add), ["k_H32", "k_GC", "PS:3"], ["k_H32"])
                    E("dve", lambda: nc.vector.tensor_tensor(out=Hb, in0=H32, in1=bd64, op=ALU.mult), ["k_H32", "k_bd64"], ["k_Hb"])
                    if d == 0:
                        E("act", lambda: nc.scalar.copy(out=yacc[:, csl], in_=b1[:, 0:64]), ["PS:3"], ["k_yacc"])
                    else:
                        E("dve", lambda: nc.vector.tensor_tensor(out=yacc[:, csl], in0=yacc[:, csl], in1=b1[:, 0:64], op=ALU.add),
                          ["PS:3", "k_yacc"], ["k_yacc"])

                for _ in (phaseA(batches[0], 0) if batches else ()):
                    pass
                for bn in range(len(batches)):
                    nxt = phaseA(batches[bn + 1], (bn + 1) % 2) if bn + 1 < len(batches) else None
                    for ci, c in enumerate(batches[bn]):
                        phaseB(c, ci, bn % 2)
                        if nxt is not None:
                            for _ in range(4):
                                next(nxt, None)
                    if nxt is not None:
                        for _ in nxt:
                            pass
                P.barrier()
            for (t0, tn) in TT512:
                tsl = slice(t0, t0 + tn)
                a_, b_, c_ = T1[:, tsl], T2[:, tsl], Lam[:, tsl]
                E("dve", lambda: nc.vector.tensor_scalar(out=a_, in0=asum[:, tsl], scalar1=kaT[:, hp:hp + 1], scalar2=None, op0=ALU.mult),
                  ["k_asum", "k_kaT"], ["k_T1"])
                E("dve", lambda: nc.vector.tensor_scalar(out=a_, in0=a_, scalar1=ka1[:, hp:hp + 1], scalar2=ka1[:, hp:hp + 1],
                                                         op0=ALU.add, op1=ALU.add), ["k_T1", "k_ka1"], ["k_T1"])
                E("dve", lambda: nc.vector.tensor_tensor(out=a_, in0=a_, in1=ks[:, tsl], op=ALU.mult), ["k_T1", "k_ks"], ["k_T1"])
                E("dve", lambda: nc.vector.scalar_tensor_tensor(out=a_, in0=rs[:, tsl], scalar=rkT[:, hp:hp + 1], in1=a_, op0=ALU.mult, op1=ALU.mult),
                  ["k_rs", "k_rkT", "k_T1"], ["k_T1"])
                E("pe", lambda: nc.tensor.matmul(K.bank[0][:, :tn], lhsT=bd64, rhs=a_, start=True, stop=True), ["k_bd64", "k_T1"], ["PS:0"])
                E("dve", lambda: nc.vector.tensor_tensor(out=a_, in0=K.bank[0][:, :tn], in1=vb[:, tsl], op=ALU.mult), ["PS:0", "k_vb"], ["k_T1"])
                E("act", lambda: nc.scalar.activation(out=b_, in_=yacc[:, tsl], func=AF.Square), ["k_yacc"], ["k_T2"])
                E("pe", lambda: nc.tensor.matmul(K.bank[1][:, :tn], lhsT=bd64, rhs=yacc[:, tsl], start=True, stop=True), ["k_bd64", "k_yacc"], ["PS:1"])
                E("pe", lambda: nc.tensor.matmul(K.bank[2][:, :tn], lhsT=bd64, rhs=b_, start=True, stop=True), ["k_bd64", "k_T2"], ["PS:2"])
                E("dve", lambda: nc.vector.tensor_scalar(out=c_, in0=K.bank[1][:, :tn], scalar1=1.0 / 64.0, scalar2=None, op0=ALU.mult),
                  ["PS:1"], ["k_Lam"])
                E("act", lambda: nc.scalar.activation(out=b_, in_=c_, func=AF.Square), ["k_Lam"], ["k_T2"])
                E("dve", lambda: nc.vector.scalar_tensor_tensor(out=b_, in0=K.bank[2][:, :tn], scalar=1.0 / 64.0, in1=b_, op0=ALU.mult, op1=ALU.subtract),
                  ["PS:2", "k_T2"], ["k_T2"])
                E("dve", lambda: nc.vector.tensor_scalar(out=b_, in0=b_, scalar1=64e-5, scalar2=None, op0=ALU.add), ["k_T2"], ["k_T2"])
                E("act", lambda: nc.scalar.activation(out=b_, in_=b_, func=AF.Sqrt), ["k_T2"], ["k_T2"])
                E("dve", lambda: nc.vector.reciprocal(out=b_, in_=b_), ["k_T2"], ["k_T2"])
                E("dve", lambda: nc.vector.tensor_tensor(out=c_, in0=yacc[:, tsl], in1=c_, op=ALU.subtract), ["k_yacc", "k_Lam"], ["k_Lam"])
                E("dve", lambda: nc.vector.tensor_tensor(out=c_, in0=c_, in1=b_, op=ALU.mult), ["k_Lam", "k_T2"], ["k_Lam"])
                E("dve", lambda: nc.vector.scalar_tensor_tensor(out=c_, in0=c_, scalar=lnT[:, hp:hp + 1], in1=a_, op0=ALU.mult, op1=ALU.add),
                  ["k_Lam", "k_lnT", "k_T1"], ["k_Lam"])
                E("pe", lambda: nc.tensor.matmul(K.bank[3][:, :tn], lhsT=gupb[:, cols], rhs=sgd[:, tsl], start=True, stop=True),
                  ["k_gupb", "k_sgd"], ["PS:3"])
                ob, obn = outb[oc % 2], f"k_ob{oc % 2}"
                oc += 1
                E("dve", lambda: nc.vector.tensor_tensor(out=ob[:, :tn], in0=c_, in1=K.bank[3][:, :tn], op=ALU.mult), ["k_Lam", "PS:3"], [obn])
                E("q_sp", lambda: nc.sync.dma_start(out=K.YT[2048 + hp * 128:2048 + (hp + 1) * 128, tsl], in_=ob[:, :tn]), [obn], ["YT"])
    P.barrier()


RW_NB = 9
TTL = 384
NSUB = TTL // 128


def bcast_mod(K, l, col0, r, dst, dstn, rowbuf, ones1):
    nc, P = K.nc, K.P
    P.emit("q_act", lambda: nc.scalar.dma_start(out=rowbuf, in_=K.modrow[l, r:r + 1, col0:col0 + D]), reads=["modrow"], writes=["t_rowbuf"])
    for j in range(4):
        P.emit("pe", lambda: nc.tensor.matmul(K.bank[7], lhsT=ones1, rhs=rowbuf[:, j * 512:(j + 1) * 512], start=True, stop=True),
               reads=["t_rowbuf", "t_ones1"], writes=["PS:7"])
        P.emit("act", lambda: nc.scalar.copy(out=dst[:, j * 512:(j + 1) * 512], in_=K.bank[7]), reads=["PS:7"], writes=[dstn])


def rms_to_fm(K, hsrc, hsn, ss, ssn, yb, ybn, junk, A, An, l, jsh, r, dst_fn, dstn):
    nc, P = K.nc, K.P
    P.emit("act", lambda: nc.scalar.activation(out=junk, in_=hsrc, func=AF.Square, accum_out=ss), reads=[hsn], writes=["junk", ssn])
    P.emit("dve", lambda: nc.vector.tensor_scalar(out=ss, in0=ss, scalar1=1.0 / D, scalar2=EPS, op0=ALU.mult, op1=ALU.add),
           reads=[ssn], writes=[ssn])
    P.emit("act", lambda: nc.scalar.activation(out=ss, in_=ss, func=AF.Sqrt), reads=[ssn], writes=[ssn])
    P.emit("dve", lambda: nc.vector.reciprocal(out=ss, in_=ss), reads=[ssn], writes=[ssn])
    P.emit("act", lambda: nc.scalar.activation(out=yb, in_=hsrc, func=AF.Copy, scale=ss), reads=[hsn, ssn], writes=[ybn])
    for g4 in range(4):
        pt = K.bank[g4].bitcast(BF16)[:, :512]
        for j in range(4):
            dc = g4 * 4 + j
            P.emit("pe", lambda: nc.tensor.transpose(out=pt[:, j * 128:(j + 1) * 128], in_=yb[:, dc * 128:(dc + 1) * 128],
                                                      identity=K.ident_b), reads=[ybn, "ident_b"], writes=[f"PS:{g4}"])
        for j in range(4):
            dc = g4 * 4 + j
            P.emit("dve", lambda: nc.vector.tensor_scalar(out=dst_fn(dc), in0=pt[:, j * 128:(j + 1) * 128],
                                                          scalar1=A[:, dc, r:r + 1], scalar2=modv(K, l, jsh + dc, r),
                                                          op0=ALU.mult, op1=ALU.add),
                   reads=[f"PS:{g4}", An, "modT"], writes=[dstn])


def stage_tail(K, l):
    nc, P, I, L = K.nc, K.P, K.I, K.L
    wbr, wout, wf1, wf2 = K.wb[l]["br"], K.wb[l]["out"], K.wb[l]["ff1"], K.wb[l]["ff2"]
    modv4 = K.modT.rearrange("p (l j r) -> p l j r", l=L, j=96, r=2)
    with contextlib.ExitStack() as st:
        sb = lambda n, s_, d: K.sb(n, s_, d, st)
        wst = [sb(f"t_w{i}", (128, 24, 512), BF16) for i in range(2)]
        R1 = sb("t_R1", (128, 64 * TTL), BF16)
        yT = R1[:, 0:24 * TTL].rearrange("p (c t) -> p c t", c=24)
        gbuf = [R1[:, (24 + 12 * i) * TTL:(36 + 12 * i) * TTL].rearrange("p (c t) -> p c t", c=12) for i in range(2)]
        mT = R1[:, 48 * TTL:64 * TTL].rearrange("p (c t) -> p c t", c=16)
        hidT = R1.rearrange("p (c t) -> p c t", c=64)
        ht = sb("t_h", (128, NSUB, D), F32)
        u2T = sb("t_u2T", (128, KC, TTL), BF16)
        gl = [sb(f"t_gl{i}", (128, D), F32) for i in range(2)]
        gc = sb("t_gc", (128, D), F32)
        rowbuf = sb("t_rowbuf", (1, D), F32)
        ones1 = sb("t_ones1", (1, 128), F32)
        tmpa = [sb(f"t_ta{i}", (128, 512), F32) for i in range(3)]
        gT = sb("t_gT", (128, KC), F32)
        A = sb("t_A2", (128, KC, 2), F32)
        ss = sb("t_ss", (128, 1), F32)
        yb = sb("t_yb", (128, D), BF16)
        junk = sb("t_junk", (128, D), BF16)
        P.emit("dve", lambda: nc.vector.memset(ones1, 1.0), writes=["t_ones1"])
        bcast_mod(K, l, 2 * D, 0, gl[0], "t_gl0", rowbuf, ones1)
        bcast_mod(K, l, 5 * D, 0, gl[1], "t_gl1", rowbuf, ones1)
        load_fm_vec(K, st, I["norm2_g"][l].rearrange("(kc p) -> kc p", p=128), KC, gT, "t_gT")
        for r in range(2):
            P.emit("dve", lambda: nc.vector.tensor_scalar(out=A[:, :, r], in0=modv4[:, l, 64:80, r], scalar1=1.0,
                                                          scalar2=None, op0=ALU.add), reads=["modT"], writes=["t_A2"])
            P.emit("dve", lambda: nc.vector.tensor_tensor(out=A[:, :, r], in0=A[:, :, r], in1=gT, op=ALU.mult),
                   reads=["t_A2", "t_gT"], writes=["t_A2"])
        cnt = dict(w=0, p=0, g=0)

        def wload(src_ap, nk, rn):
            b = cnt["w"] % 2
            cnt["w"] += 1
            P.emit("q_sp", lambda: nc.sync.dma_start(out=wst[b][:, :nk, :], in_=src_ap), reads=[rn], writes=[f"t_w{b}"])
            return wst[b], f"t_w{b}"

        def nextbank():
            b = cnt["p"] % 7
            cnt["p"] += 1
            return K.bank[b], f"PS:{b}"

        def resid(ps, pn, s, cg, which, has_ctx):
            cols = slice(cg * 512, (cg + 1) * 512)
            if has_ctx and s < 2:
                g, gn = gc, "t_gc"
            else:
                g, gn = gl[which], f"t_gl{which}"
            P.emit("dve", lambda: nc.vector.tensor_tensor(out=tmpa[0], in0=ps, in1=g[:, cols], op=ALU.mult),
                   reads=[pn, gn], writes=["t_ta0"])
            P.emit("pool", lambda: nc.gpsimd.tensor_tensor(out=ht[:, s, cols], in0=ht[:, s, cols], in1=tmpa[0], op=ALU.add),
                   reads=["t_ta0", "t_h"], writes=["t_h"])

        for ti in range(T // TTL):
            t0 = ti * TTL
            tsl = slice(t0, t0 + TTL)
            has_ctx = ti == 0
            if has_ctx:
                bcast_mod(K, l, 2 * D, 1, gc, "t_gc", rowbuf, ones1)
            P.emit("q_act", lambda: nc.scalar.dma_start(out=yT, in_=K.YT[:, tsl].rearrange("(c p) t -> p c t", p=128)),
                   reads=["YT"], writes=["t_yT", "t_hid", "t_fence"])
            P.emit("q_act", lambda: nc.scalar.dma_start(out=ht, in_=K.H[tsl, :].rearrange("(s p) d -> p s d", p=128)),
                   reads=["H"], writes=["t_h"])
            for cg in range(4):
                gb = gbuf[cnt["g"] % 2]
                gbn = f"t_g{cnt['g'] % 2}"
                cnt["g"] += 1
                for n in range(3):
                    r0 = GT + n * D + cg * 512
                    P.emit("q_act", lambda: nc.scalar.dma_start(out=gb[:, n * 4:(n + 1) * 4, :],
                                                                in_=K.PT[r0:r0 + 512, tsl].rearrange("(c p) t -> p c t", p=128)),
                           reads=["PT", "t_fence"], writes=[gbn + f"_{n}"])
                wt, wn = wload(wbr[:, cg * 512:(cg + 1) * 512].rearrange("(c p) d -> p c d", p=128), 24, f"wb{l}_br")
                for dcl in range(4):
                    dc = cg * 4 + dcl
                    banks = [nextbank() for _ in range(3)]
                    for n in range(3):
                        ps, pn = banks[n]
                        for w8 in range(8):
                            P.emit("pe", lambda: nc.tensor.matmul(ps[:, :TTL], lhsT=wt[:, n * 8 + w8, dcl * 128:(dcl + 1) * 128],
                                                                  rhs=yT[:, n * 8 + w8, :], start=(w8 == 0), stop=(w8 == 7)),
                                   reads=[wn, "t_yT"], writes=[pn])
                    for n in range(3):
                        ps, pn = banks[n]
                        P.emit("dve", lambda: nc.vector.tensor_tensor(out=tmpa[n][:, :TTL], in0=ps[:, :TTL],
                                                                      in1=gb[:, n * 4 + dcl, :], op=ALU.mult),
                               reads=[pn, gbn + f"_{n}"], writes=[f"t_ta{n}"])
                    P.emit("pool", lambda: nc.gpsimd.tensor_tensor(out=tmpa[0][:, :TTL], in0=tmpa[0][:, :TTL], in1=tmpa[1][:, :TTL],
                                                                   op=ALU.add), reads=["t_ta0", "t_ta1"], writes=["t_ta0"])
                    P.emit("pool", lambda: nc.gpsimd.tensor_tensor(out=mT[:, dc, :], in0=tmpa[0][:, :TTL], in1=tmpa[2][:, :TTL],
                                                                   op=ALU.add), reads=["t_ta0", "t_ta2"], writes=["t_mT"])
            for cg in range(4):
                wt, wn = wload(wout[:, cg * 512:(cg + 1) * 512].rearrange("(c p) d -> p c d", p=128), 16, f"wb{l}_out")
                for s_ in range(NSUB):
                    ps, pn = nextbank()
                    for k in range(KC):
                        P.emit("pe", lambda: nc.tensor.matmul(ps, lhsT=mT[:, k, s_ * 128:(s_ + 1) * 128], rhs=wt[:, k, :],
                                                              start=(k == 0), stop=(k == KC - 1)), reads=[wn, "t_mT"], writes=[pn])
                    resid(ps, pn, s_, cg, 0, has_ctx)
            for s_ in range(NSUB):
                r = 1 if (has_ctx and s_ < 2) else 0
                rms_to_fm(K, ht[:, s_, :], "t_h", ss, "t_ss", yb, "t_yb", junk, A, "t_A2", l, 48, r,
                          lambda dc, s_=s_: u2T[:, dc, s_ * 128:(s_ + 1) * 128], "t_u2T")
            if has_ctx:
                bcast_mod(K, l, 5 * D, 1, gc, "t_gc", rowbuf, ones1)
            for fg in range(16):
                wt, wn = wload(wf1[:, fg * 512:(fg + 1) * 512].rearrange("(c p) d -> p c d", p=128), 16, f"wb{l}_ff1")
                for fc in range(4):
                    ps, pn = nextbank()
                    for k in range(KC):
                        P.emit("pe", lambda: nc.tensor.matmul(ps[:, :TTL], lhsT=wt[:, k, fc * 128:(fc + 1) * 128], rhs=u2T[:, k, :],
                                                              start=(k == 0), stop=(k == KC - 1)), reads=[wn, "t_u2T"], writes=[pn])
                    tb = 1 + (fc % 2)
                    P.emit("act", lambda: nc.scalar.activation(out=tmpa[tb][:, :TTL], in_=ps[:, :TTL], func=AF.Relu),
                           reads=[pn], writes=[f"t_ta{tb}"])
                    P.emit("pool", lambda: nc.gpsimd.tensor_tensor(out=hidT[:, fg * 4 + fc, :], in0=tmpa[tb][:, :TTL],
                                                                   in1=tmpa[tb][:, :TTL], op=ALU.mult),
                           reads=[f"t_ta{tb}", "t_mT", "t_yT"], writes=["t_hid"])
            for cg in range(4):
                banks = [nextbank() for _ in range(NSUB)]
                for fq in range(4):
                    wt, wn = wload(wf2[fq * 2048:(fq + 1) * 2048, cg * 512:(cg + 1) * 512].rearrange("(c p) d -> p c d", p=128),
                                   16, f"wb{l}_ff2")
                    for s_ in range(NSUB):
                        ps, pn = banks[s_]
                        for j in range(16):
                            P.emit("pe", lambda: nc.tensor.matmul(ps, lhsT=hidT[:, fq * 16 + j, s_ * 128:(s_ + 1) * 128], rhs=wt[:, j, :],
                                                                  start=(fq == 0 and j == 0), stop=(fq == 3 and j == 15)),
                                   reads=[wn, "t_hid"], writes=[pn])
                for s_ in range(NSUB):
                    ps, pn = banks[s_]
                    resid(ps, pn, s_, cg, 1, has_ctx)
            P.emit("q_act", lambda: nc.scalar.dma_start(out=K.H[tsl, :].rearrange("(s p) d -> p s d", p=128), in_=ht),
                   reads=["t_h"], writes=["H"])
    P.barrier()


def stage_final(K):
    nc, P, I = K.nc, K.P, K.I
    with contextlib.ExitStack() as st:
        sb = lambda n, s_, d: K.sb(n, s_, d, st)
        fg = bcast_row_big(K, st, I["final_g"].rearrange("(o d) -> o d", o=1), "fg")
        hb = [sb(f"f_h{i}", (128, D), F32) for i in range(2)]
        ob = [sb(f"f_o{i}", (128, D), F32) for i in range(2)]
        junk = sb("f_junk", (128, D), BF16)
        ss = sb("f_ss", (128, 2), F32)
        for tt in range(SEQ // 128):
            b = tt % 2
            P.emit("q_sp", lambda: nc.sync.dma_start(out=hb[b], in_=K.H[CTX + tt * 128:CTX + (tt + 1) * 128, :]),
                   reads=["H"], writes=[f"f_h{b}"])
            sv = ss[:, b:b + 1]
            P.emit("act", lambda: nc.scalar.activation(out=junk, in_=hb[b], func=AF.Square, accum_out=sv),
                   reads=[f"f_h{b}"], writes=["f_junk", f"f_ss{b}"])
            P.emit("dve", lambda: nc.vector.tensor_scalar(out=sv, in0=sv, scalar1=1.0 / D, scalar2=EPS, op0=ALU.mult, op1=ALU.add),
                   reads=[f"f_ss{b}"], writes=[f"f_ss{b}"])
            P.emit("act", lambda: nc.scalar.activation(out=sv, in_=sv, func=AF.Sqrt), reads=[f"f_ss{b}"], writes=[f"f_ss{b}"])
            P.emit("dve", lambda: nc.vector.reciprocal(out=sv, in_=sv), reads=[f"f_ss{b}"], writes=[f"f_ss{b}"])
            P.emit("dve", lambda: nc.vector.scalar_tensor_tensor(out=ob[b], in0=hb[b], scalar=sv, in1=fg, op0=ALU.mult, op1=ALU.mult),
                   reads=[f"f_h{b}", f"f_ss{b}", "fg"], writes=[f"f_o{b}"])
            P.emit("q_sp", lambda: nc.sync.dma_start(out=K.out[tt * 128:(tt + 1) * 128, :], in_=ob[b]), reads=[f"f_o{b}"], writes=["out"])


def bcast_row_big(K, st, src_row, tag):
    nc, P = K.nc, K.P
    row = K.sb(f"bbr_{tag}", (1, D), F32, st)
    ones = K.sb(f"bbo_{tag}", (1, 128), F32, st)
    dst = K.sb(f"bbd_{tag}", (128, D), F32, st)
    P.emit("q_sp", lambda: nc.sync.dma_start(out=row, in_=src_row), writes=[f"bbr_{tag}"])
    P.emit("dve", lambda: nc.vector.memset(ones, 1.0), writes=[f"bbo_{tag}"])
    for j in range(4):
        P.emit("pe", lambda: nc.tensor.matmul(K.bank[7], lhsT=ones, rhs=row[:, j * 512:(j + 1) * 512], start=True, stop=True),
               reads=[f"bbr_{tag}", f"bbo_{tag}"], writes=["PS:7"])
        P.emit("act", lambda: nc.scalar.copy(out=dst[:, j * 512:(j + 1) * 512], in_=K.bank[7]), reads=["PS:7"], writes=[tag])
    return dst


_CACHE = {}


def kernel(**inputs):
    ncores = 4
    if "nc" not in _CACHE:
        _CACHE["nc"] = build(L=DEPTH)[0]
    nc = _CACHE["nc"]
    consts = _consts()
    shared = {}
    for n in PARAM_SHAPES:
        shared[n] = np.ascontiguousarray(np.asarray(inputs[n], dtype=np.float32))
    for n, v in consts.items():
        shared["k_" + n] = v
    x = np.asarray(inputs["x"], dtype=np.float32)
    ctx = np.asarray(inputs["ctx"], dtype=np.float32)
    c = np.asarray(inputs["c"], dtype=np.float32)
    c_ctx = np.asarray(inputs["c_ctx"], dtype=np.float32)
    in_maps = []
    for b in range(ncores):
        m = dict(shared)
        m["x"] = np.ascontiguousarray(x[b])
        m["ctx"] = np.ascontiguousarray(ctx[b])
        m["cc"] = np.ascontiguousarray(np.stack([c[b], c_ctx], 0))
        in_maps.append(m)
    res = run_bass_kernel_spmd(nc, in_maps, core_ids=list(range(ncores)))
    return np.stack([np.asarray(res.results[b]["out"], dtype=np.float32) for b in range(ncores)], 0)
```

```python
import contextlib
import numpy as np
import concourse.bass as bass
import concourse.mybir as mybir
from concourse.bass_utils import run_bass_kernel_spmd

F32 = mybir.dt.float32
BF16 = mybir.dt.bfloat16
AF = mybir.ActivationFunctionType
ALU = mybir.AluOpType
AX = mybir.AxisListType

D = 2048
KC = 16
SEQ = 2048
CTX = 256
T = SEQ + CTX
DEPTH = 4
N_IN = 15232
DFF = 8192
EPS = 1e-6
TT512 = [(0, 512), (512, 512), (1024, 512), (1536, 512), (2048, 256)]
NT128 = T // 128


NSEM_DMA = 24


class _E:
    def __init__(self, P, name, eng, step, host=None):
        self.name = name
        self.eng = eng
        self.step = step
        self.is_dma = step == 16
        if self.is_dma:
            self.sems = [P.nc.alloc_semaphore(name=f"s_{name}_{i}") for i in range(NSEM_DMA)]
        else:
            self.sem = P.nc.alloc_semaphore(name="s_" + name)
        self.count = 0
        self.host = host if host is not None else self
        self.seen = {}
        self.seen_dma = set()

    def sem_for(self, idx):
        if self.is_dma:
            k = idx - 1
            return self.sems[k % NSEM_DMA], 16 * (k // NSEM_DMA + 1)
        return self.sem, idx


class Prog:
    def __init__(self, nc):
        self.nc = nc
        self.E = {}
        for name, eng in (("pe", nc.tensor), ("act", nc.scalar), ("dve", nc.vector),
                          ("pool", nc.gpsimd), ("sp", nc.sync)):
            self.E[name] = _E(self, name, eng, 1)
        for name, host in (("q_sp", "sp"), ("q_pool", "pool"), ("q_act", "act")):
            self.E[name] = _E(self, name, self.E[host].eng, 16, host=self.E[host])
        self.lastw = {}
        self.readers = {}
        self.multi = {}
        self.bg = set()
        self.ninst = 0

    def _wait(self, H, e, idx):
        Dp = self.E[e]
        if Dp.is_dma:
            if (e, idx) in H.seen_dma:
                return
            sem, val = Dp.sem_for(idx)
            H.eng.wait_ge(sem, val)
            H.seen_dma.add((e, idx))
        else:
            if H.seen.get(e, 0) >= idx:
                return
            H.eng.wait_ge(Dp.sem, idx)
            H.seen[e] = idx

    def emit(self, en, fn, reads=(), writes=()):
        E = self.E[en]
        H = E.host
        ps_r = [r for r in reads if r.startswith("PS:")]
        if ps_r:
            reads = [r for r in reads if not r.startswith("PS:")]
            writes = list(writes) + [r for r in ps_r if r not in writes]
        deps = set()
        for r in list(reads) + list(writes):
            for ent in self.multi.get(r, ()):
                deps.add(ent)
        for r in reads:
            ent = self.lastw.get(r)
            if ent is not None:
                deps.add(ent)
        for w in writes:
            ent = self.lastw.get(w)
            if ent is not None:
                deps.add(ent)
            for ent in self.readers.get(w, ()):
                deps.add(ent)
        mx = {}
        for e, idx in deps:
            Dp = self.E[e]
            if Dp.is_dma:
                self._wait(H, e, idx)
            else:
                if Dp is E and en == "pe":
                    continue
                if mx.get(e, 0) < idx:
                    mx[e] = idx
        for e, idx in mx.items():
            self._wait(H, e, idx)
        if E.is_dma and E.count >= NSEM_DMA:
            self._wait(H, en, E.count + 1 - NSEM_DMA)
        inst = fn()
        self.ninst += 1
        E.count += 1
        sem, _ = E.sem_for(E.count)
        inst.then_inc(sem, E.step)
        ent = (en, E.count)
        for w in writes:
            self.lastw[w] = ent
            self.readers[w] = []
        for r in reads:
            lst = self.readers.setdefault(r, [])
            if not E.is_dma:
                lst[:] = [x for x in lst if x[0] != en]
            lst.append(ent)
        return inst

    def _wait_all(self, H, final=False):
        for n, Dp in self.E.items():
            if Dp.count == 0:
                continue
            if Dp.is_dma:
                for idx in range(max(1, Dp.count - NSEM_DMA + 1), Dp.count + 1):
                    if (n, idx) not in self.bg or final:
                        self._wait(H, n, idx)
            elif Dp is not H:
                self._wait(H, n, Dp.count)

    def barrier(self):
        for hn in ("pe", "act", "dve", "pool", "sp"):
            self._wait_all(self.E[hn])
        self.lastw = {}
        self.readers = {}

    def finish(self):
        self._wait_all(self.E["sp"], final=True)


class Ctx:
    pass


def _consts():
    c = {}
    c["ident"] = np.eye(128, dtype=np.float32)
    i = np.arange(128)
    dfw = (i[None, :] - i[:, None]).astype(np.float32)
    c["ret_dfw"] = np.maximum(dfw, 0.0)
    c["ret_mfw"] = (dfw >= 0).astype(np.float32)
    c["ret_dbw"] = np.maximum(-dfw, 0.0)
    c["ret_mbw"] = (dfw <= 0).astype(np.float32)
    idx = i.astype(np.float32)
    c["ret_qe"] = np.stack([np.tile(idx + 1.0, (128, 1)), np.tile(128.0 - idx, (128, 1))], 1).astype(np.float32)
    c["ret_ke"] = np.stack([127.0 - idx, idx], 1).astype(np.float32)
    mprev = (i[:, None] >= i[None, :]).astype(np.float32)
    mnext = (i[:, None] <= i[None, :]).astype(np.float32)
    c["swa_mprev"] = np.tile(mprev, (1, 4))
    c["swa_mnext"] = np.tile(mnext, (1, 4))
    n_freq = 32
    inv = (10000.0 ** (-np.arange(n_freq, dtype=np.float32) / n_freq)).astype(np.float32)
    t = np.arange(SEQ)
    row = (t // 64).astype(np.float32)
    col = (t % 64).astype(np.float32)
    ang_r = (row[:, None] * inv).astype(np.float32)
    ang_c = (col[:, None] * inv).astype(np.float32)
    cos = np.concatenate([np.cos(ang_r), np.cos(ang_r), np.cos(ang_c), np.cos(ang_c)], 1).T
    sin = np.concatenate([-np.sin(ang_r), np.sin(ang_r), -np.sin(ang_c), np.sin(ang_c)], 1).T
    c["rope_cos"] = np.ascontiguousarray(cos, dtype=np.float32)
    c["rope_sin"] = np.ascontiguousarray(sin, dtype=np.float32)
    perm = np.zeros((128, 128), np.float32)
    for m in range(128):
        blk = m // 32
        src = (blk ^ 1) * 32 + (m % 32)
        perm[src, m] = 1.0
    c["rope_perm"] = perm
    bd = np.zeros((128, 128), np.float32)
    bd[:64, :64] = 1.0
    bd[64:, 64:] = 1.0
    c["bd64"] = bd
    j = np.arange(64)
    strict = (j[:, None] < j[None, :]).astype(np.float32)
    incl = (j[:, None] <= j[None, :]).astype(np.float32)
    c["rk_mask"] = np.concatenate([np.concatenate([strict, incl], 1), np.concatenate([strict, incl], 1)], 0)
    c["rk_strict_t"] = (j[:, None] > j[None, :]).astype(np.float32)
    rm = np.ones((128, T), np.float32)
    rm[:, ::64] = 0.0
    c["rk_reset"] = rm
    blk_f = np.concatenate([strict, incl], 1)
    blk_b = np.concatenate([strict.T, incl.T], 1)
    c["rk_gmask"] = np.stack([np.tile(blk_f, (1, 4)), np.tile(blk_b, (1, 4))], 0).astype(np.float32)
    c["rk_qmask"] = np.stack([np.tile(strict.T, (1, 2)), np.tile(strict, (1, 2))], 0).astype(np.float32)
    c["rk_i2"] = np.tile(np.eye(64, dtype=np.float32), (1, 2))
    return c


PARAM_SHAPES = {
    "norm1_g": (DEPTH, D), "norm2_g": (DEPTH, D), "w_mod": (DEPTH, D, 6 * D), "b_mod": (DEPTH, 6 * D),
    "w_in": (DEPTH, D, N_IN), "ret_decay": (DEPTH, 2, 8), "swa_sink": (DEPTH, 8),
    "rwkv_mu": (DEPTH, 2, 3456), "rwkv_w0": (DEPTH, 2, 1024), "rwkv_w_up": (DEPTH, 2, 64, 1024),
    "rwkv_a0": (DEPTH, 2, 1024), "rwkv_a_up": (DEPTH, 2, 64, 1024), "rwkv_g_up": (DEPTH, 128, 1024),
    "rwkv_k_k": (DEPTH, 1024), "rwkv_k_a": (DEPTH, 1024), "rwkv_r_k": (DEPTH, 16, 64),
    "rwkv_ln_g": (DEPTH, 1024), "w_branch": (DEPTH, 3, 1024, D), "w_out": (DEPTH, D, D),
    "w_ff1": (DEPTH, D, DFF), "w_ff2": (DEPTH, DFF, D), "final_g": (D,),
}


RQ, RK, RG, SQ, SK, RW, GT, PT_ROWS = 0, 1024, 2048, 3072, 4096, 4352, 7808, 13952
PV_RK, PV_RV, PV_SV, PV_COLS = 0, 1024, 2048, 2304
C_RET, C_SWA, C_RWKV, C_GATE = 0, 4096, 5632, 9088


def build(L=DEPTH, dump=(), stop_after=None):
    nc = bass.Bass("TRN2", target_bir_lowering=False)
    P = Prog(nc)
    K = Ctx()
    K.nc, K.P, K.L = nc, P, L
    K.dump = set(dump)

    def dram_in(name, shape, dt=F32):
        return nc.dram_tensor(name, list(shape), dt, kind="ExternalInput").ap()

    def dram(name, shape, dt):
        kind = "ExternalOutput" if name in K.dump else "Internal"
        return nc.dram_tensor(name, list(shape), dt, kind=kind).ap()

    K.dram = dram
    I = {}
    I["x"] = dram_in("x", (SEQ, D))
    I["ctx"] = dram_in("ctx", (CTX, D))
    I["cc"] = dram_in("cc", (2, D))
    for n, s in PARAM_SHAPES.items():
        s = tuple(s)
        if n != "final_g":
            s = (L,) + s[1:]
        I[n] = dram_in(n, s)
    for n, v in _consts().items():
        I["k_" + n] = dram_in("k_" + n, v.shape)
    K.I = I
    K.out = nc.dram_tensor("out", [SEQ, D], F32, kind="ExternalOutput").ap()

    K.H = dram("H", (T, D), F32)
    K.modrow = dram("modrow", (L, 2, 6 * D), F32)
    K.PT = dram("PT", (PT_ROWS, T), BF16)
    K.PV = dram("PV", (T, PV_COLS), BF16)
    K.YT = dram("YT", (3072, T), BF16)
    K.wb = {}
    for l in range(L):
        K.wb[l] = {
            "in": dram(f"wb{l}_in", (D, N_IN), BF16),
            "br": dram(f"wb{l}_br", (3072, D), BF16),
            "out": dram(f"wb{l}_out", (D, D), BF16),
            "ff1": dram(f"wb{l}_ff1", (D, DFF), BF16),
            "ff2": dram(f"wb{l}_ff2", (DFF, D), BF16),
        }

    with contextlib.ExitStack() as top:
        K.uniq = 0

        def sb(name, shape, dt, st=top):
            K.uniq += 1
            return st.enter_context(nc.sbuf_tensor(f"s{K.uniq}_{name}", list(shape), dt)).ap()

        def pst(name, shape, dt, st=top):
            K.uniq += 1
            return st.enter_context(nc.psum_tensor(f"p{K.uniq}_{name}", list(shape), dt)).ap()

        K.sb, K.pst = sb, pst
        K.ident_f = sb("ident_f", (128, 128), F32)
        K.ident_b = sb("ident_b", (128, 128), BF16)
        K.modT = sb("modT", (128, L * 96 * 2), F32)
        K.bank = [pst(f"bank{i}", (128, 512), F32) for i in range(8)]

        P.emit("q_sp", lambda: nc.sync.dma_start(out=K.ident_f, in_=I["k_ident"]), writes=["ident_f"])
        P.emit("dve", lambda: nc.vector.tensor_copy(out=K.ident_b, in_=K.ident_f), reads=["ident_f"], writes=["ident_b"])
        P.emit("q_sp", lambda: nc.sync.dma_start(out=K.H[0:CTX, :], in_=I["ctx"]), writes=["H"])
        P.emit("q_sp", lambda: nc.sync.dma_start(out=K.H[CTX:T, :], in_=I["x"]), writes=["H"])

        stages = []
        stages.append(("prep0", lambda: stage_prep(K, 0)))
        stages.append(("mod", lambda: stage_mod(K)))
        for l in range(L):
            if l + 1 < L:
                stages.append((f"prep{l+1}", lambda l=l: stage_prep(K, l + 1)))
            stages.append((f"proj_{l}", lambda l=l: stage_np(K, l)))
            stages.append((f"ret_{l}", lambda l=l: stage_ret(K, l)))
            stages.append((f"swa_{l}", lambda l=l: stage_swa(K, l)))
            stages.append((f"rwkv_{l}", lambda l=l: stage_rwkv(K, l)))
            stages.append((f"tail_{l}", lambda l=l: stage_tail(K, l)))
        stages.append(("final", lambda: stage_final(K)))
        for name, fn in stages:
            fn()
            if stop_after == name:
                break
        P.finish()
    K.ninst = P.ninst
    return nc, K


def stage_prep(K, l):
    nc, P, I = K.nc, K.P, K.I
    jobs = [
        (I["w_in"][l], K.wb[l]["in"], D, f"wb{l}_in"),
        (I["w_branch"][l].rearrange("n w d -> (n w) d"), K.wb[l]["br"], 3072, f"wb{l}_br"),
        (I["w_out"][l], K.wb[l]["out"], D, f"wb{l}_out"),
        (I["w_ff1"][l], K.wb[l]["ff1"], D, f"wb{l}_ff1"),
        (I["w_ff2"][l], K.wb[l]["ff2"], DFF, f"wb{l}_ff2"),
    ]
    for src, dst, R, rn in jobs:
        ents = []
        for i, r0 in enumerate(range(0, R, 256)):
            P.emit("q_pool", lambda: nc.gpsimd.dma_start(out=dst[r0:r0 + 256, :], in_=src[r0:r0 + 256, :]), writes=[f"{rn}_p{i}"])
            ents.append(P.lastw[f"{rn}_p{i}"])
        P.multi[rn] = ents
        P.bg.update(ents)


def modv(K, l, j, r):
    o = ((l * 96 + j) * 2 + r)
    return K.modT[:, o:o + 1]


def load_fm_vec(K, st, src2d, n, dst, tag):
    nc, P = K.nc, K.P
    tmp = K.sb(f"lfv_{tag}", (n, 128), F32, st)
    ps = K.bank[7][:, :n]
    P.emit("q_sp", lambda: nc.sync.dma_start(out=tmp, in_=src2d), writes=[f"lfv_{tag}"])
    P.emit("pe", lambda: nc.tensor.transpose(out=ps, in_=tmp, identity=K.ident_f[0:n, 0:n]),
           reads=[f"lfv_{tag}", "ident_f"], writes=["PS:7"])
    P.emit("dve", lambda: nc.vector.tensor_copy(out=dst, in_=ps), reads=["PS:7"], writes=[tag])


def stage_mod(K):
    nc, P, I, L = K.nc, K.P, K.I, K.L
    with contextlib.ExitStack() as st:
        cc = K.sb("cc", (2, D), F32, st)
        sc = K.sb("sc", (2, D), F32, st)
        scT = K.sb("scT", (128, KC * 2), F32, st)
        wt = [K.sb(f"wmod{i}", (128, 2048), F32, st) for i in range(3)]
        bm = K.sb("bmod", (2, 2048), F32, st)
        mrow = [K.sb(f"mrow{i}", (2, 2048), F32, st) for i in range(2)]
        psA = [K.bank[j] for j in range(4)]
        psT = K.bank[4][:, :32]
        P.emit("q_sp", lambda: nc.sync.dma_start(out=cc, in_=I["cc"]), writes=["cc"])
        P.emit("act", lambda: nc.scalar.activation(out=sc, in_=cc, func=AF.Silu), reads=["cc"], writes=["sc"])
        for k in range(KC):
            P.emit("pe", lambda: nc.tensor.transpose(out=psT[:, 2 * k:2 * k + 2], in_=sc[0:2, k * 128:(k + 1) * 128],
                                                      identity=K.ident_f[0:2, 0:2]),
                   reads=["sc", "ident_f"], writes=["PS:4"])
        P.emit("dve", lambda: nc.vector.tensor_copy(out=scT, in_=psT), reads=["PS:4"], writes=["scT"])
        cnt = 0
        for l in range(L):
            for pc in range(6):
                cs = slice(pc * 2048, (pc + 1) * 2048)
                for k in range(KC):
                    b = cnt % 3
                    q = "q_sp" if cnt % 2 == 0 else "q_act"
                    eng = nc.sync if cnt % 2 == 0 else nc.scalar
                    cnt += 1
                    P.emit(q, lambda: eng.dma_start(out=wt[b], in_=I["w_mod"][l, k * 128:(k + 1) * 128, cs]),
                           writes=[f"wmod{b}"])
                    for j in range(4):
                        P.emit("pe", lambda: nc.tensor.matmul(psA[j][0:2, :], lhsT=scT[:, 2 * k:2 * k + 2],
                                                              rhs=wt[b][:, j * 512:(j + 1) * 512],
                                                              start=(k == 0), stop=(k == KC - 1)),
                               reads=["scT", f"wmod{b}"], writes=[f"PS:{j}"])
                for r in range(2):
                    P.emit("q_sp", lambda: nc.sync.dma_start(out=bm[r:r + 1, :], in_=I["b_mod"][l:l + 1, cs]),
                           writes=["bmod"])
                mr = mrow[(l * 6 + pc) % 2]
                mrn = f"mrow{(l * 6 + pc) % 2}"
                for j in range(4):
                    P.emit("dve", lambda: nc.vector.tensor_tensor(out=mr[:, j * 512:(j + 1) * 512], in0=psA[j][0:2, :],
                                                                  in1=bm[:, j * 512:(j + 1) * 512], op=ALU.add),
                           reads=[f"PS:{j}", "bmod"], writes=[mrn])
                P.emit("q_sp", lambda: nc.sync.dma_start(out=K.modrow[l, :, cs], in_=mr), reads=[mrn], writes=["modrow"])
                for i in range(16):
                    P.emit("pe", lambda: nc.tensor.transpose(out=psT[:, 2 * i:2 * i + 2], in_=mr[0:2, i * 128:(i + 1) * 128],
                                                              identity=K.ident_f[0:2, 0:2]),
                           reads=[mrn, "ident_f"], writes=["PS:4"])
                o = (l * 96 + pc * 16) * 2
                P.emit("dve", lambda: nc.vector.tensor_copy(out=K.modT[:, o:o + 32], in_=psT), reads=["PS:4"], writes=["modT"])
    P.barrier()


def stage_norm1(K, l, which=1):
    nc, P, I, L = K.nc, K.P, K.I, K.L
    gname = "norm1_g" if which == 1 else "norm2_g"
    jsh, jsc = (0, 16) if which == 1 else (48, 64)
    modv4 = K.modT.rearrange("p (l j r) -> p l j r", l=L, j=96, r=2)
    with contextlib.ExitStack() as st:
        gT = K.sb("gT", (128, KC), F32, st)
        A = K.sb("A1", (128, KC, 2), F32, st)
        hb = [K.sb(f"hb{i}", (128, D), F32, st) for i in range(2)]
        yb = [K.sb(f"yb{i}", (128, D), BF16, st) for i in range(2)]
        junk = K.sb("junk", (128, D), BF16, st)
        ss = K.sb("ss", (128, 2), F32, st)
        pT = [K.bank[i].bitcast(BF16)[:, :512] for i in range(4)]
        load_fm_vec(K, st, I[gname][l].rearrange("(kc p) -> kc p", p=128), KC, gT, "gT")
        for r in range(2):
            P.emit("dve", lambda: nc.vector.tensor_scalar(out=A[:, :, r], in0=modv4[:, l, jsc:jsc + 16, r], scalar1=1.0,
                                                          scalar2=None, op0=ALU.add), reads=["modT"], writes=["A1"])
            P.emit("dve", lambda: nc.vector.tensor_tensor(out=A[:, :, r], in0=A[:, :, r], in1=gT, op=ALU.mult),
                   reads=["A1", "gT"], writes=["A1"])
        if "dbg_A" in K.dump and l == 0 and which == 1:
            dA = K.dram("dbg_A", (128, KC * 2), F32)
            dg = K.dram("dbg_g", (128, KC), F32)
            P.emit("q_sp", lambda: nc.sync.dma_start(out=dA, in_=A.rearrange("p a b -> p (a b)")), reads=["A1"], writes=["dbg_A"])
            P.emit("q_sp", lambda: nc.sync.dma_start(out=dg, in_=gT), reads=["gT"], writes=["dbg_g"])
        for tt in range(NT128):
            b = tt % 2
            r = 1 if tt < 2 else 0
            P.emit("q_sp", lambda: nc.sync.dma_start(out=hb[b], in_=K.H[tt * 128:(tt + 1) * 128, :]),
                   reads=["H"], writes=[f"hb{b}"])
            P.emit("act", lambda: nc.scalar.activation(out=junk, in_=hb[b], func=AF.Square, accum_out=ss[:, b:b + 1]),
                   reads=[f"hb{b}"], writes=["junk", f"ss{b}"])
            P.emit("dve", lambda: nc.vector.tensor_scalar(out=ss[:, b:b + 1], in0=ss[:, b:b + 1], scalar1=1.0 / D, scalar2=EPS,
                                                          op0=ALU.mult, op1=ALU.add), reads=[f"ss{b}"], writes=[f"ss{b}"])
            P.emit("act", lambda: nc.scalar.activation(out=ss[:, b:b + 1], in_=ss[:, b:b + 1], func=AF.Sqrt),
                   reads=[f"ss{b}"], writes=[f"ss{b}"])
            P.emit("dve", lambda: nc.vector.reciprocal(out=ss[:, b:b + 1], in_=ss[:, b:b + 1]), reads=[f"ss{b}"], writes=[f"ss{b}"])
            P.emit("act", lambda: nc.scalar.activation(out=yb[b], in_=hb[b], func=AF.Copy, scale=ss[:, b:b + 1]),
                   reads=[f"hb{b}", f"ss{b}"], writes=[f"yb{b}"])
            for g4 in range(4):
                pt = pT[g4]
                for j in range(4):
                    dc = g4 * 4 + j
                    P.emit("pe", lambda: nc.tensor.transpose(out=pt[:, j * 128:(j + 1) * 128], in_=yb[b][:, dc * 128:(dc + 1) * 128],
                                                              identity=K.ident_b),
                           reads=[f"yb{b}", "ident_b"], writes=[f"PS:{g4}"])
                for j in range(4):
                    dc = g4 * 4 + j
                    P.emit("dve", lambda: nc.vector.tensor_scalar(out=K.uT[:, dc, tt * 128:(tt + 1) * 128],
                                                                  in0=pt[:, j * 128:(j + 1) * 128],
                                                                  scalar1=A[:, dc, r:r + 1], scalar2=modv(K, l, jsh + dc, r),
                                                                  op0=ALU.mult, op1=ALU.add),
                           reads=[f"PS:{g4}", "A1", "modT"], writes=["uT"])
    P.barrier()


def proj_groups():
    fm, tm = [], []
    s = 128.0 ** -0.5

    def add_fm(c0, n, dst, kind, scale=1.0):
        o = 0
        while o < n:
            w = min(512, n - o)
            fm.append((c0 + o, w, dst + o, kind, scale))
            o += w

    def add_tm(c0, n, dst, scale=1.0):
        o = 0
        while o < n:
            w = min(512, n - o)
            tm.append((c0 + o, w, dst + o, scale))
            o += w

    add_fm(C_RET + 0, 1024, RQ, "copy")
    add_fm(C_RET + 1024, 1024, RK, "copy", s)
    add_tm(C_RET + 1024, 1024, PV_RK, s)
    add_tm(C_RET + 2048, 1024, PV_RV)
    add_fm(C_RET + 3072, 1024, RG, "silu")
    add_fm(C_SWA + 0, 1024, SQ, "copy")
    add_fm(C_SWA + 1024, 256, SK, "copy")
    add_tm(C_SWA + 1280, 256, PV_SV)
    add_fm(C_RWKV, 3456, RW, "copy")
    add_fm(C_GATE, 6144, GT, "sigmoid")
    return fm, tm


def stage_proj(K, l):
    nc, P = K.nc, K.P
    wb = K.wb[l]["in"]
    wbn = f"wb{l}_in"
    fm, tm = proj_groups()
    with contextlib.ExitStack() as st:
        wbuf = [K.sb(f"wbuf{i}", (128, KC, 512), BF16, st) for i in range(2)]
        ot = [K.sb(f"ot{i}", (128, T), BF16, st) for i in range(2)]
        ot2 = [K.sb(f"ot2_{i}", (128, 512), BF16, st) for i in range(2)]
        ps = K.bank
        pc = 0
        oc = 0
        ec = 0
        for gi, (c0, n, dst, kind, scale) in enumerate(fm + [(g[0], g[1], g[2], "tm", g[3]) for g in tm]):
            b = gi % 2
            P.emit("q_sp", lambda: nc.sync.dma_start(out=wbuf[b][:, :, :n],
                                                     in_=wb[:, c0:c0 + n].rearrange("(kc p) n -> p kc n", p=128)),
                   reads=[wbn], writes=[f"wbuf{b}"])
            if kind != "tm":
                for ci in range(n // 128):
                    o = ot[oc % 2]
                    on = f"ot{oc % 2}"
                    oc += 1
                    for (t0, tn) in TT512:
                        p = ps[pc % 8]
                        pn = f"PS:{pc % 8}"
                        pc += 1
                        for k in range(KC):
                            P.emit("pe", lambda: nc.tensor.matmul(p[:, :tn], lhsT=wbuf[b][:, k, ci * 128:(ci + 1) * 128],
                                                                  rhs=K.uT[:, k, t0:t0 + tn], start=(k == 0), stop=(k == KC - 1)),
                                   reads=[f"wbuf{b}", "uT"], writes=[pn])
                        if kind == "copy":
                            if ec % 2 == 0:
                                P.emit("act", lambda: nc.scalar.activation(out=o[:, t0:t0 + tn], in_=p[:, :tn], func=AF.Copy, scale=scale),
                                       reads=[pn], writes=[on])
                            else:
                                P.emit("dve", lambda: nc.vector.tensor_scalar(out=o[:, t0:t0 + tn], in0=p[:, :tn], scalar1=scale,
                                                                              scalar2=None, op0=ALU.mult), reads=[pn], writes=[on])
                            ec += 1
                        else:
                            f = AF.Silu if kind == "silu" else AF.Sigmoid
                            P.emit("act", lambda: nc.scalar.activation(out=o[:, t0:t0 + tn], in_=p[:, :tn], func=f),
                                   reads=[pn], writes=[on])
                    r0 = dst + ci * 128
                    P.emit("q_sp", lambda: nc.sync.dma_start(out=K.PT[r0:r0 + 128, :], in_=o), reads=[on], writes=["PT"])
            else:
                for tt in range(NT128):
                    p = ps[pc % 8]
                    pn = f"PS:{pc % 8}"
                    pc += 1
                    o = ot2[oc % 2]
                    on = f"ot2_{oc % 2}"
                    oc += 1
                    for k in range(KC):
                        P.emit("pe", lambda: nc.tensor.matmul(p[:, :n], lhsT=K.uT[:, k, tt * 128:(tt + 1) * 128],
                                                              rhs=wbuf[b][:, k, :n], start=(k == 0), stop=(k == KC - 1)),
                               reads=[f"wbuf{b}", "uT"], writes=[pn])
                    if ec % 2 == 0:
                        P.emit("act", lambda: nc.scalar.activation(out=o[:, :n], in_=p[:, :n], func=AF.Copy, scale=scale),
                               reads=[pn], writes=[on])
                    else:
                        P.emit("dve", lambda: nc.vector.tensor_scalar(out=o[:, :n], in0=p[:, :n], scalar1=scale, scalar2=None,
                                                                      op0=ALU.mult), reads=[pn], writes=[on])
                    ec += 1
                    P.emit("q_sp", lambda: nc.sync.dma_start(out=K.PV[tt * 128:(tt + 1) * 128, dst:dst + n], in_=o[:, :n]),
                           reads=[on], writes=["PV"])
    P.barrier()


def stage_np(K, l):
    with contextlib.ExitStack() as st:
        K.uT = K.sb("uT", (128, KC, T), BF16, st)
        stage_norm1(K, l)
        stage_proj(K, l)


def bcast_row(K, st, src_row, n, tag):
    nc, P = K.nc, K.P
    row = K.sb(f"br_{tag}", (1, n), F32, st)
    ones = K.sb(f"bo_{tag}", (1, 128), F32, st)
    ps = K.bank[7][:, :n]
    dst = K.sb(f"bd_{tag}", (128, n), F32, st)
    P.emit("q_sp", lambda: nc.sync.dma_start(out=row, in_=src_row), writes=[f"br_{tag}"])
    P.emit("dve", lambda: nc.vector.memset(ones, 1.0), writes=[f"bo_{tag}"])
    P.emit("pe", lambda: nc.tensor.matmul(ps, lhsT=ones, rhs=row, start=True, stop=True),
           reads=[f"br_{tag}", f"bo_{tag}"], writes=["PS:7"])
    P.emit("dve", lambda: nc.vector.tensor_copy(out=dst, in_=ps), reads=["PS:7"], writes=[tag])
    return dst


def stage_ret(K, l):
    nc, P, I = K.nc, K.P, K.I
    with contextlib.ExitStack() as st:
        sb, pst = (lambda n, s, d: K.sb(n, s, d, st)), (lambda n, s, d: K.pst(n, s, d, st))
        cdf = [sb(f"cdf{d}", (128, 128), F32) for d in range(2)]
        cmk = [sb(f"cmk{d}", (128, 128), F32) for d in range(2)]
        cqe = sb("cqe", (128, 2, 128), F32)
        cke = sb("cke", (128, 2), F32)
        onesf = sb("onesf", (128, 128), F32)
        for d, (a, b) in enumerate((("k_ret_dfw", "k_ret_mfw"), ("k_ret_dbw", "k_ret_mbw"))):
            P.emit("q_sp", lambda: nc.sync.dma_start(out=cdf[d], in_=I[a]), writes=[f"cdf{d}"])
            P.emit("q_sp", lambda: nc.sync.dma_start(out=cmk[d], in_=I[b]), writes=[f"cmk{d}"])
        P.emit("q_sp", lambda: nc.sync.dma_start(out=cqe, in_=I["k_ret_qe"]), writes=["cqe"])
        P.emit("q_sp", lambda: nc.sync.dma_start(out=cke, in_=I["k_ret_ke"]), writes=["cke"])
        P.emit("dve", lambda: nc.vector.memset(onesf, 1.0 / 128.0), writes=["onesf"])
        dl = bcast_row(K, st, I["ret_decay"][l:l + 1].rearrange("o a h -> o (a h)"), 16, "dl")
        lg = sb("lg", (128, 16), F32)
        P.emit("act", lambda: nc.scalar.activation(out=lg, in_=dl, func=AF.Exp, scale=-1.0), reads=["dl"], writes=["lg"])
        P.emit("act", lambda: nc.scalar.activation(out=lg, in_=lg, func=AF.Ln, bias=1.0), reads=["lg"], writes=["lg"])
        P.emit("dve", lambda: nc.vector.tensor_scalar(out=lg, in0=lg, scalar1=-1.0, scalar2=None, op0=ALU.mult),
               reads=["lg"], writes=["lg"])
        DM = [sb(f"DM{d}", (128, 128), F32) for d in range(2)]
        qdec = [sb(f"qdec{d}", (128, 128), F32) for d in range(2)]
        kdec = sb("kdec", (128, 2), F32)
        gam = sb("gam", (128, 2), F32)
        inb = []
        for i in range(2):
            inb.append(dict(qT=sb(f"r_qT{i}", (128, T), BF16), kT=sb(f"r_kT{i}", (128, T), BF16),
                            sg=sb(f"r_sg{i}", (128, T), BF16), ktm=sb(f"r_ktm{i}", (128, NT128, 128), BF16),
                            vtm=sb(f"r_vtm{i}", (128, NT128, 128), BF16)))
        vt = sb("r_vt", (128, 2, NT128, 128), BF16)
        qt = [sb(f"r_qt{i}", (128, 128), BF16) for i in range(4)]
        sm = [sb(f"r_sm{i}", (128, 128), BF16) for i in range(4)]
        S = [sb(f"r_S{d}", (128, 128), F32) for d in range(2)]
        Sb = [sb(f"r_Sb{d}", (128, 128), BF16) for d in range(2)]
        yacc = [sb(f"r_y{d}", (128, T), F32) for d in range(2)]
        tmp = [sb(f"r_t{i}", (128, 512), F32) for i in range(4)]
        outb = [sb(f"r_o{i}", (128, 512), BF16) for i in range(2)]
        psS = [K.bank[i][:, :128] for i in range(2)]
        psY = [K.bank[2 + i][:, :128] for i in range(2)]
        psK = [K.bank[4 + i][:, :128] for i in range(2)]
        psM = K.bank[6]
        psV = K.bank[7]

        def load_head(h):
            B = inb[h % 2]
            i = h % 2
            for nm, row in (("qT", RQ), ("kT", RK), ("sg", RG)):
                P.emit("q_sp", lambda: nc.sync.dma_start(out=B[nm], in_=K.PT[row + h * 128:row + (h + 1) * 128, :]),
                       reads=["PT"], writes=[f"r_{nm}{i}"])
            for nm, col in (("ktm", PV_RK), ("vtm", PV_RV)):
                P.emit("q_act", lambda: nc.scalar.dma_start(
                    out=B[nm], in_=K.PV[:, col + h * 128:col + (h + 1) * 128].rearrange("(c p) d -> p c d", p=128)),
                    reads=["PV"], writes=[f"r_{nm}{i}"])

        load_head(0)
        cnt = 0
        oc = 0
        for h in range(8):
            if h + 1 < 8:
                load_head(h + 1)
            B = inb[h % 2]
            i = h % 2
            for d in range(2):
                col = d * 8 + h
                lgc = lg[:, col:col + 1]
                P.emit("act", lambda: nc.scalar.activation(out=DM[d], in_=cdf[d], func=AF.Exp, scale=lgc),
                       reads=["lg", f"cdf{d}"], writes=[f"DM{d}"])
                P.emit("dve", lambda: nc.vector.tensor_tensor(out=DM[d], in0=DM[d], in1=cmk[d], op=ALU.mult),
                       reads=[f"DM{d}", f"cmk{d}"], writes=[f"DM{d}"])
                P.emit("act", lambda: nc.scalar.activation(out=qdec[d], in_=cqe[:, d, :], func=AF.Exp, scale=lgc),
                       reads=["lg", "cqe"], writes=[f"qdec{d}"])
                P.emit("act", lambda: nc.scalar.activation(out=kdec[:, d:d + 1], in_=cke[:, d:d + 1], func=AF.Exp, scale=lgc),
                       reads=["lg", "cke"], writes=[f"kdec{d}"])
                P.emit("act", lambda: nc.scalar.activation(out=gam[:, d:d + 1], in_=lgc, func=AF.Exp, scale=128.0),
                       reads=["lg"], writes=[f"gam{d}"])
                P.emit("dve", lambda: nc.vector.tensor_scalar(out=vt[:, d], in0=B["vtm"], scalar1=kdec[:, d:d + 1], scalar2=None,
                                                              op0=ALU.mult), reads=[f"r_vtm{i}", f"kdec{d}"], writes=[f"r_vt{d}"])
                P.emit("dve", lambda: nc.vector.memset(S[d], 0.0), writes=[f"r_S{d}"])
                P.emit("dve", lambda: nc.vector.memset(Sb[d], 0.0), writes=[f"r_Sb{d}"])
            order = [list(range(NT128)), [1, 0] + list(range(NT128 - 1, 1, -1))]
            for step in range(NT128):
                for d in range(2):
                    c = order[d][step]
                    cs = slice(c * 128, (c + 1) * 128)
                    j = cnt % 2
                    j4 = cnt % 4
                    cnt += 1
                    P.emit("pe", lambda: nc.tensor.matmul(psS[j], lhsT=B["kT"][:, cs], rhs=B["qT"][:, cs], start=True, stop=True),
                           reads=[f"r_kT{i}", f"r_qT{i}"], writes=[f"PS:{j}"])
                    P.emit("dve", lambda: nc.vector.tensor_tensor(out=sm[j4], in0=psS[j], in1=DM[d], op=ALU.mult),
                           reads=[f"PS:{j}", f"DM{d}"], writes=[f"r_sm{j4}"])
                    P.emit("pool", lambda: nc.gpsimd.tensor_tensor(out=qt[j4], in0=B["qT"][:, cs], in1=qdec[d], op=ALU.mult),
                           reads=[f"r_qT{i}", f"qdec{d}"], writes=[f"r_qt{j4}"])
                    P.emit("pe", lambda: nc.tensor.matmul(psY[j], lhsT=B["vtm"][:, c, :], rhs=sm[j4], start=True, stop=False),
                           reads=[f"r_vtm{i}", f"r_sm{j4}"], writes=[f"PS:{2 + j}"])
                    P.emit("pe", lambda: nc.tensor.matmul(psY[j], lhsT=Sb[d], rhs=qt[j4], start=False, stop=True),
                           reads=[f"r_Sb{d}", f"r_qt{j4}"], writes=[f"PS:{2 + j}"])
                    P.emit("act", lambda: nc.scalar.copy(out=yacc[d][:, cs], in_=psY[j]), reads=[f"PS:{2 + j}"], writes=[f"r_y{d}"])
                    P.emit("pe", lambda: nc.tensor.matmul(psK[j], lhsT=B["ktm"][:, c, :], rhs=vt[:, d, c, :], start=True, stop=True),
                           reads=[f"r_ktm{i}", f"r_vt{d}"], writes=[f"PS:{4 + j}"])
                    P.emit("dve", lambda: nc.vector.scalar_tensor_tensor(out=S[d], in0=S[d], scalar=gam[:, d:d + 1], in1=psK[j],
                                                                         op0=ALU.mult, op1=ALU.add),
                           reads=[f"r_S{d}", f"gam{d}", f"PS:{4 + j}"], writes=[f"r_S{d}"])
                    P.emit("act", lambda: nc.scalar.copy(out=Sb[d], in_=S[d]), reads=[f"r_S{d}"], writes=[f"r_Sb{d}"])
            for (t0, tn) in TT512:
                ts_ = slice(t0, t0 + tn)
                y, ysq, msq, yc = tmp[0], tmp[1], tmp[2], tmp[3]
                P.emit("dve", lambda: nc.vector.tensor_tensor(out=y[:, :tn], in0=yacc[0][:, ts_], in1=yacc[1][:, ts_], op=ALU.add),
                       reads=["r_y0", "r_y1"], writes=["r_t0"])
                P.emit("act", lambda: nc.scalar.activation(out=ysq[:, :tn], in_=y[:, :tn], func=AF.Square), reads=["r_t0"], writes=["r_t1"])
                P.emit("pe", lambda: nc.tensor.matmul(psM[:, :tn], lhsT=onesf, rhs=y[:, :tn], start=True, stop=True),
                       reads=["onesf", "r_t0"], writes=["PS:6"])
                P.emit("pe", lambda: nc.tensor.matmul(psV[:, :tn], lhsT=onesf, rhs=ysq[:, :tn], start=True, stop=True),
                       reads=["onesf", "r_t1"], writes=["PS:7"])
                P.emit("act", lambda: nc.scalar.activation(out=msq[:, :tn], in_=psM[:, :tn], func=AF.Square), reads=["PS:6"], writes=["r_t2"])
                P.emit("dve", lambda: nc.vector.tensor_tensor(out=msq[:, :tn], in0=psV[:, :tn], in1=msq[:, :tn], op=ALU.subtract),
                       reads=["PS:7", "r_t2"], writes=["r_t2"])
                P.emit("dve", lambda: nc.vector.tensor_scalar(out=msq[:, :tn], in0=msq[:, :tn], scalar1=1e-5, scalar2=None, op0=ALU.add),
                       reads=["r_t2"], writes=["r_t2"])
                P.emit("act", lambda: nc.scalar.activation(out=msq[:, :tn], in_=msq[:, :tn], func=AF.Sqrt), reads=["r_t2"], writes=["r_t2"])
                P.emit("dve", lambda: nc.vector.reciprocal(out=msq[:, :tn], in_=msq[:, :tn]), reads=["r_t2"], writes=["r_t2"])
                P.emit("dve", lambda: nc.vector.tensor_tensor(out=yc[:, :tn], in0=y[:, :tn], in1=psM[:, :tn], op=ALU.subtract),
                       reads=["r_t0", "PS:6"], writes=["r_t3"])
                P.emit("dve", lambda: nc.vector.tensor_tensor(out=yc[:, :tn], in0=yc[:, :tn], in1=msq[:, :tn], op=ALU.mult),
                       reads=["r_t3", "r_t2"], writes=["r_t3"])
                ob = outb[oc % 2]
                obn = f"r_o{oc % 2}"
                oc += 1
                P.emit("dve", lambda: nc.vector.tensor_tensor(out=ob[:, :tn], in0=yc[:, :tn], in1=B["sg"][:, ts_], op=ALU.mult),
                       reads=["r_t3", f"r_sg{i}"], writes=[obn])
                P.emit("q_sp", lambda: nc.sync.dma_start(out=K.YT[h * 128:(h + 1) * 128, ts_], in_=ob[:, :tn]),
                       reads=[obn], writes=["YT"])
    P.barrier()


def stage_swa(K, l):
    nc, P, I = K.nc, K.P, K.I
    SC = 128.0 ** -0.5
    with contextlib.ExitStack() as st:
        sb = lambda n, s_, d: K.sb(n, s_, d, st)
        cosT = sb("w_cos", (128, SEQ), F32)
        sinT = sb("w_sin", (128, SEQ), F32)
        permf = sb("w_permf", (128, 128), F32)
        permb = sb("w_permb", (128, 128), BF16)
        mk32 = sb("w_mk32", (128, 2, 512), F32)
        mk = sb("w_mk", (128, 2, 512), BF16)
        onesb = sb("w_onesb", (128, 128), BF16)
        P.emit("q_sp", lambda: nc.sync.dma_start(out=cosT, in_=I["k_rope_cos"]), writes=["w_cos"])
        P.emit("q_sp", lambda: nc.sync.dma_start(out=sinT, in_=I["k_rope_sin"]), writes=["w_sin"])
        P.emit("q_sp", lambda: nc.sync.dma_start(out=permf, in_=I["k_rope_perm"]), writes=["w_permf"])
        P.emit("dve", lambda: nc.vector.tensor_copy(out=permb, in_=permf), reads=["w_permf"], writes=["w_permb"])
        P.emit("q_sp", lambda: nc.sync.dma_start(out=mk32[:, 0, :], in_=I["k_swa_mprev"]), writes=["w_mk32"])
        P.emit("q_sp", lambda: nc.sync.dma_start(out=mk32[:, 1, :], in_=I["k_swa_mnext"]), writes=["w_mk32"])
        P.emit("dve", lambda: nc.vector.tensor_copy(out=mk, in_=mk32), reads=["w_mk32"], writes=["w_mk"])
        P.emit("dve", lambda: nc.vector.memset(onesb, 1.0), writes=["w_onesb"])
        sk = bcast_row(K, st, I["swa_sink"][l:l + 1], 8, "sk")
        es = sb("w_es", (128, 8), F32)
        P.emit("act", lambda: nc.scalar.activation(out=es, in_=sk, func=AF.Exp), reads=["sk"], writes=["w_es"])
        raw = [sb(f"w_raw{i}", (128, T), BF16) for i in range(2)]
        kR = sb("w_kR", (128, T), BF16)
        vtm = sb("w_vtm", (128, NT128, 128), BF16)
        Qb = sb("w_Qb", (128, NT128, 4, 128), BF16)
        ysw = sb("w_ysw", (128, 4, T), BF16)
        Et = [sb(f"w_E{i}", (128, 512), BF16) for i in range(4)]
        t1 = [sb(f"w_t1_{i}", (128, 512), F32) for i in range(2)]
        t2 = [sb(f"w_t2_{i}", (128, 512), F32) for i in range(2)]
        den = sb("w_den", (128, 512), F32)
        rc = 0

        def rope_into(src, srcn, dst_fn, dstn):
            nonlocal rc
            for c in range(2):
                P.emit("pool", lambda: nc.gpsimd.tensor_copy(out=dst_fn(c), in_=src[:, c * 128:(c + 1) * 128]),
                       reads=[srcn], writes=[dstn])
            for q4 in range(4):
                t0 = CTX + q4 * 512
                b = rc % 2
                rc += 1
                pp = K.bank[6 + b]
                P.emit("pe", lambda: nc.tensor.matmul(pp, lhsT=permb, rhs=src[:, t0:t0 + 512], start=True, stop=True),
                       reads=["w_permb", srcn], writes=[f"PS:{6 + b}"])
                P.emit("dve", lambda: nc.vector.tensor_tensor(out=t1[b], in0=pp, in1=sinT[:, q4 * 512:(q4 + 1) * 512], op=ALU.mult),
                       reads=[f"PS:{6 + b}", "w_sin"], writes=[f"w_t1_{b}"])
                P.emit("pool", lambda: nc.gpsimd.tensor_tensor(out=t2[b], in0=src[:, t0:t0 + 512], in1=cosT[:, q4 * 512:(q4 + 1) * 512],
                                                               op=ALU.mult), reads=[srcn, "w_cos"], writes=[f"w_t2_{b}"])
                for c4 in range(4):
                    c = 2 + q4 * 4 + c4
                    P.emit("dve", lambda: nc.vector.tensor_tensor(out=dst_fn(c), in0=t1[b][:, c4 * 128:(c4 + 1) * 128],
                                                                  in1=t2[b][:, c4 * 128:(c4 + 1) * 128], op=ALU.add),
                           reads=[f"w_t1_{b}", f"w_t2_{b}"], writes=[dstn])

        lc = 0
        ec = 0
        bc = 0
        for g in range(2):
            r = raw[lc % 2]
            rn = f"w_raw{lc % 2}"
            lc += 1
            P.emit("q_sp", lambda: nc.sync.dma_start(out=r, in_=K.PT[SK + g * 128:SK + (g + 1) * 128, :]), reads=["PT"], writes=[rn])
            P.emit("q_act", lambda: nc.scalar.dma_start(
                out=vtm, in_=K.PV[:, PV_SV + g * 128:PV_SV + (g + 1) * 128].rearrange("(c p) d -> p c d", p=128)),
                reads=["PV"], writes=["w_vtm"])
            rope_into(r, rn, lambda c: kR[:, c * 128:(c + 1) * 128], "w_kR")
            for j in range(4):
                h = 4 * g + j
                r = raw[lc % 2]
                rn = f"w_raw{lc % 2}"
                lc += 1
                P.emit("q_sp", lambda: nc.sync.dma_start(out=r, in_=K.PT[SQ + h * 128:SQ + (h + 1) * 128, :]), reads=["PT"], writes=[rn])
                rope_into(r, rn, lambda c, j=j: Qb[:, c, j, :], "w_Qb")
            for qb in range(NT128):
                if qb < 2:
                    tiles = [(0, None), (1, None)]
                else:
                    n = qb - 2
                    tiles = []
                    if n > 0:
                        tiles.append((qb - 1, 0))
                    tiles.append((qb, None))
                    if n < 15:
                        tiles.append((qb + 1, 1))
                    tiles += [(0, None), (1, None)]
                ob = bc % 2
                bc += 1
                psO, psOn = K.bank[2 + ob], f"PS:{2 + ob}"
                psD, psDn = K.bank[4 + ob], f"PS:{4 + ob}"
                qrhs = Qb[:, qb].rearrange("p j t -> p (j t)")
                for ti, (kt, mi) in enumerate(tiles):
                    sbk = ec % 2
                    e4 = ec % 4
                    ec += 1
                    psS, psSn = K.bank[sbk], f"PS:{sbk}"
                    E, En = Et[e4], f"w_E{e4}"
                    P.emit("pe", lambda: nc.tensor.matmul(psS, lhsT=kR[:, kt * 128:(kt + 1) * 128], rhs=qrhs, start=True, stop=True),
                           reads=["w_kR", "w_Qb"], writes=[psSn])
                    P.emit("act", lambda: nc.scalar.activation(out=E, in_=psS, func=AF.Exp, scale=SC), reads=[psSn], writes=[En])
                    if mi is not None:
                        P.emit("dve", lambda: nc.vector.tensor_tensor(out=E, in0=E, in1=mk[:, mi, :], op=ALU.mult),
                               reads=[En, "w_mk"], writes=[En])
                    first, last = ti == 0, ti == len(tiles) - 1
                    P.emit("pe", lambda: nc.tensor.matmul(psO, lhsT=vtm[:, kt, :], rhs=E, start=first, stop=last),
                           reads=["w_vtm", En], writes=[psOn])
                    P.emit("pe", lambda: nc.tensor.matmul(psD, lhsT=onesb, rhs=E, start=first, stop=last),
                           reads=["w_onesb", En], writes=[psDn])
                for j in range(4):
                    col = 4 * g + j
                    P.emit("dve", lambda: nc.vector.tensor_scalar(out=den[:, j * 128:(j + 1) * 128], in0=psD[:, j * 128:(j + 1) * 128],
                                                                  scalar1=es[:, col:col + 1], scalar2=None, op0=ALU.add),
                           reads=[psDn, "w_es"], writes=["w_den"])
                P.emit("dve", lambda: nc.vector.reciprocal(out=den, in_=den), reads=["w_den"], writes=["w_den"])
                P.emit("dve", lambda: nc.vector.tensor_tensor(out=ysw[:, :, qb * 128:(qb + 1) * 128],
                                                              in0=psO.rearrange("p (j t) -> p j t", j=4),
                                                              in1=den.rearrange("p (j t) -> p j t", j=4), op=ALU.mult),
                       reads=[psOn, "w_den"], writes=["w_ysw"])
            for j in range(4):
                h = 4 * g + j
                P.emit("q_sp", lambda: nc.sync.dma_start(out=K.YT[1024 + h * 128:1024 + (h + 1) * 128, :], in_=ysw[:, j, :]),
                       reads=["w_ysw"], writes=["YT"])
    P.barrier()


def stage_rwkv(K, l):
    nc, P, I = K.nc, K.P, K.I
    NCH = T // 64
    with contextlib.ExitStack() as st:
        sb = lambda n, s_, d: K.sb(n, s_, d, st)
        E = lambda en, fn, r=(), w=(): P.emit(en, fn, reads=r, writes=w)
        reset = sb("k_reset", (128, T), BF16)
        gmask = sb("k_gmask", (64, 2, 512), F32)
        qmask = sb("k_qmask", (64, 2, 128), F32)
        i2 = sb("k_i2", (64, 128), F32)
        bd64 = sb("k_bd64", (128, 128), F32)
        E("q_sp", lambda: nc.sync.dma_start(out=gmask, in_=I["k_rk_gmask"].rearrange("d p x -> p d x")), w=["k_gmask"])
        E("q_sp", lambda: nc.sync.dma_start(out=qmask, in_=I["k_rk_qmask"].rearrange("d p x -> p d x")), w=["k_qmask"])
        E("q_sp", lambda: nc.sync.dma_start(out=i2, in_=I["k_rk_i2"]), w=["k_i2"])
        E("q_sp", lambda: nc.sync.dma_start(out=bd64, in_=I["k_bd64"]), w=["k_bd64"])
        muT = [sb(f"k_mu{i}", (128, 27), F32) for i in range(2)]
        for i in range(2):
            load_fm_vec(K, st, I["rwkv_mu"][l, i].rearrange("(c p) -> c p", p=128), 27, muT[i], f"k_mu{i}")
        w0T = sb("k_w0T", (128, 16), F32)
        a0T = sb("k_a0T", (128, 16), F32)
        load_fm_vec(K, st, I["rwkv_w0"][l].rearrange("d (c p) -> (d c) p", p=128), 16, w0T, "k_w0T")
        load_fm_vec(K, st, I["rwkv_a0"][l].rearrange("d (c p) -> (d c) p", p=128), 16, a0T, "k_a0T")
        kkT = sb("k_kkT", (128, 8), F32)
        kaT = sb("k_kaT", (128, 8), F32)
        ka1 = sb("k_ka1", (128, 8), F32)
        rkT = sb("k_rkT", (128, 8), F32)
        lnT = sb("k_lnT", (128, 8), F32)
        load_fm_vec(K, st, I["rwkv_k_k"][l].rearrange("(c p) -> c p", p=128), 8, kkT, "k_kkT")
        load_fm_vec(K, st, I["rwkv_k_a"][l].rearrange("(c p) -> c p", p=128), 8, kaT, "k_kaT")
        load_fm_vec(K, st, I["rwkv_r_k"][l].rearrange("(c e) n -> c (e n)", e=2), 8, rkT, "k_rkT")
        load_fm_vec(K, st, I["rwkv_ln_g"][l].rearrange("(c p) -> c p", p=128), 8, lnT, "k_lnT")
        E("dve", lambda: nc.vector.tensor_scalar(out=ka1, in0=kaT, scalar1=-1.0, scalar2=1.0, op0=ALU.mult, op1=ALU.add),
          ["k_kaT"], ["k_ka1"])
        up32 = sb("k_up32", (128, 1024), F32)
        wupb = sb("k_wupb", (128, 1024), BF16)
        aupb = sb("k_aupb", (128, 1024), BF16)
        gupb = sb("k_gupb", (128, 1024), BF16)
        for src, dst, dn in ((I["rwkv_w_up"][l].rearrange("d r w -> (d r) w"), wupb, "k_wupb"),
                             (I["rwkv_a_up"][l].rearrange("d r w -> (d r) w"), aupb, "k_aupb"),
                             (I["rwkv_g_up"][l], gupb, "k_gupb")):
            E("q_sp", lambda: nc.sync.dma_start(out=up32, in_=src), w=["k_up32"])
            E("dve", lambda: nc.vector.tensor_copy(out=dst, in_=up32), ["k_up32"], [dn])
        raw = sb("k_raw", (128, T), BF16)
        rs = sb("k_rs", (128, T), F32)
        ks = sb("k_ks", (128, T), F32)
        vb = sb("k_vb", (128, T), BF16)
        T0 = sb("k_T0", (128, T), F32)
        T1 = sb("k_T1", (128, T), F32)
        T2 = sb("k_T2", (128, T), F32)
        kk = sb("k_kk", (128, T), F32)
        Lam = sb("k_Lam", (128, T), F32)
        ad = T1
        asum = sb("k_asum", (128, T), F32)
        yacc = sb("k_yacc", (128, T), F32)
        ar = sb("k_ar", (128, NCH, 2, 64), BF16)
        bk = sb("k_bk", (128, NCH, 2, 64), BF16)
        BKf = sb("k_BKf", (128, NCH, 2, 64), BF16)
        LC = sb("k_LC", (128, NCH), F32)
        GC = sb("k_GC", (128, NCH), F32)
        twd = sb("k_twd", (128, T), BF16)
        adT = sb("k_adT", (128, T), BF16)
        sgd = sb("k_sgd", (128, T), BF16)
        c3 = lambda a: a.rearrange("p (c t) -> p c t", t=64)
        E("q_sp", lambda: nc.sync.dma_start(out=T0, in_=I["k_rk_reset"]), w=["k_T0"])
        E("dve", lambda: nc.vector.tensor_copy(out=reset, in_=T0), ["k_T0"], ["k_reset"])

        def shift(row0, mi, dst, dstn):
            E("q_sp", lambda: nc.sync.dma_start(out=raw, in_=K.PT[row0:row0 + 128, :]), ["PT"], ["k_raw"])
            E("dve", lambda: nc.vector.tensor_tensor(out=T0[:, 1:T], in0=raw[:, 0:T - 1], in1=raw[:, 1:T], op=ALU.subtract),
              ["k_raw"], ["k_T0"])
            E("pool", lambda: nc.gpsimd.tensor_tensor(out=T1[:, 0:T - 1], in0=raw[:, 1:T], in1=raw[:, 0:T - 1], op=ALU.subtract),
              ["k_raw"], ["k_T1"])
            for b0 in (0, CTX):
                E("dve", lambda: nc.vector.tensor_scalar(out=T0[:, b0:b0 + 1], in0=raw[:, b0:b0 + 1], scalar1=-1.0, scalar2=None,
                                                         op0=ALU.mult), ["k_raw", "k_T0"], ["k_T0"])
            for b1 in (CTX - 1, T - 1):
                E("pool", lambda: nc.gpsimd.tensor_scalar(out=T1[:, b1:b1 + 1], in0=raw[:, b1:b1 + 1], scalar1=-1.0, scalar2=None,
                                                          op0=ALU.mult), ["k_raw", "k_T1"], ["k_T1"])
            E("dve", lambda: nc.vector.scalar_tensor_tensor(out=T0, in0=T0, scalar=muT[0][:, mi:mi + 1], in1=raw, op0=ALU.mult, op1=ALU.add),
              ["k_T0", "k_mu0", "k_raw"], ["k_T0"])
            E("dve", lambda: nc.vector.scalar_tensor_tensor(out=dst, in0=T1, scalar=muT[1][:, mi:mi + 1], in1=T0, op0=ALU.mult, op1=ALU.add),
              ["k_T1", "k_mu1", "k_T0"], [dstn])

        shift(RW + 3072, 24, T2, "k_T2")
        E("act", lambda: nc.scalar.activation(out=twd, in_=T2, func=AF.Tanh), ["k_T2"], ["k_twd"])
        shift(RW + 3200, 25, T2, "k_T2")
        E("act", lambda: nc.scalar.copy(out=adT, in_=T2), ["k_T2"], ["k_adT"])
        shift(RW + 3328, 26, T2, "k_T2")
        E("act", lambda: nc.scalar.activation(out=sgd, in_=T2, func=AF.Sigmoid), ["k_T2"], ["k_sgd"])

        TMb = [sb(f"a_TM{i}", (64, 4, 384), BF16) for i in range(2)]
        Vpb = [sb(f"a_Vp{i}", (64, 4, 2, 128), BF16) for i in range(2)]
        Up = [sb(f"k_Up{i}", (64, 2, 128), BF16) for i in range(2)]
        Ub = [sb(f"k_Ub{i}", (64, 128), BF16) for i in range(2)]
        W32 = sb("k_W32", (64, 128), F32)
        H32 = sb("k_H32", (128, 128), F32)
        Hb = sb("k_Hb", (128, 128), BF16)
        i8 = sb("k_i8", (64, 512), F32)
        for i in range(4):
            E("dve", lambda: nc.vector.tensor_copy(out=i8[:, i * 128:(i + 1) * 128], in_=i2), ["k_i2"], ["k_i8"])
        for i in range(2):
            E("dve", lambda: nc.vector.memset(Vpb[i], 0.0), w=[f"a_Vp{i}"])
            E("dve", lambda: nc.vector.memset(Up[i], 0.0), w=[f"k_Up{i}"])
        outb = [sb(f"k_ob{i}", (128, 512), BF16) for i in range(2)]
        cc = 0
        oc = 0

        for hp in range(8):
            cols = slice(hp * 128, (hp + 1) * 128)
            shift(RW + hp * 128, hp, rs, "k_rs")
            shift(RW + 1024 + hp * 128, 8 + hp, ks, "k_ks")
            shift(RW + 2048 + hp * 128, 16 + hp, T2, "k_T2")
            E("act", lambda: nc.scalar.copy(out=vb, in_=T2), ["k_T2"], ["k_vb"])
            E("dve", lambda: nc.vector.tensor_scalar(out=T0, in0=ks, scalar1=kkT[:, hp:hp + 1], scalar2=None, op0=ALU.mult),
              ["k_ks", "k_kkT"], ["k_T0"])
            E("act", lambda: nc.scalar.activation(out=T1, in_=T0, func=AF.Square), ["k_T0"], ["k_T1"])
            for (t0, tn) in TT512:
                tsl = slice(t0, t0 + tn)
                E("pe", lambda: nc.tensor.matmul(K.bank[0][:, :tn], lhsT=bd64, rhs=T1[:, tsl], start=True, stop=True),
                  ["k_bd64", "k_T1"], ["PS:0"])
                E("act", lambda: nc.scalar.activation(out=T2[:, tsl], in_=K.bank[0][:, :tn], func=AF.Sqrt), ["PS:0"], ["k_T2"])
            E("dve", lambda: nc.vector.tensor_scalar(out=T2, in0=T2, scalar1=1e-12, scalar2=None, op0=ALU.max), ["k_T2"], ["k_T2"])
            E("dve", lambda: nc.vector.reciprocal(out=T2, in_=T2), ["k_T2"], ["k_T2"])
            E("dve", lambda: nc.vector.tensor_tensor(out=kk, in0=T0, in1=T2, op=ALU.mult), ["k_T0", "k_T2"], ["k_kk"])
            for d in range(2):
                dsl = slice(d * 64, (d + 1) * 64)
                for (t0, tn) in TT512:
                    tsl = slice(t0, t0 + tn)
                    E("pe", lambda: nc.tensor.matmul(K.bank[0][:, :tn], lhsT=wupb[dsl, cols], rhs=twd[dsl, tsl], start=True, stop=True),
                      ["k_wupb", "k_twd"], ["PS:0"])
                    E("act", lambda: nc.scalar.activation(out=T2[:, tsl], in_=K.bank[0][:, :tn], func=AF.Sigmoid,
                                                          bias=w0T[:, d * 8 + hp:d * 8 + hp + 1]), ["PS:0", "k_w0T"], ["k_T2"])
                    E("pe", lambda: nc.tensor.matmul(K.bank[1][:, :tn], lhsT=aupb[dsl, cols], rhs=adT[dsl, tsl], start=True, stop=True),
                      ["k_aupb", "k_adT"], ["PS:1"])
                    E("act", lambda: nc.scalar.activation(out=ad[:, tsl], in_=K.bank[1][:, :tn], func=AF.Sigmoid,
                                                          bias=a0T[:, d * 8 + hp:d * 8 + hp + 1]), ["PS:1", "k_a0T"], ["k_T1"])
                E("dve", lambda: nc.vector.tensor_scalar(out=T2, in0=T2, scalar1=-0.6065306597126334, scalar2=None, op0=ALU.mult),
                  ["k_T2"], ["k_T2"])
                if d == 0:
                    E("pool", lambda: nc.gpsimd.tensor_copy(out=asum, in_=ad), ["k_T1"], ["k_asum"])
                else:
                    E("pool", lambda: nc.gpsimd.tensor_tensor(out=asum, in0=asum, in1=ad, op=ALU.add), ["k_T1", "k_asum"], ["k_asum"])
                E("dve", lambda: nc.vector.tensor_tensor_scan(out=Lam, data0=reset, data1=T2, initial=0.0, op0=ALU.mult, op1=ALU.add),
                  ["k_reset", "k_T2"], ["k_Lam"])
                E("dve", lambda: nc.vector.tensor_copy(out=LC, in_=c3(Lam)[:, :, 63]), ["k_Lam"], ["k_LC"])
                if d == 1:
                    for c in range(NCH):
                        csl = slice(c * 64, (c + 1) * 64)
                        E("pool", lambda: nc.gpsimd.tensor_scalar(out=Lam[:, csl], in0=Lam[:, csl], scalar1=-1.0, scalar2=LC[:, c:c + 1],
                                                                  op0=ALU.mult, op1=ALU.add), ["k_Lam", "k_LC"], ["k_Lam"])
                    E("dve", lambda: nc.vector.tensor_tensor(out=Lam, in0=Lam, in1=T2, op=ALU.add), ["k_Lam", "k_T2"], ["k_Lam"])
                E("act", lambda: nc.scalar.activation(out=GC, in_=LC, func=AF.Exp), ["k_LC"], ["k_GC"])
                E("act", lambda: nc.scalar.activation(out=T0, in_=Lam, func=AF.Exp), ["k_Lam"], ["k_T0"])
                E("dve", lambda: nc.vector.tensor_tensor(out=ar[:, :, 1, :], in0=c3(rs), in1=c3(T0), op=ALU.mult), ["k_rs", "k_T0"], ["k_ar"])
                E("dve", lambda: nc.vector.tensor_tensor(out=T0, in0=Lam, in1=T2, op=ALU.subtract), ["k_Lam", "k_T2"], ["k_T0"])
                E("act", lambda: nc.scalar.activation(out=T0, in_=T0, func=AF.Exp), ["k_T0"], ["k_T0"])
                E("dve", lambda: nc.vector.scalar_tensor_tensor(out=ar[:, :, 0, :], in0=c3(kk), scalar=-1.0, in1=c3(T0), op0=ALU.mult, op1=ALU.mult),
                  ["k_kk", "k_T0"], ["k_ar"])
                E("act", lambda: nc.scalar.activation(out=T2, in_=Lam, func=AF.Exp, scale=-1.0), ["k_Lam"], ["k_T2"])
                E("pool", lambda: nc.gpsimd.tensor_tensor(out=T0, in0=kk, in1=ad, op=ALU.mult), ["k_kk", "k_T1"], ["k_T0"])
                E("dve", lambda: nc.vector.tensor_scalar(out=T1, in0=ad, scalar1=kaT[:, hp:hp + 1], scalar2=ka1[:, hp:hp + 1],
                                                         op0=ALU.mult, op1=ALU.add), ["k_T1", "k_kaT", "k_ka1"], ["k_T1"])
                E("dve", lambda: nc.vector.tensor_tensor(out=T1, in0=T1, in1=ks, op=ALU.mult), ["k_T1", "k_ks"], ["k_T1"])
                E("dve", lambda: nc.vector.tensor_tensor(out=bk[:, :, 0, :], in0=c3(T0), in1=c3(T2), op=ALU.mult), ["k_T0", "k_T2"], ["k_bk"])
                E("pool", lambda: nc.gpsimd.tensor_tensor(out=bk[:, :, 1, :], in0=c3(T1), in1=c3(T2), op=ALU.mult), ["k_T1", "k_T2"], ["k_bk"])
                for c in range(NCH):
                    csl = slice(c * 64, (c + 1) * 64)
                    E("act", lambda: nc.scalar.activation(out=T2[:, csl], in_=Lam[:, csl], func=AF.Exp, scale=-1.0, bias=LC[:, c:c + 1]),
                      ["k_Lam", "k_LC"], ["k_T2"])
                E("dve", lambda: nc.vector.tensor_tensor(out=BKf[:, :, 0, :], in0=c3(T0), in1=c3(T2), op=ALU.mult), ["k_T0", "k_T2"], ["k_BKf"])
                E("pool", lambda: nc.gpsimd.tensor_tensor(out=BKf[:, :, 1, :], in0=c3(T1), in1=c3(T2), op=ALU.mult), ["k_T1", "k_T2"], ["k_BKf"])
                P.barrier()
                G32b = T0[0:64, 0:2048].rearrange("p (c x) -> p c x", c=4)
                Pb = [T1[0:64, i * 512:(i + 1) * 512] for i in range(2)]
                Qb = [T1[0:64, 1024 + i * 512:1024 + (i + 1) * 512] for i in range(2)]
                Ac = [T2[0:64, i * 512:(i + 1) * 512] for i in range(2)]
                TTb = [T2[0:64, 1024 + i * 512:1024 + (i + 1) * 512] for i in range(2)]
                Lb = Lam.bitcast(BF16)
                G16b = [Lb[0:64, i * 2048:(i + 1) * 2048].rearrange("p (c x) -> p c x", c=4) for i in range(2)]
                E("dve", lambda: nc.vector.memset(H32, 0.0), w=["k_H32"])
                E("dve", lambda: nc.vector.memset(Hb, 0.0), w=["k_Hb"])
                order = list(range(NCH)) if d == 0 else [3, 2, 1, 0] + list(range(NCH - 1, 3, -1))
                batches = [order[i:i + 4] for i in range(0, NCH, 4)][:RW_NB]

                def phaseA(cl, bi):
                    tb = K.bank[7].bitcast(BF16)
                    TM, Vp4 = TMb[bi], Vpb[bi]
                    for half in range(2):
                        for q2 in range(2):
                            c = cl[half * 2 + q2]
                            for q in range(2):
                                E("pe", lambda: nc.tensor.transpose(out=tb[0:64, q2 * 384 + q * 128:q2 * 384 + (q + 1) * 128],
                                                                    in_=BKf[:, c, q, :], identity=K.ident_b), ["k_BKf", "ident_b"], ["PS:7"])
                            E("pe", lambda: nc.tensor.transpose(out=tb[0:64, q2 * 384 + 256:q2 * 384 + 384], in_=vb[:, c * 64:(c + 1) * 64],
                                                                identity=K.ident_b), ["k_vb", "ident_b"], ["PS:7"])
                        hs = slice(half * 2, half * 2 + 2)
                        E("act", lambda: nc.scalar.copy(out=TM[:, hs, :], in_=tb[0:64, 0:768].rearrange("p (c x) -> p c x", c=2)),
                          ["PS:7"], [f"a_TM{bi}"])
                        E("act", lambda: nc.scalar.copy(out=Vp4[:, hs, 0, 0:64], in_=TM[:, hs, 256:320]), [f"a_TM{bi}"], [f"a_Vp{bi}"])
                        E("dve", lambda: nc.vector.tensor_copy(out=Vp4[:, hs, 1, 64:128], in_=TM[:, hs, 320:384]), [f"a_TM{bi}"], [f"a_Vp{bi}"])
                        yield
                    for half in range(2):
                        hs = slice(half * 2, half * 2 + 2)
                        for q2 in range(2):
                            c = cl[half * 2 + q2]
                            for e in range(2):
                                esl = slice(e * 64, (e + 1) * 64)
                                arc = ar[esl, c].rearrange("p a t -> p (a t)")
                                bnk, bn = K.bank[e], f"PS:{e}"
                                E("pe", lambda: nc.tensor.matmul(bnk[0:64, q2 * 256:q2 * 256 + 128], lhsT=bk[esl, c, 0, :], rhs=arc,
                                                                 start=True, stop=True), ["k_bk", "k_ar"], [bn])
                                E("pe", lambda: nc.tensor.matmul(bnk[0:64, q2 * 256 + 128:q2 * 256 + 256], lhsT=bk[esl, c, 1, :], rhs=arc,
                                                                 start=True, stop=True), ["k_bk", "k_ar"], [bn])
                        for e in range(2):
                            E("dve", lambda: nc.vector.tensor_tensor(out=G32b[:, hs, e * 256:(e + 1) * 256],
                                                                     in0=K.bank[e][0:64, :].rearrange("p (c x) -> p c x", c=2),
                                                                     in1=gmask[:, d, :].rearrange("p (c x) -> p c x", c=2), op=ALU.mult),
                              [f"PS:{e}", "k_gmask"], ["a_G32"])
                        yield
                    E("act", lambda: nc.scalar.copy(out=G16b[bi], in_=G32b), ["a_G32"], [f"a_G16_{bi}"])
                    Nv = G32b.rearrange("p c (e x) -> p c e x", e=2)[:, :, :, 0:64]
                    E("dve", lambda: nc.vector.tensor_tensor(out=Ac[0].rearrange("p (c e x) -> p c e x", c=4, e=2), in0=Nv,
                                                             in1=i8.rearrange("p (c e x) -> p c e x", c=4, e=2), op=ALU.add),
                      ["a_G32", "k_i8"], ["a_Ac0"])
                    for ci in range(4):
                        for e in range(2):
                            p_ = ci * 2 + e
                            E("pe", lambda: nc.tensor.transpose(out=K.bank[5][0:64, p_ * 64:(p_ + 1) * 64], in_=G32b[:, ci, e * 256:e * 256 + 64],
                                                                identity=K.ident_f[0:64, 0:64]), ["a_G32", "ident_f"], ["PS:5"])
                    E("act", lambda: nc.scalar.copy(out=Qb[0], in_=K.bank[5][0:64, :]), ["PS:5"], ["a_Qb0"])
                    yield
                    for kx in range(1, 6):
                        pi, po = (kx - 1) % 2, kx % 2
                        for ci in range(4):
                            for e in range(2):
                                p_ = ci * 2 + e
                                psl = slice(p_ * 64, (p_ + 1) * 64)
                                Pprev = G32b[:, ci, e * 256:e * 256 + 64] if kx == 1 else Pb[pi][:, psl]
                                Pn = "a_G32" if kx == 1 else f"a_Pb{pi}"
                                if kx < 5:
                                    E("pe", lambda: nc.tensor.matmul(K.bank[4][0:64, psl], lhsT=Qb[pi][:, psl], rhs=Pprev, start=True, stop=True),
                                      [f"a_Qb{pi}", Pn], ["PS:4"])
                                E("pe", lambda: nc.tensor.matmul(K.bank[5][0:64, psl], lhsT=Pprev, rhs=Qb[pi][:, psl], start=True, stop=True),
                                  [f"a_Qb{pi}", Pn], ["PS:5"])
                        if kx < 5:
                            E("dve", lambda: nc.vector.tensor_copy(out=Pb[po], in_=K.bank[4][0:64, :]), ["PS:4"], [f"a_Pb{po}"])
                        E("act", lambda: nc.scalar.copy(out=Qb[po], in_=K.bank[5][0:64, :]), ["PS:5"], [f"a_Qb{po}"])
                        yield
                        for p_ in range(8):
                            psl = slice(p_ * 64, (p_ + 1) * 64)
                            E("pe", lambda: nc.tensor.matmul(K.bank[6][0:64, psl], lhsT=Qb[po][:, psl], rhs=Ac[pi][:, psl], start=True, stop=True),
                              [f"a_Qb{po}", f"a_Ac{pi}"], ["PS:6"])
                        dst, dn = (TTb[bi], f"a_TT{bi}") if kx == 5 else (Ac[po], f"a_Ac{po}")
                        E("dve", lambda: nc.vector.tensor_tensor(out=dst, in0=Ac[pi], in1=K.bank[6][0:64, :], op=ALU.add),
                          [f"a_Ac{pi}", "PS:6"], [dn])
                        yield

                def phaseB(c, ci, bi):
                    nonlocal cc
                    csl = slice(c * 64, (c + 1) * 64)
                    j = cc % 2
                    cc += 1
                    G16 = G16b[bi][:, ci, :]
                    TM = TMb[bi][:, ci, :]
                    Vp = Vpb[bi][:, ci]
                    gn, tn_, vn, ttn = f"a_G16_{bi}", f"a_TM{bi}", f"a_Vp{bi}", f"a_TT{bi}"
                    b2, b3 = K.bank[2], K.bank[3]
                    E("pe", lambda: nc.tensor.matmul(b2[0:64, 0:128], lhsT=ar[:, c, 0, :], rhs=Hb, start=True, stop=True), ["k_ar", "k_Hb"], ["PS:2"])
                    for e in range(2):
                        e6 = slice(e * 64, (e + 1) * 64)
                        E("pe", lambda: nc.tensor.matmul(b2[0:64, 128 + e * 64:128 + (e + 1) * 64], lhsT=G16[:, e * 256 + 128:e * 256 + 192],
                                                         rhs=TM[:, 256 + e * 64:256 + (e + 1) * 64], start=True, stop=True), [gn, tn_], ["PS:2"])
                    E("act", lambda: nc.scalar.copy(out=W32, in_=b2[0:64, 0:128]), ["PS:2"], ["k_W32"])
                    E("dve", lambda: nc.vector.tensor_tensor(out=W32, in0=W32, in1=b2[0:64, 128:256], op=ALU.add), ["k_W32", "PS:2"], ["k_W32"])
                    for e in range(2):
                        e6 = slice(e * 64, (e + 1) * 64)
                        p_ = ci * 2 + e
                        E("pe", lambda: nc.tensor.matmul(b2[0:64, 256 + e * 64:256 + (e + 1) * 64], lhsT=TTb[bi][:, p_ * 64:(p_ + 1) * 64],
                                                         rhs=W32[:, e6], start=True, stop=True), [ttn, "k_W32"], ["PS:2"])
                    E("act", lambda: nc.scalar.copy(out=Ub[j], in_=b2[0:64, 256:384]), ["PS:2"], [f"k_Ub{j}"])
                    E("pe", lambda: nc.tensor.matmul(b3[:, 128:256], lhsT=TM[:, 0:128], rhs=Ub[j], start=True, stop=False), [tn_, f"k_Ub{j}"], ["PS:3"])
                    E("pe", lambda: nc.tensor.matmul(b3[:, 128:256], lhsT=TM[:, 128:256], rhs=TM[:, 256:384], start=False, stop=True), [tn_], ["PS:3"])
                    E("act", lambda: nc.scalar.copy(out=Up[j][:, 0, 0:64], in_=Ub[j][:, 0:64]), [f"k_Ub{j}"], [f"k_Up{j}"])
                    E("dve", lambda: nc.vector.tensor_copy(out=Up[j][:, 1, 64:128], in_=Ub[j][:, 64:128]), [f"k_Ub{j}"], [f"k_Up{j}"])
                    b1 = K.bank[3]
                    E("pe", lambda: nc.tensor.matmul(b1[:, 0:64], lhsT=Hb, rhs=ar[:, c, 1, :], start=True, stop=False), ["k_Hb", "k_ar"], ["PS:3"])
                    for e in range(2):
                        E("pe", lambda: nc.tensor.matmul(b1[:, 0:64], lhsT=Up[j][:, e, :], rhs=G16[:, e * 256 + 64:e * 256 + 128],
                                                         start=False, stop=False), [f"k_Up{j}", gn], ["PS:3"])
                        E("pe", lambda: nc.tensor.matmul(b1[:, 0:64], lhsT=Vp[:, e, :], rhs=G16[:, e * 256 + 192:e * 256 + 256],
                                                         start=False, stop=(e == 1)), [vn, gn], ["PS:3"])
                    E("dve", lambda: nc.vector.scalar_tensor_tensor(out=H32, in0=H32, scalar=GC[:, c:c + 1], in1=b3[:, 128:256],
                                                                    op0=ALU.mult, op1=ALU.add), ["k_H32", "k_GC", "PS:3"], ["k_H32"])
                    E("dve", lambda: nc.vector.tensor_tensor(out=Hb, in0=H32, in1=bd64, op=ALU.mult), ["k_H32", "k_bd64"], ["k_Hb"])
                    if d == 0:
                        E("act", lambda: nc.scalar.copy(out=yacc[:, csl], in_=b1[:, 0:64]), ["PS:3"], ["k_yacc"])
                    else:
                        E("dve", lambda: nc.vector.tensor_tensor(out=yacc[:, csl], in0=yacc[:, csl], in1=b1[:, 0:64], op=ALU.add),
                          ["PS:3", "k_yacc"], ["k_yacc"])

                for _ in (phaseA(batches[0], 0) if batches else ()):
                    pass
                for bn in range(len(batches)):
                    nxt = phaseA(batches[bn + 1], (bn + 1) % 2) if bn + 1 < len(batches) else None
                    for ci, c in enumerate(batches[bn]):
                        phaseB(c, ci, bn % 2)
                        if nxt is not None:
                            for _ in range(4):
                                next(nxt, None)
                    if nxt is not None:
                        for _ in nxt:
                            pass
                P.barrier()
            for (t0, tn) in TT512:
                tsl = slice(t0, t0 + tn)
                a_, b_, c_ = T1[:, tsl], T2[:, tsl], Lam[:, tsl]
                E("dve", lambda: nc.vector.tensor_scalar(out=a_, in0=asum[:, tsl], scalar1=kaT[:, hp:hp + 1], scalar2=None, op0=ALU.mult),
                  ["k_asum", "k_kaT"], ["k_T1"])
                E("dve", lambda: nc.vector.tensor_scalar(out=a_, in0=a_, scalar1=ka1[:, hp:hp + 1], scalar2=ka1[:, hp:hp + 1],
                                                         op0=ALU.add, op1=ALU.add), ["k_T1", "k_ka1"], ["k_T1"])
                E("dve", lambda: nc.vector.tensor_tensor(out=a_, in0=a_, in1=ks[:, tsl], op=ALU.mult), ["k_T1", "k_ks"], ["k_T1"])
                E("dve", lambda: nc.vector.scalar_tensor_tensor(out=a_, in0=rs[:, tsl], scalar=rkT[:, hp:hp + 1], in1=a_, op0=ALU.mult, op1=ALU.mult),
                  ["k_rs", "k_rkT", "k_T1"], ["k_T1"])
                E("pe", lambda: nc.tensor.matmul(K.bank[0][:, :tn], lhsT=bd64, rhs=a_, start=True, stop=True), ["k_bd64", "k_T1"], ["PS:0"])
                E("dve", lambda: nc.vector.tensor_tensor(out=a_, in0=K.bank[0][:, :tn], in1=vb[:, tsl], op=ALU.mult), ["PS:0", "k_vb"], ["k_T1"])
                E("act", lambda: nc.scalar.activation(out=b_, in_=yacc[:, tsl], func=AF.Square), ["k_yacc"], ["k_T2"])
                E("pe", lambda: nc.tensor.matmul(K.bank[1][:, :tn], lhsT=bd64, rhs=yacc[:, tsl], start=True, stop=True), ["k_bd64", "k_yacc"], ["PS:1"])
                E("pe", lambda: nc.tensor.matmul(K.bank[2][:, :tn], lhsT=bd64, rhs=b_, start=True, stop=True), ["k_bd64", "k_T2"], ["PS:2"])
                E("dve", lambda: nc.vector.tensor_scalar(out=c_, in0=K.bank[1][:, :tn], scalar1=1.0 / 64.0, scalar2=None, op0=ALU.mult),
                  ["PS:1"], ["k_Lam"])
                E("act", lambda: nc.scalar.activation(out=b_, in_=c_, func=AF.Square), ["k_Lam"], ["k_T2"])
                E("dve", lambda: nc.vector.scalar_tensor_tensor(out=b_, in0=K.bank[2][:, :tn], scalar=1.0 / 64.0, in1=b_, op0=ALU.mult, op1=ALU.subtract),
                  ["PS:2", "k_T2"], ["k_T2"])
                E("dve", lambda: nc.vector.tensor_scalar(out=b_, in0=b_, scalar1=64e-5, scalar2=None, op0=ALU.add), ["k_T2"], ["k_T2"])
                E("act", lambda: nc.scalar.activation(out=b_, in_=b_, func=AF.Sqrt), ["k_T2"], ["k_T2"])
                E("dve", lambda: nc.vector.reciprocal(out=b_, in_=b_), ["k_T2"], ["k_T2"])
                E("dve", lambda: nc.vector.tensor_tensor(out=c_, in0=yacc[:, tsl], in1=c_, op=ALU.subtract), ["k_yacc", "k_Lam"], ["k_Lam"])
                E("dve", lambda: nc.vector.tensor_tensor(out=c_, in0=c_, in1=b_, op=ALU.mult), ["k_Lam", "k_T2"], ["k_Lam"])
                E("dve", lambda: nc.vector.scalar_tensor_tensor(out=c_, in0=c_, scalar=lnT[:, hp:hp + 1], in1=a_, op0=ALU.mult, op1=ALU.add),
                  ["k_Lam", "k_lnT", "k_T1"], ["k_Lam"])
                E("pe", lambda: nc.tensor.matmul(K.bank[3][:, :tn], lhsT=gupb[:, cols], rhs=sgd[:, tsl], start=True, stop=True),
                  ["k_gupb", "k_sgd"], ["PS:3"])
                ob, obn = outb[oc % 2], f"k_ob{oc % 2}"
                oc += 1
                E("dve", lambda: nc.vector.tensor_tensor(out=ob[:, :tn], in0=c_, in1=K.bank[3][:, :tn], op=ALU.mult), ["k_Lam", "PS:3"], [obn])
                E("q_sp", lambda: nc.sync.dma_start(out=K.YT[2048 + hp * 128:2048 + (hp + 1) * 128, tsl], in_=ob[:, :tn]), [obn], ["YT"])
    P.barrier()


RW_NB = 9
TTL = 384
NSUB = TTL // 128


def bcast_mod(K, l, col0, r, dst, dstn, rowbuf, ones1):
    nc, P = K.nc, K.P
    P.emit("q_act", lambda: nc.scalar.dma_start(out=rowbuf, in_=K.modrow[l, r:r + 1, col0:col0 + D]), reads=["modrow"], writes=["t_rowbuf"])
    for j in range(4):
        P.emit("pe", lambda: nc.tensor.matmul(K.bank[7], lhsT=ones1, rhs=rowbuf[:, j * 512:(j + 1) * 512], start=True, stop=True),
               reads=["t_rowbuf", "t_ones1"], writes=["PS:7"])
        P.emit("act", lambda: nc.scalar.copy(out=dst[:, j * 512:(j + 1) * 512], in_=K.bank[7]), reads=["PS:7"], writes=[dstn])


def rms_to_fm(K, hsrc, hsn, ss, ssn, yb, ybn, junk, A, An, l, jsh, r, dst_fn, dstn):
    nc, P = K.nc, K.P
    P.emit("act", lambda: nc.scalar.activation(out=junk, in_=hsrc, func=AF.Square, accum_out=ss), reads=[hsn], writes=["junk", ssn])
    P.emit("dve", lambda: nc.vector.tensor_scalar(out=ss, in0=ss, scalar1=1.0 / D, scalar2=EPS, op0=ALU.mult, op1=ALU.add),
           reads=[ssn], writes=[ssn])
    P.emit("act", lambda: nc.scalar.activation(out=ss, in_=ss, func=AF.Sqrt), reads=[ssn], writes=[ssn])
    P.emit("dve", lambda: nc.vector.reciprocal(out=ss, in_=ss), reads=[ssn], writes=[ssn])
    P.emit("act", lambda: nc.scalar.activation(out=yb, in_=hsrc, func=AF.Copy, scale=ss), reads=[hsn, ssn], writes=[ybn])
    for g4 in range(4):
        pt = K.bank[g4].bitcast(BF16)[:, :512]
        for j in range(4):
            dc = g4 * 4 + j
            P.emit("pe", lambda: nc.tensor.transpose(out=pt[:, j * 128:(j + 1) * 128], in_=yb[:, dc * 128:(dc + 1) * 128],
                                                      identity=K.ident_b), reads=[ybn, "ident_b"], writes=[f"PS:{g4}"])
        for j in range(4):
            dc = g4 * 4 + j
            P.emit("dve", lambda: nc.vector.tensor_scalar(out=dst_fn(dc), in0=pt[:, j * 128:(j + 1) * 128],
                                                          scalar1=A[:, dc, r:r + 1], scalar2=modv(K, l, jsh + dc, r),
                                                          op0=ALU.mult, op1=ALU.add),
                   reads=[f"PS:{g4}", An, "modT"], writes=[dstn])


def stage_tail(K, l):
    nc, P, I, L = K.nc, K.P, K.I, K.L
    wbr, wout, wf1, wf2 = K.wb[l]["br"], K.wb[l]["out"], K.wb[l]["ff1"], K.wb[l]["ff2"]
    modv4 = K.modT.rearrange("p (l j r) -> p l j r", l=L, j=96, r=2)
    with contextlib.ExitStack() as st:
        sb = lambda n, s_, d: K.sb(n, s_, d, st)
        wst = [sb(f"t_w{i}", (128, 24, 512), BF16) for i in range(2)]
        R1 = sb("t_R1", (128, 64 * TTL), BF16)
        yT = R1[:, 0:24 * TTL].rearrange("p (c t) -> p c t", c=24)
        gbuf = [R1[:, (24 + 12 * i) * TTL:(36 + 12 * i) * TTL].rearrange("p (c t) -> p c t", c=12) for i in range(2)]
        mT = R1[:, 48 * TTL:64 * TTL].rearrange("p (c t) -> p c t", c=16)
        hidT = R1.rearrange("p (c t) -> p c t", c=64)
        ht = sb("t_h", (128, NSUB, D), F32)
        u2T = sb("t_u2T", (128, KC, TTL), BF16)
        gl = [sb(f"t_gl{i}", (128, D), F32) for i in range(2)]
        gc = sb("t_gc", (128, D), F32)
        rowbuf = sb("t_rowbuf", (1, D), F32)
        ones1 = sb("t_ones1", (1, 128), F32)
        tmpa = [sb(f"t_ta{i}", (128, 512), F32) for i in range(3)]
        gT = sb("t_gT", (128, KC), F32)
        A = sb("t_A2", (128, KC, 2), F32)
        ss = sb("t_ss", (128, 1), F32)
        yb = sb("t_yb", (128, D), BF16)
        junk = sb("t_junk", (128, D), BF16)
        P.emit("dve", lambda: nc.vector.memset(ones1, 1.0), writes=["t_ones1"])
        bcast_mod(K, l, 2 * D, 0, gl[0], "t_gl0", rowbuf, ones1)
        bcast_mod(K, l, 5 * D, 0, gl[1], "t_gl1", rowbuf, ones1)
        load_fm_vec(K, st, I["norm2_g"][l].rearrange("(kc p) -> kc p", p=128), KC, gT, "t_gT")
        for r in range(2):
            P.emit("dve", lambda: nc.vector.tensor_scalar(out=A[:, :, r], in0=modv4[:, l, 64:80, r], scalar1=1.0,
                                                          scalar2=None, op0=ALU.add), reads=["modT"], writes=["t_A2"])
            P.emit("dve", lambda: nc.vector.tensor_tensor(out=A[:, :, r], in0=A[:, :, r], in1=gT, op=ALU.mult),
                   reads=["t_A2", "t_gT"], writes=["t_A2"])
        cnt = dict(w=0, p=0, g=0)

        def wload(src_ap, nk, rn):
            b = cnt["w"] % 2
            cnt["w"] += 1
            P.emit("q_sp", lambda: nc.sync.dma_start(out=wst[b][:, :nk, :], in_=src_ap), reads=[rn], writes=[f"t_w{b}"])
            return wst[b], f"t_w{b}"

        def nextbank():
            b = cnt["p"] % 7
            cnt["p"] += 1
            return K.bank[b], f"PS:{b}"

        def resid(ps, pn, s, cg, which, has_ctx):
            cols = slice(cg * 512, (cg + 1) * 512)
            if has_ctx and s < 2:
                g, gn = gc, "t_gc"
            else:
                g, gn = gl[which], f"t_gl{which}"
            P.emit("dve", lambda: nc.vector.tensor_tensor(out=tmpa[0], in0=ps, in1=g[:, cols], op=ALU.mult),
                   reads=[pn, gn], writes=["t_ta0"])
            P.emit("pool", lambda: nc.gpsimd.tensor_tensor(out=ht[:, s, cols], in0=ht[:, s, cols], in1=tmpa[0], op=ALU.add),
                   reads=["t_ta0", "t_h"], writes=["t_h"])

        for ti in range(T // TTL):
            t0 = ti * TTL
            tsl = slice(t0, t0 + TTL)
            has_ctx = ti == 0
            if has_ctx:
                bcast_mod(K, l, 2 * D, 1, gc, "t_gc", rowbuf, ones1)
            P.emit("q_act", lambda: nc.scalar.dma_start(out=yT, in_=K.YT[:, tsl].rearrange("(c p) t -> p c t", p=128)),
                   reads=["YT"], writes=["t_yT", "t_hid", "t_fence"])
            P.emit("q_act", lambda: nc.scalar.dma_start(out=ht, in_=K.H[tsl, :].rearrange("(s p) d -> p s d", p=128)),
                   reads=["H"], writes=["t_h"])
            for cg in range(4):
                gb = gbuf[cnt["g"] % 2]
                gbn = f"t_g{cnt['g'] % 2}"
                cnt["g"] += 1
                for n in range(3):
                    r0 = GT + n * D + cg * 512
                    P.emit("q_act", lambda: nc.scalar.dma_start(out=gb[:, n * 4:(n + 1) * 4, :],
                                                                in_=K.PT[r0:r0 + 512, tsl].rearrange("(c p) t -> p c t", p=128)),
                           reads=["PT", "t_fence"], writes=[gbn + f"_{n}"])
                wt, wn = wload(wbr[:, cg * 512:(cg + 1) * 512].rearrange("(c p) d -> p c d", p=128), 24, f"wb{l}_br")
                for dcl in range(4):
                    dc = cg * 4 + dcl
                    banks = [nextbank() for _ in range(3)]
                    for n in range(3):
                        ps, pn = banks[n]
                        for w8 in range(8):
                            P.emit("pe", lambda: nc.tensor.matmul(ps[:, :TTL], lhsT=wt[:, n * 8 + w8, dcl * 128:(dcl + 1) * 128],
                                                                  rhs=yT[:, n * 8 + w8, :], start=(w8 == 0), stop=(w8 == 7)),
                                   reads=[wn, "t_yT"], writes=[pn])
                    for n in range(3):
                        ps, pn = banks[n]
                        P.emit("dve", lambda: nc.vector.tensor_tensor(out=tmpa[n][:, :TTL], in0=ps[:, :TTL],
                                                                      in1=gb[:, n * 4 + dcl, :], op=ALU.mult),
                               reads=[pn, gbn + f"_{n}"], writes=[f"t_ta{n}"])
                    P.emit("pool", lambda: nc.gpsimd.tensor_tensor(out=tmpa[0][:, :TTL], in0=tmpa[0][:, :TTL], in1=tmpa[1][:, :TTL],
                                                                   op=ALU.add), reads=["t_ta0", "t_ta1"], writes=["t_ta0"])
                    P.emit("pool", lambda: nc.gpsimd.tensor_tensor(out=mT[:, dc, :], in0=tmpa[0][:, :TTL], in1=tmpa[2][:, :TTL],
                                                                   op=ALU.add), reads=["t_ta0", "t_ta2"], writes=["t_mT"])
            for cg in range(4):
                wt, wn = wload(wout[:, cg * 512:(cg + 1) * 512].rearrange("(c p) d -> p c d", p=128), 16, f"wb{l}_out")
                for s_ in range(NSUB):
                    ps, pn = nextbank()
                    for k in range(KC):
                        P.emit("pe", lambda: nc.tensor.matmul(ps, lhsT=mT[:, k, s_ * 128:(s_ + 1) * 128], rhs=wt[:, k, :],
                                                              start=(k == 0), stop=(k == KC - 1)), reads=[wn, "t_mT"], writes=[pn])
                    resid(ps, pn, s_, cg, 0, has_ctx)
            for s_ in range(NSUB):
                r = 1 if (has_ctx and s_ < 2) else 0
                rms_to_fm(K, ht[:, s_, :], "t_h", ss, "t_ss", yb, "t_yb", junk, A, "t_A2", l, 48, r,
                          lambda dc, s_=s_: u2T[:, dc, s_ * 128:(s_ + 1) * 128], "t_u2T")
            if has_ctx:
                bcast_mod(K, l, 5 * D, 1, gc, "t_gc", rowbuf, ones1)
            for fg in range(16):
                wt, wn = wload(wf1[:, fg * 512:(fg + 1) * 512].rearrange("(c p) d -> p c d", p=128), 16, f"wb{l}_ff1")
                for fc in range(4):
                    ps, pn = nextbank()
                    for k in range(KC):
                        P.emit("pe", lambda: nc.tensor.matmul(ps[:, :TTL], lhsT=wt[:, k, fc * 128:(fc + 1) * 128], rhs=u2T[:, k, :],
                                                              start=(k == 0), stop=(k == KC - 1)), reads=[wn, "t_u2T"], writes=[pn])
                    tb = 1 + (fc % 2)
                    P.emit("act", lambda: nc.scalar.activation(out=tmpa[tb][:, :TTL], in_=ps[:, :TTL], func=AF.Relu),
                           reads=[pn], writes=[f"t_ta{tb}"])
                    P.emit("pool", lambda: nc.gpsimd.tensor_tensor(out=hidT[:, fg * 4 + fc, :], in0=tmpa[tb][:, :TTL],
                                                                   in1=tmpa[tb][:, :TTL], op=ALU.mult),
                           reads=[f"t_ta{tb}", "t_mT", "t_yT"], writes=["t_hid"])
            for cg in range(4):
                banks = [nextbank() for _ in range(NSUB)]
                for fq in range(4):
                    wt, wn = wload(wf2[fq * 2048:(fq + 1) * 2048, cg * 512:(cg + 1) * 512].rearrange("(c p) d -> p c d", p=128),
                                   16, f"wb{l}_ff2")
                    for s_ in range(NSUB):
                        ps, pn = banks[s_]
                        for j in range(16):
                            P.emit("pe", lambda: nc.tensor.matmul(ps, lhsT=hidT[:, fq * 16 + j, s_ * 128:(s_ + 1) * 128], rhs=wt[:, j, :],
                                                                  start=(fq == 0 and j == 0), stop=(fq == 3 and j == 15)),
                                   reads=[wn, "t_hid"], writes=[pn])
                for s_ in range(NSUB):
                    ps, pn = banks[s_]
                    resid(ps, pn, s_, cg, 1, has_ctx)
            P.emit("q_act", lambda: nc.scalar.dma_start(out=K.H[tsl, :].rearrange("(s p) d -> p s d", p=128), in_=ht),
                   reads=["t_h"], writes=["H"])
    P.barrier()


def stage_final(K):
    nc, P, I = K.nc, K.P, K.I
    with contextlib.ExitStack() as st:
        sb = lambda n, s_, d: K.sb(n, s_, d, st)
        fg = bcast_row_big(K, st, I["final_g"].rearrange("(o d) -> o d", o=1), "fg")
        hb = [sb(f"f_h{i}", (128, D), F32) for i in range(2)]
        ob = [sb(f"f_o{i}", (128, D), F32) for i in range(2)]
        junk = sb("f_junk", (128, D), BF16)
        ss = sb("f_ss", (128, 2), F32)
        for tt in range(SEQ // 128):
            b = tt % 2
            P.emit("q_sp", lambda: nc.sync.dma_start(out=hb[b], in_=K.H[CTX + tt * 128:CTX + (tt + 1) * 128, :]),
                   reads=["H"], writes=[f"f_h{b}"])
            sv = ss[:, b:b + 1]
            P.emit("act", lambda: nc.scalar.activation(out=junk, in_=hb[b], func=AF.Square, accum_out=sv),
                   reads=[f"f_h{b}"], writes=["f_junk", f"f_ss{b}"])
            P.emit("dve", lambda: nc.vector.tensor_scalar(out=sv, in0=sv, scalar1=1.0 / D, scalar2=EPS, op0=ALU.mult, op1=ALU.add),
                   reads=[f"f_ss{b}"], writes=[f"f_ss{b}"])
            P.emit("act", lambda: nc.scalar.activation(out=sv, in_=sv, func=AF.Sqrt), reads=[f"f_ss{b}"], writes=[f"f_ss{b}"])
            P.emit("dve", lambda: nc.vector.reciprocal(out=sv, in_=sv), reads=[f"f_ss{b}"], writes=[f"f_ss{b}"])
            P.emit("dve", lambda: nc.vector.scalar_tensor_tensor(out=ob[b], in0=hb[b], scalar=sv, in1=fg, op0=ALU.mult, op1=ALU.mult),
                   reads=[f"f_h{b}", f"f_ss{b}", "fg"], writes=[f"f_o{b}"])
            P.emit("q_sp", lambda: nc.sync.dma_start(out=K.out[tt * 128:(tt + 1) * 128, :], in_=ob[b]), reads=[f"f_o{b}"], writes=["out"])


def bcast_row_big(K, st, src_row, tag):
    nc, P = K.nc, K.P
    row = K.sb(f"bbr_{tag}", (1, D), F32, st)
    ones = K.sb(f"bbo_{tag}", (1, 128), F32, st)
    dst = K.sb(f"bbd_{tag}", (128, D), F32, st)
    P.emit("q_sp", lambda: nc.sync.dma_start(out=row, in_=src_row), writes=[f"bbr_{tag}"])
    P.emit("dve", lambda: nc.vector.memset(ones, 1.0), writes=[f"bbo_{tag}"])
    for j in range(4):
        P.emit("pe", lambda: nc.tensor.matmul(K.bank[7], lhsT=ones, rhs=row[:, j * 512:(j + 1) * 512], start=True, stop=True),
               reads=[f"bbr_{tag}", f"bbo_{tag}"], writes=["PS:7"])
        P.emit("act", lambda: nc.scalar.copy(out=dst[:, j * 512:(j + 1) * 512], in_=K.bank[7]), reads=["PS:7"], writes=[tag])
    return dst


_CACHE = {}


def kernel(**inputs):
    ncores = 4
    if "nc" not in _CACHE:
        _CACHE["nc"] = build(L=DEPTH)[0]
    nc = _CACHE["nc"]
    consts = _consts()
    shared = {}
    for n in PARAM_SHAPES:
        shared[n] = np.ascontiguousarray(np.asarray(inputs[n], dtype=np.float32))
    for n, v in consts.items():
        shared["k_" + n] = v
    x = np.asarray(inputs["x"], dtype=np.float32)
    ctx = np.asarray(inputs["ctx"], dtype=np.float32)
    c = np.asarray(inputs["c"], dtype=np.float32)
    c_ctx = np.asarray(inputs["c_ctx"], dtype=np.float32)
    in_maps = []
    for b in range(ncores):
        m = dict(shared)
        m["x"] = np.ascontiguousarray(x[b])
        m["ctx"] = np.ascontiguousarray(ctx[b])
        m["cc"] = np.ascontiguousarray(np.stack([c[b], c_ctx], 0))
        in_maps.append(m)
    res = run_bass_kernel_spmd(nc, in_maps, core_ids=list(range(ncores)))
    return np.stack([np.asarray(res.results[b]["out"], dtype=np.float32) for b in range(ncores)], 0)
```
